# Optimizing a Trainium2 kernel written in Bass

```python
import jax, jax.numpy as jnp
from jax import lax
import numpy as np

D_MODEL = 1024
BATCH = 32
SEQ = 256
DEPTH = 2
DEC_BATCH = 4
DEC_SEQ = 2048
PAST_LEN = 512

GRID_W = 64
HEAD_DIM = 64
FOURIER_GROUPS = 4
FOURIER_GROUP_W = 64
FOURIER_W = FOURIER_GROUPS * FOURIER_GROUP_W
RET_HEADS = 4
RET_DK = 64
RET_DV = 64
RET_QK_W = RET_HEADS * RET_DK
RET_W = RET_HEADS * RET_DV
RET_CHUNK = 128
ATT_Q_HEADS = 8
ATT_KV_HEADS = 2
ATT_GROUP = ATT_Q_HEADS // ATT_KV_HEADS
ATT_W = ATT_Q_HEADS * HEAD_DIM
ATT_KV_W = ATT_KV_HEADS * HEAD_DIM
WINDOW = 128
ATT_BLOCK = 128
N_BRANCH = 3
ROPE_BASE = 10000.0
EPS = 1e-6
SPLITS = (FOURIER_W, FOURIER_W, RET_QK_W, RET_QK_W, RET_W, RET_W,
          ATT_W, ATT_KV_W, ATT_KV_W, ATT_W, N_BRANCH * D_MODEL)
IN_W = sum(SPLITS)

kernel_name = 'hybrid_fourier_retention_swa_diffusion_step'

F32 = jnp.float32


def rms_norm(x, g):
    xf = x.astype(F32)
    y = xf * lax.rsqrt(jnp.mean(xf * xf, axis=-1, keepdims=True) + EPS)
    return (y * g.astype(F32)).astype(x.dtype)


def split_columns(u):
    parts = []
    start = 0
    for w in SPLITS:
        parts.append(u[..., start:start + w])
        start += w
    return parts


def axial_rope(x):
    n = x.shape[1]
    rows = n // GRID_W
    row = jnp.repeat(jnp.arange(rows), GRID_W)
    col = jnp.tile(jnp.arange(GRID_W), rows)
    quarter = HEAD_DIM // 4
    half = HEAD_DIM // 2
    inv = ROPE_BASE ** (-jnp.arange(quarter, dtype=F32) / quarter)

    def rot(xp, pos):
        ang = pos.astype(F32)[:, None] * inv[None, :]
        cos = jnp.cos(ang)[None, :, None, :]
        sin = jnp.sin(ang)[None, :, None, :]
        x1, x2 = xp[..., :quarter], xp[..., quarter:]
        return jnp.concatenate([x1 * cos - x2 * sin, x1 * sin + x2 * cos], axis=-1)

    xf = x.astype(F32)
    return jnp.concatenate([rot(xf[..., :half], row), rot(xf[..., half:], col)], axis=-1).astype(x.dtype)


def fourier_mix(f):
    b, t, _ = f.shape
    fg = f.astype(F32).reshape(b, t, FOURIER_GROUPS, FOURIER_GROUP_W)
    return jnp.fft.fftn(fg, axes=(1, 3), norm='ortho').real.reshape(b, t, FOURIER_W).astype(f.dtype)


def retention_scan(q, k, v, log_gamma, s0):
    b, t, h, _ = q.shape
    dv = v.shape[-1]
    nc = t // RET_CHUNK

    def chunks(a):
        return jnp.moveaxis(a.astype(F32).reshape(b, nc, RET_CHUNK, h, a.shape[-1]), 1, 0)

    i = jnp.arange(RET_CHUNK, dtype=F32)
    diff = i[:, None] - i[None, :]
    intra = jnp.where(diff[None] >= 0,
                      jnp.exp(jnp.maximum(diff, 0.0)[None] * log_gamma[:, None, None]), 0.0)
    read = jnp.exp((i[:, None] + 1.0) * log_gamma[None, :])
    write = jnp.exp((RET_CHUNK - 1.0 - i)[:, None] * log_gamma[None, :])
    carry_decay = jnp.exp(RET_CHUNK * log_gamma)

    def step(s, qkv):
        qc, kc, vc = qkv
        att = jnp.einsum('bihd,bjhd->bhij', qc, kc) * intra
        o = (jnp.einsum('bhij,bjhe->bihe', att, vc)
             + jnp.einsum('bihd,bhde->bihe', qc, s) * read[None, :, :, None])
        s = carry_decay[None, :, None, None] * s + jnp.einsum(
            'bjhd,bjhe->bhde', kc * write[None, :, :, None], vc)
        return s, o

    s_fin, o = lax.scan(step, s0.astype(F32), (chunks(q), chunks(k), chunks(v)))
    return jnp.moveaxis(o, 0, 1).reshape(b, t, h, dv), s_fin


def bidir_retention(q, k, v, ret_logit, s0_f, s0_b):
    lg = jax.nn.log_sigmoid(ret_logit.astype(F32))
    of, sf = retention_scan(q, k, v, lg[0], s0_f)
    ob, sb = retention_scan(q[:, ::-1], k[:, ::-1], v[:, ::-1], lg[1], s0_b)
    return of + ob[:, ::-1], jnp.stack([sf, sb], axis=1)


def softmax_with_sink(logits, sink):
    s = jnp.broadcast_to(sink.astype(F32)[:, :, None, None], logits.shape[:-1] + (1,))
    p = jax.nn.softmax(jnp.concatenate([logits, s], axis=-1), axis=-1)
    return p[..., :-1]


def context_attention(q, k, v, sink):
    b, l = q.shape[:2]
    nb = l // ATT_BLOCK
    scale = HEAD_DIM ** -0.5
    qb = jnp.moveaxis(q.reshape(b, nb, ATT_BLOCK, ATT_KV_HEADS, ATT_GROUP, HEAD_DIM), 1, 0)
    kf = k.astype(F32)
    vf = v.astype(F32)

    def one(qblk):
        logits = jnp.einsum('bqkgd,bskd->bkgqs', qblk.astype(F32), kf) * scale
        p = softmax_with_sink(logits, sink)
        return jnp.einsum('bkgqs,bskd->bqkgd', p, vf)

    o = lax.map(one, qb)
    return jnp.moveaxis(o, 0, 1).reshape(b, l, ATT_W).astype(q.dtype)


def latent_attention(q, k, v, k_ctx, v_ctx, sink):
    b, n = q.shape[:2]
    blk = ATT_BLOCK
    nb = n // blk
    scale = HEAD_DIM ** -0.5
    qb = q.astype(F32).reshape(b, nb, blk, ATT_KV_HEADS, ATT_GROUP, HEAD_DIM)
    pad = ((0, 0), (blk, blk), (0, 0), (0, 0))
    kp = jnp.pad(k.astype(F32), pad)
    vp = jnp.pad(v.astype(F32), pad)
    band = jnp.arange(nb)[:, None] * blk + jnp.arange(3 * blk)[None, :]
    kb = kp[:, band]
    vb = vp[:, band]
    qpos = jnp.arange(nb)[:, None] * blk + jnp.arange(blk)[None, :]
    kpos = band - blk
    valid = ((jnp.abs(qpos[:, :, None] - kpos[:, None, :]) <= WINDOW)
             & (kpos[:, None, :] >= 0) & (kpos[:, None, :] < n))
    loc = jnp.einsum('bnqkgd,bnskd->bnkgqs', qb, kb) * scale
    loc = jnp.where(valid[None, :, None, None], loc, -jnp.inf)
    ctx = jnp.einsum('bnqkgd,blkd->bnkgql', qb, k_ctx.astype(F32)) * scale
    p = softmax_with_sink(jnp.concatenate([loc, ctx], axis=-1), sink)
    o = (jnp.einsum('bnkgqs,bnskd->bnqkgd', p[..., :3 * blk], vb)
         + jnp.einsum('bnkgql,blkd->bnqkgd', p[..., 3 * blk:], v_ctx.astype(F32)))
    return o.reshape(b, n, ATT_W).astype(q.dtype)


def trunk_layer(x, cvec, w_mod, b_mod, g_pre, g_post, w_in, w_four, ret_logit, ret_gn,
                attn_sink, w_pa, w_pb, w_pc, w_out, k_ctx=None, v_ctx=None, s_ctx=None):
    b, t, _ = x.shape
    is_latent = k_ctx is not None
    mod = (jax.nn.silu(cvec) @ w_mod + b_mod).reshape(-1, 1, 3 * D_MODEL)
    shift, scale, gate = jnp.split(mod, 3, axis=-1)
    h = rms_norm(x, g_pre) * (1.0 + scale) + shift
    fx, fz, rq, rk, rv, rz, aq, ak, av, az, mg = split_columns(h @ w_in)

    ya = (fourier_mix(fx) @ w_four) * jax.nn.silu(fz)

    rq = rq.reshape(b, t, RET_HEADS, RET_DK)
    rk = rk.reshape(b, t, RET_HEADS, RET_DK) * (RET_DK ** -0.5)
    rv = rv.reshape(b, t, RET_HEADS, RET_DV)
    aq = aq.reshape(b, t, ATT_Q_HEADS, HEAD_DIM)
    ak = ak.reshape(b, t, ATT_KV_HEADS, HEAD_DIM)
    av = av.reshape(b, t, ATT_KV_HEADS, HEAD_DIM)
    if is_latent:
        rq, rk = axial_rope(rq), axial_rope(rk)
        aq, ak = axial_rope(aq), axial_rope(ak)
        s0f, s0b = s_ctx[:, 0], s_ctx[:, 1]
    else:
        s0f = s0b = jnp.zeros((b, RET_HEADS, RET_DK, RET_DV), F32)
    ro, s_fin = bidir_retention(rq, rk, rv, ret_logit, s0f, s0b)
    ro = rms_norm(ro, ret_gn.reshape(RET_HEADS, RET_DV)).reshape(b, t, RET_W)
    yb = ro.astype(x.dtype) * jax.nn.silu(rz)

    if is_latent:
        ao = latent_attention(aq, ak, av, k_ctx, v_ctx, attn_sink)
    else:
        ao = context_attention(aq, ak, av, attn_sink)
    yc = ao * jax.nn.silu(az)

    ga, gb, gc = jnp.split(jax.nn.sigmoid(mg), N_BRANCH, axis=-1)
    merged = ga * (ya @ w_pa) + gb * (yb @ w_pb) + gc * (yc @ w_pc)
    out = merged @ w_out
    x = x + gate * rms_norm(out, g_post)
    return x, ak, av, s_fin


def setup_inputs(seed: int = 0) -> dict:
    key = jax.random.key(seed)
    ks = jax.random.split(key, 24)

    def nrm(k, shape, s):
        return jax.random.normal(k, shape, F32) * s

    d = D_MODEL
    base_logit = jnp.asarray(np.log(2.0 ** (5 + np.arange(RET_HEADS)) - 1.0), F32)
    return {
        'x_prompt': nrm(ks[0], (BATCH, SEQ, d), 1.0),
        'x_sample': nrm(ks[1], (DEC_BATCH, DEC_SEQ, d), 1.0),
        'cache_k': nrm(ks[2], (DEC_BATCH, DEPTH, PAST_LEN, ATT_KV_HEADS, HEAD_DIM), 1.0),
        'cache_v': nrm(ks[3], (DEC_BATCH, DEPTH, PAST_LEN, ATT_KV_HEADS, HEAD_DIM), 1.0),
        'state_ret': nrm(ks[4], (DEC_BATCH, DEPTH, 2, RET_HEADS, RET_DK, RET_DV), 1.0),
        'c': nrm(ks[5], (DEC_BATCH, d), 1.0),
        'c_ctx': nrm(ks[6], (d,), 1.0),
        'w_mod': nrm(ks[7], (DEPTH, d, 3 * d), 0.5 * d ** -0.5),
        'b_mod': nrm(ks[8], (DEPTH, 3 * d), 0.02),
        'g_pre': 1.0 + nrm(ks[9], (DEPTH, d), 0.02),
        'g_post': 1.0 + nrm(ks[10], (DEPTH, d), 0.02),
        'w_in': nrm(ks[11], (DEPTH, d, IN_W), d ** -0.5),
        'w_four': nrm(ks[12], (DEPTH, FOURIER_W, FOURIER_W), FOURIER_W ** -0.5),
        'ret_decay': base_logit[None, None, :] + nrm(ks[13], (DEPTH, 2, RET_HEADS), 0.1),
        'ret_gn': 1.0 + nrm(ks[14], (DEPTH, RET_W), 0.02),
        'attn_sink': nrm(ks[15], (DEPTH, ATT_KV_HEADS, ATT_GROUP), 0.5),
        'w_branch_a': nrm(ks[16], (DEPTH, FOURIER_W, d), FOURIER_W ** -0.5),
        'w_branch_b': nrm(ks[17], (DEPTH, RET_W, d), RET_W ** -0.5),
        'w_branch_c': nrm(ks[18], (DEPTH, ATT_W, d), ATT_W ** -0.5),
        'w_out': nrm(ks[19], (DEPTH, d, d), d ** -0.5),
    }


def reference(x_prompt, x_sample, cache_k, cache_v, state_ret, c, c_ctx, w_mod, b_mod,
              g_pre, g_post, w_in, w_four, ret_decay, ret_gn, attn_sink,
              w_branch_a, w_branch_b, w_branch_c, w_out):
    xp = x_prompt
    ks_new, vs_new, ss_new = [], [], []
    for l in range(DEPTH):
        xp, k_l, v_l, s_l = trunk_layer(
            xp, c_ctx, w_mod[l], b_mod[l], g_pre[l], g_post[l], w_in[l], w_four[l],
            ret_decay[l], ret_gn[l], attn_sink[l], w_branch_a[l], w_branch_b[l],
            w_branch_c[l], w_out[l])
        ks_new.append(k_l)
        vs_new.append(v_l)
        ss_new.append(s_l)
    new_cache_k = jnp.stack(ks_new, axis=1)
    new_cache_v = jnp.stack(vs_new, axis=1)
    new_state_ret = jnp.stack(ss_new, axis=1)

    xs = x_sample
    for l in range(DEPTH):
        xs = trunk_layer(
            xs, c, w_mod[l], b_mod[l], g_pre[l], g_post[l], w_in[l], w_four[l],
            ret_decay[l], ret_gn[l], attn_sink[l], w_branch_a[l], w_branch_b[l],
            w_branch_c[l], w_out[l],
            k_ctx=cache_k[:, l], v_ctx=cache_v[:, l], s_ctx=state_ret[:, l])[0]

    return (xp, xs, new_cache_k, new_cache_v, new_state_ret)
```

```python
import numpy as np
import ml_dtypes
import concourse.bass as bass
import concourse.mybir as mybir
from concourse.bass_utils import run_bass_kernel_spmd

F32 = mybir.dt.float32
BF = mybir.dt.bfloat16
AF = mybir.ActivationFunctionType
ALU = mybir.AluOpType
AX = mybir.AxisListType

D = 1024
T = 2048
NT = 16
DEPTH = 2
EPS = 1e-6
NEG = -30000.0
DEBUG = False
STOP = None


class _Stop(Exception):
    pass

C_FX, C_FZ, C_RQ, C_RK, C_RV, C_RZ, C_AQ, C_AK, C_AV, C_AZ, C_GA, C_GB, C_GC = (
    0, 256, 512, 768, 1024, 1280, 1536, 2048, 2176, 2304, 2816, 3840, 4864)

CT_DPOS, CT_DNEG, CT_MGE, CT_MLE, CT_IP1, CT_CMI = 0, 128, 256, 384, 512, 640
CT_CM1MP, CT_P, CT_BPREV, CT_BNEXT, CT_BCTX, CT_ZERO, CT_KEEPF, CT_KEEPB, CT_W = 768, 769, 770, 786, 802, 803, 804, 820, 836
BT_ID, BT_PERM, BT_TPREV, BT_TNEXT, BT_BMASK, BT_W = 0, 128, 256, 384, 512, 768

BIG = 1 << 40


class Sched:
    ENG = ("pe", "act", "dve", "pool", "sp")

    def __init__(self, nc, n_dma_sp=24, n_dma_pool=16):
        self.nc = nc
        self.q = {e: [] for e in self.ENG}
        self.sem = {e: nc.alloc_semaphore(f"s_{e}") for e in ("pe", "act", "dve", "pool")}
        self.cnt = {e: 0 for e in ("pe", "act", "dve", "pool")}
        self.dpool = {
            "sp": [nc.alloc_semaphore(f"dsp{i}") for i in range(n_dma_sp)],
            "pool": [nc.alloc_semaphore(f"dpl{i}") for i in range(n_dma_pool)],
        }
        self.duse = {"sp": [0] * n_dma_sp, "pool": [0] * n_dma_pool}
        self.drr = {"sp": 0, "pool": 0}
        self.clock = {e: {} for e in self.ENG}
        self.tokclock = {}
        self.acc = {}
        self.ninst = 0

    @staticmethod
    def _intervals(off, dims):
        dims = [(abs(s), c) for s, c in dims if c > 1 and s != 0]
        dims.sort()
        ivs = [(off, off + 1)]
        for s, c in dims:
            if len(ivs) == 1 and s <= ivs[0][1] - ivs[0][0]:
                lo, hi = ivs[0]
                ivs = [(lo, hi + (c - 1) * s)]
            elif len(ivs) * c <= 64:
                ivs = [(lo + k * s, hi + k * s) for k in range(c) for lo, hi in ivs]
            else:
                lo = min(i[0] for i in ivs)
                hi = max(i[1] for i in ivs)
                ivs = [(lo, hi + (c - 1) * s)]
        return tuple(sorted(ivs))

    @classmethod
    def region(cls, ap):
        t = ap.tensor
        name = t.name
        aps = ap.ap
        off = int(ap.offset)
        tn = type(t).__name__
        esz = {BF: 2, F32: 4}.get(ap.dtype, 4)
        if tn.startswith("DRam"):
            return (name, 0, 1, cls._intervals(off * esz, [(s_ * esz, c_) for s_, c_ in aps] + [(1, esz)]))
        if tn.startswith("PSum"):
            ivs = cls._intervals(off, aps[1:]) if aps[0][0] else cls._intervals(off, aps)
            pstep = aps[0][0]
            banks = set()
            for lo, hi in ivs:
                if pstep:
                    lo, hi = lo % pstep, (hi - 1) % pstep + 1
                b0, b1 = (lo * esz) // 2048, ((hi * esz) - 1) // 2048
                banks.update(range(b0, b1 + 1))
            return ("PSUM", 0, 128, tuple((b * 2048, (b + 1) * 2048) for b in sorted(banks)))
        pstep, npart = aps[0]
        if pstep == 0:
            p0, f0 = 0, off
        else:
            p0, f0 = off // pstep, off % pstep
        return (name, p0, p0 + npart, cls._intervals(f0 * esz, [(s_ * esz, c_) for s_, c_ in aps[1:]] + [(1, esz)]))

    @staticmethod
    def _ov(a, b):
        if not (a[1] < b[2] and b[1] < a[2]):
            return False
        for lo, hi in a[3]:
            for lo2, hi2 in b[3]:
                if lo < hi2 and lo2 < hi:
                    return True
        return False

    @staticmethod
    def _contains(outer, inner):
        if not (outer[1] <= inner[1] and inner[2] <= outer[2]):
            return False
        for lo, hi in inner[3]:
            ok = False
            for lo2, hi2 in outer[3]:
                if lo2 <= lo and hi <= hi2:
                    ok = True
                    break
            if not ok:
                return False
        return True

    @staticmethod
    def _need(need, key, val):
        if need.get(key, 0) < val:
            need[key] = val

    def _collect(self, rregs, wregs):
        need = {}
        for r in rregs:
            ispsum = r[0] == "PSUM"
            for rec in self.acc.get(r[0], ()):
                if (rec[1] or ispsum) and self._ov(r, rec[0]):
                    self._need(need, rec[2], rec[3])
        for w in wregs:
            for rec in self.acc.get(w[0], ()):
                if self._ov(w, rec[0]):
                    self._need(need, rec[2], rec[3])
        return need

    def _record(self, rregs, wregs, key, val):
        for w in wregs:
            lst = self.acc.setdefault(w[0], [])
            lst[:] = [rec for rec in lst if not self._contains(w, rec[0])]
            lst.append((w, True, key, val))
        for r in rregs:
            lst = self.acc.setdefault(r[0], [])
            if r[0] == "PSUM":
                lst[:] = [rec for rec in lst if not self._contains(r, rec[0])]
                lst.append((r, True, key, val))
                continue
            if not isinstance(key, tuple):
                lst[:] = [rec for rec in lst if not (rec[2] == key and not rec[1] and rec[0] == r)]
            lst.append((r, False, key, val))

    def _semof(self, key):
        if isinstance(key, tuple):
            return self.dpool[key[0]][key[1]]
        return self.sem[key]

    def _waits(self, eng, need):
        ck = self.clock[eng]
        out = []
        for key, val in need.items():
            if key == "pe" and eng == "pe":
                continue
            if ck.get(key, 0) >= val:
                continue
            out.append((key, val))
        for key, val in out:
            tc = self.tokclock.get((key, val))
            if tc:
                for k2, v2 in tc.items():
                    if ck.get(k2, 0) < v2:
                        ck[k2] = v2
            if ck.get(key, 0) < val:
                ck[key] = val
        return [(self._semof(k), v) for k, v in out]

    def op(self, eng, fn, reads=(), writes=(), signal=True, check_w=True):
        rregs = [self.region(a) for a in reads]
        wregs = [self.region(a) for a in writes]
        need = self._collect(rregs, wregs if check_w else ())
        waits = self._waits(eng, need)
        val = self.cnt[eng] + 1
        sem = self.sem[eng]
        if signal:
            self.cnt[eng] = val
            self.tokclock[(eng, val)] = dict(self.clock[eng])

        def emit(e, fn=fn, waits=waits, signal=signal, sem=sem):
            for s, v in waits:
                e.wait_ge(s, v)
            ins = fn(e)
            if signal:
                ins.then_inc(sem, 1)

        self.q[eng].append(emit)
        self._record(rregs, wregs, eng, val)
        self.ninst += 1

    def dma(self, eng, out, in_, **kw):
        i = self.drr[eng]
        self.drr[eng] = (i + 1) % len(self.dpool[eng])
        use = self.duse[eng][i]
        key = (eng, i)
        rregs = [self.region(in_)]
        wregs = [self.region(out)]
        need = self._collect(rregs, wregs)
        if use > 0:
            self._need(need, key, 16 * use)
        waits = self._waits(eng, need)
        self.duse[eng][i] = use + 1
        val = 16 * (use + 1)
        sem = self.dpool[eng][i]
        self.tokclock[(key, val)] = dict(self.clock[eng])

        def emit(e, waits=waits, sem=sem, out=out, in_=in_, kw=kw):
            for s, v in waits:
                e.wait_ge(s, v)
            e.dma_start(out=out, in_=in_, **kw).then_inc(sem, 16)

        self.q[eng].append(emit)
        self._record(rregs, wregs, key, val)
        self.ninst += 1

    def finish(self):
        need = {}
        for eng in ("sp", "pool"):
            for i, use in enumerate(self.duse[eng]):
                if use:
                    need[(eng, i)] = 16 * use
        for e in ("pe", "act", "dve", "pool"):
            if self.cnt[e]:
                need[e] = self.cnt[e]
        waits = [(self._semof(k), v) for k, v in need.items()]

        def emit(e, waits=waits):
            for s, v in waits:
                e.wait_ge(s, v)

        self.q["sp"].append(emit)

    def run(self):
        nc = self.nc
        q = self.q
        with nc.Block() as block:

            @block.tensor
            def _(e):
                for f in q["pe"]:
                    f(e)

            @block.scalar
            def _(e):
                for f in q["act"]:
                    f(e)

            @block.vector
            def _(e):
                for f in q["dve"]:
                    f(e)

            @block.gpsimd
            def _(e):
                for f in q["pool"]:
                    f(e)

            @block.sync
            def _(e):
                for f in q["sp"]:
                    f(e)


class Ring:
    def __init__(self, items):
        self.items = list(items)
        self.i = 0

    def get(self):
        x = self.items[self.i]
        self.i = (self.i + 1) % len(self.items)
        return x


def build_nc(debug=False):
    nc = bass.Bass("TRN2", target_bir_lowering=False)
    S = Sched(nc)
    dbg_outs = []

    def din(name, shape, dt=F32):
        return nc.dram_tensor(name, list(shape), dt, kind="ExternalInput").ap()

    def dout(name, shape, dt=F32):
        return nc.dram_tensor(name, list(shape), dt, kind="ExternalOutput").ap()

    x_in = din("x", [T, D])
    cvec = din("cvec", [D])
    kctxT = din("kctxT", [DEPTH, 128, 512])
    vctx = din("vctx", [DEPTH, 512, 128])
    s0 = din("s0", [DEPTH, 2, 128, 2, 64])
    w_mod = din("w_mod", [DEPTH, D, 3 * D])
    b_mod = din("b_mod", [DEPTH, 3 * D])
    g_pre = din("g_pre", [DEPTH, D])
    g_post = din("g_post", [DEPTH, D])
    w_in = din("w_in", [DEPTH, D, 5888])
    w_four = din("w_four", [DEPTH, 256, 256])
    ret_decay = din("ret_decay", [DEPTH, 8])
    ret_gn = din("ret_gn", [DEPTH, 256])
    attn_sink = din("attn_sink", [DEPTH, 8])
    w_pa = din("w_pa", [DEPTH, 256, D])
    w_pb = din("w_pb", [DEPTH, 256, D])
    w_pc = din("w_pc", [DEPTH, 512, D])
    w_out = din("w_out", [DEPTH, D, D])
    dftc = din("dftc", [4, 128, 8192], BF)
    dfts = din("dfts", [4, 128, 8192], BF)
    cdft_d = din("cdft", [256, 512], BF)
    ropec_d = din("ropec", [128, T], BF)
    ropes_d = din("ropes", [128, T], BF)
    ctab_d = din("ctab", [128, CT_W])
    btab_d = din("btab", [128, BT_W], BF)

    y_out = dout("y", [T, D])
    ck_out = dout("ck", [DEPTH, T, 128])
    cv_out = dout("cv", [DEPTH, T, 128])
    st_out = dout("st", [DEPTH, 2, 8, 128, 2, 64])
    x1s = nc.dram_tensor("x1s", [T, D], F32, kind="Internal").ap()

    sb = nc.alloc_sbuf_tensor
    hT = sb("hT", [128, 8, T], BF)
    ybuf = sb("ybuf", [128, 8, T], BF)
    ARN = 24576
    arena = sb("arena", [128, ARN], BF)
    NWS = 4
    wsb = [sb(f"ws{i}", [128, 2048], BF) for i in range(NWS)]
    gv_bc = [sb(f"gvbc{l}", [128, D], F32) for l in range(DEPTH)]
    sh_bc = [sb(f"shbc{l}", [128, D], F32) for l in range(DEPTH)]
    gg_bc = [sb(f"ggbc{l}", [128, D], F32) for l in range(DEPTH)]
    ropec = sb("ropec_sb", [128, T], BF)
    ropes = sb("ropes_sb", [128, T], BF)
    ctab = sb("ctab_sb", [128, CT_W], F32)
    btab = sb("btab_sb", [128, BT_W], BF)
    DT = sb("DT", [128, 4, 128], BF)
    gn_bc = sb("gn_bc", [128, 256], F32)
    xst = Ring([sb(f"xst{i}", [128, D], F32) for i in range(3)])
    ftr = Ring([sb(f"ft{i}", [128, 512], F32) for i in range(4)])
    _bt = [sb(f"bt{i}", [128, 512], BF) for i in range(6)]
    btr = Ring(_bt[0:4])
    hbr = Ring([sb(f"hb{i}", [128, D], BF) for i in range(2)])
    robr = Ring([sb(f"rob{i}", [128, 256], BF) for i in range(2)])
    dfr = Ring([sb(f"df{i}", [128, 512], BF) for i in range(2)])
    kt_a = _bt[4]
    kt_b = _bt[5]
    junk = sb("junk", [128, D], BF)
    small = sb("small", [128, 256], F32)
    stat = sb("stat", [128, 8, NT], F32)
    sc_sb = sb("sc_sb", [128, 8], BF)
    Sring = Ring([sb(f"S{i}", [128, 2, 128], F32) for i in range(4)])

    ident = btab[:, BT_ID:BT_ID + 128]
    perm = btab[:, BT_PERM:BT_PERM + 128]

    PS = nc.alloc_psum_tensor("PS", [128, 4096], F32)
    ps = [PS[:, i * 512:(i + 1) * 512] for i in range(8)]
    psbf = [p.bitcast(BF) for p in ps]
    bmask3 = btab[:, BT_BMASK:BT_BMASK + 256].rearrange("p (m c) -> p m c", m=2)
    rotA = Ring([2, 3, 4, 5, 6, 7])
    rotAll = Ring(list(range(8)))
    pairs = Ring([4, 6])
    rotB = Ring([4, 5, 6, 7])

    def mm(out, lhsT, rhs, start=True, stop=True, signal=None, check_w=None, skip=False):
        kw = {"skip_group_check": True} if skip else {}
        S.op("pe", lambda e: e.matmul(out, lhsT, rhs, start=start, stop=stop, **kw),
             reads=[lhsT, rhs], writes=[out],
             signal=stop if signal is None else signal,
             check_w=start if check_w is None else check_w)

    def tr(out, in_):
        S.op("pe", lambda e: e.transpose(out, in_, ident), reads=[in_, ident], writes=[out])

    def act(out, in_, func, reads=None, **kw):
        extra = [v for v in (kw.get("scale"), kw.get("bias"), kw.get("accum_out")) if hasattr(v, "tensor")]
        wr = [out] + ([kw["accum_out"]] if kw.get("accum_out") is not None else [])
        rd = [in_] + [v for v in (kw.get("scale"), kw.get("bias")) if hasattr(v, "tensor")]
        S.op("act", lambda e: e.activation(out, in_, func, **kw), reads=rd, writes=wr)

    def tt(eng, out, in0, in1, op):
        S.op(eng, lambda e: e.tensor_tensor(out, in0, in1, op), reads=[in0, in1], writes=[out])

    def ts(eng, out, in0, s1, s2, op0, op1=None):
        rd = [in0] + [v for v in (s1, s2) if hasattr(v, "tensor")]
        if op1 is None:
            S.op(eng, lambda e: e.tensor_scalar(out, in0, s1, None, op0), reads=rd, writes=[out])
        else:
            S.op(eng, lambda e: e.tensor_scalar(out, in0, s1, s2, op0, op1), reads=rd, writes=[out])

    def stt(eng, out, in0, scalar, in1, op0, op1):
        rd = [in0, in1] + ([scalar] if hasattr(scalar, "tensor") else [])
        S.op(eng, lambda e: e.scalar_tensor_tensor(out, in0, scalar, in1, op0, op1), reads=rd, writes=[out])

    def cp(eng, out, in_):
        S.op(eng, lambda e: e.tensor_copy(out, in_), reads=[in_], writes=[out])

    def ckpt(name):
        if STOP == name:
            raise _Stop()

    def dbg(name, ap):
        if not debug:
            return
        shp = list(ap.shape)
        d = nc.dram_tensor("dbg_" + name, shp, ap.dtype, kind="ExternalOutput").ap()
        S.dma("sp", d, ap)
        dbg_outs.append("dbg_" + name)

    jobs = []

    class WS:
        issued = 0
        used = 0

    def ws_issue_upto(n):
        while WS.issued < min(n, len(jobs)):
            j = WS.issued
            buf = wsb[j % NWS]
            for dst_fn, src in jobs[j]:
                S.dma("pool", dst_fn(buf), src)
            WS.issued += 1

    def ws_next(hold=0):
        j = WS.used
        ws_issue_upto(j - hold + NWS)
        WS.used += 1
        return wsb[j % NWS]

    def job_cols(src2d, c0, ncols, kc=8):
        src = src2d[:, c0:c0 + ncols].rearrange("(k p) c -> p k c", p=128)
        return [(lambda b, kc=kc, ncols=ncols: b[:, 0:kc * ncols].rearrange("p (k c) -> p k c", k=kc), src)]

    def wview(buf, kc, ncols):
        return buf[:, 0:kc * ncols].rearrange("p (k c) -> p k c", k=kc)

    for ng in range(12):
        jobs.append(job_cols(w_mod[0], ng * 256, 256))
    for l in range(DEPTH):
        jobs.append(job_cols(w_four[l], 0, 256, kc=2))
        jobs.append(job_cols(w_in[l], C_FX, 256))
        jobs.append(job_cols(w_in[l], C_RQ, 256))
        jobs.append(job_cols(w_in[l], C_RK, 256))
        jobs.append(job_cols(w_in[l], C_RV, 256))
        jobs.append(job_cols(w_in[l], C_AK, 256))
        jobs.append(job_cols(w_in[l], C_AQ, 256))
        jobs.append(job_cols(w_in[l], C_AQ + 256, 256))
        akj = []
        for g_ in range(2):
            srck = w_in[l][:, C_AK + g_ * 64:C_AK + (g_ + 1) * 64].rearrange("(k p) d -> p k d", p=128)
            for u_ in range(2):
                akj.append((lambda b, g_=g_, u_=u_: b[:, 0:2048].rearrange(
                    "p (k g u d) -> p k g u d", k=8, g=2, u=2)[:, :, g_, u_, :], srck))
        jobs.append(akj)
        if l + 1 < DEPTH:
            for ng in range(12):
                jobs.append(job_cols(w_mod[l + 1], ng * 256, 256))
        jobs.append(job_cols(w_in[l], C_FZ, 256))
        jobs.append(job_cols(w_in[l], C_RZ, 256))
        jobs.append(job_cols(w_in[l], C_AZ, 256))
        jobs.append(job_cols(w_in[l], C_AZ + 256, 256))
        for half in range(2):
            for dp in range(4):
                jobs.append(job_cols(w_in[l], C_GA + dp * 256, 256))
                jobs.append(job_cols(w_in[l], C_GB + dp * 256, 256))
                jobs.append(job_cols(w_in[l], C_GC + dp * 256, 256))

    try:
        S.dma("sp", ctab[:], ctab_d)
        S.dma("sp", btab[:], btab_d)
        S.dma("sp", ropec[:], ropec_d)
        S.dma("sp", ropes[:], ropes_d)

        cv_col = small[:, 0:8]
        sc_col = sc_sb[:, :]
        ones_r = small[0:1, 128:256]
        S.dma("sp", cv_col, cvec.rearrange("(k p) -> p k", p=128), allow_slow_non_contiguous=True)
        act(sc_col, cv_col, AF.Silu)
        S.op("dve", lambda e: e.memset(ones_r, 1.0), writes=[ones_r])

        def compute_mod(l):
            xA, xB = xst.items[0], xst.items[1]
            rowm = xA[0:1, 0:512]
            rowb = xA[0:1, 512:1024]
            rowg = xB[0:1, 0:512]
            rowr = xB[0:1, 512:1024]
            for part in range(3):
                for n in range(2):
                    c0 = part * D + n * 512
                    S.dma("sp", rowb, b_mod[l:l + 1, c0:c0 + 512])
                    bnk = rotA.get()
                    for q2 in range(2):
                        wv = wview(ws_next(), 8, 256)
                        for k in range(8):
                            mm(ps[bnk][0:1, q2 * 256:(q2 + 1) * 256], sc_col[:, k:k + 1], wv[:, k, :],
                               start=(k == 0), stop=(k == 7), signal=(q2 == 1 and k == 7), check_w=(q2 == 0 and k == 0))
                    tt("dve", rowm, ps[bnk][0:1, :], rowb, ALU.add)
                    if part == 0:
                        src_row, dst = rowm, sh_bc[l]
                    elif part == 1:
                        S.dma("sp", rowg, g_pre[l:l + 1, n * 512:(n + 1) * 512])
                        stt("dve", rowr, rowm, 1.0, rowg, ALU.add, ALU.mult)
                        src_row, dst = rowr, gv_bc[l]
                    else:
                        S.dma("sp", rowg, g_post[l:l + 1, n * 512:(n + 1) * 512])
                        tt("dve", rowr, rowm, rowg, ALU.mult)
                        src_row, dst = rowr, gg_bc[l]
                    b2 = rotA.get()
                    mm(ps[b2][:, :], ones_r, src_row)
                    act(dst[:, n * 512:(n + 1) * 512], ps[b2][:, :], AF.Copy)

        xres = [ybuf[:, t, :].bitcast(F32) for t in range(8)] + \
               [arena[:, (t - 8) * 2048:(t - 7) * 2048].bitcast(F32) for t in range(8, NT)]
        for t in range(NT):
            S.dma("sp", xres[t], x_in[t * 128:(t + 1) * 128, :])
            act(junk[:], xres[t], AF.Square, accum_out=stat[:, 4, t:t + 1])
        compute_mod(0)

        def nrm_batch_stats(c0=0, c1=NT):
            ts("dve", stat[:, 5, c0:c1], stat[:, 4, c0:c1], 1.0 / D, EPS, ALU.mult, ALU.add)
            act(stat[:, 6, c0:c1], stat[:, 5, c0:c1], AF.Ln)
            act(stat[:, 7, c0:c1], stat[:, 6, c0:c1], AF.Exp, scale=-0.5)

        def nrm_apply(xt, t, l, tmp_ring=None):
            hb = hbr.get()
            for n in range(2):
                f = (tmp_ring or ftr).get()
                stt("dve", f[:], xt[:, n * 512:(n + 1) * 512], stat[:, 7, t:t + 1], gv_bc[l][:, n * 512:(n + 1) * 512],
                    ALU.mult, ALU.mult)
                tt("pool" if (t + n) % 2 == 0 else "dve", hb[:, n * 512:(n + 1) * 512], f[:],
                   sh_bc[l][:, n * 512:(n + 1) * 512], ALU.add)
            return hb

        def nrm_transpose(hb, t, bnk):
            for k in range(8):
                tr(psbf[bnk][:, k * 128:(k + 1) * 128], hb[:, k * 128:(k + 1) * 128])
            act(hT[:, :, t * 128:(t + 1) * 128], psbf[bnk][:, :].rearrange("p (k c) -> p k c", k=8), AF.Copy)

        ckpt("setup")
        for l in range(DEPTH):
            xsrc = x_in if l == 0 else x1s
            xdst = x1s if l == 0 else y_out

            def norm_pass_C_steps(src_dram, lnorm, tiles, tmp_ring=None, resident=None):
                hbs, xts_ = {}, {}
                steps = []
                seq = list(tiles)
                n_ = len(seq)

                def load(i):
                    if resident is not None and seq[i] in resident:
                        xts_[i] = resident[seq[i]]
                        return
                    xts_[i] = xst.get()
                    S.dma("sp", xts_[i][:], src_dram[seq[i] * 128:(seq[i] + 1) * 128, :])

                def mk(i):
                    def step():
                        if i == 0:
                            load(0)
                        if i + 1 < n_:
                            load(i + 1)
                        if i < n_:
                            hbs[i] = nrm_apply(xts_[i], seq[i], lnorm, tmp_ring)
                        if i >= 1:
                            nrm_transpose(hbs[i - 1], seq[i - 1], 6 + (i % 2))
                    return step
                for i in range(n_ + 1):
                    steps.append(mk(i))
                return steps

            def norm_pass_C(src_dram, lnorm):
                for st_ in norm_pass_C_steps(src_dram, lnorm, range(NT)):
                    st_()

            if l == 0:
                nrm_batch_stats()
                hbs0 = {}
                for step in range(NT + 1):
                    if step < NT:
                        hbs0[step] = nrm_apply(xres[step], step, 0)
                    if step >= 1:
                        nrm_transpose(hbs0[step - 1], step - 1, 6 + (step % 2))
            if l == 0:
                dbg("hT", hT[:, 0, :])
                ckpt("hT")

            def proj_ws(wv, sub, tg, bnk):
                for k in range(8):
                    mm(ps[bnk][:, :], wv[:, k, sub * 128:(sub + 1) * 128], hT[:, k, tg * 512:(tg + 1) * 512],
                       start=(k == 0), stop=(k == 7))

            rope_pend = []

            def rope_flush(keep=0):
                while len(rope_pend) > keep:
                    qb, dst, tg = rope_pend.pop(0)
                    b2 = rotA.get()
                    mm(ps[b2][:, :], perm, qb[:])
                    t1 = ftr.get()
                    tt("pool", t1[:], qb[:], ropec[:, tg * 512:(tg + 1) * 512], ALU.mult)
                    t2 = ftr.get()
                    tt("dve", t2[:], ps[b2][:, :], ropes[:, tg * 512:(tg + 1) * 512], ALU.mult)
                    tt("dve", dst, t1[:], t2[:], ALU.add)

            def rope_from_psum(bnk, dst, tg, scale=1.0):
                qb = btr.get()
                act(qb[:], ps[bnk][:, :], AF.Copy, scale=scale)
                rope_pend.append((qb, dst, tg))
                rope_flush(keep=1)

            Abuf = arena[:, 8192:16384].rearrange("p (t c) -> p t c", t=NT)
            dbufs = Ring([arena[:, 0:8192].rearrange("p (t c) -> p t c", t=NT),
                          arena[:, 16384:24576].rearrange("p (t c) -> p t c", t=NT)])
            fxT = arena[:, 16384:20480].rearrange("p (m t) -> p m t", m=2)
            W4x = arena[:, 20480:21504].rearrange("p (m c) -> p m c", m=2)
            cdft = arena[:, 21504:22528].rearrange("p (k c) -> p k c", k=2)
            def dft_load(db, src3):
                S.dma("sp", db[:, 0:8, :], src3[:, 0:8, :])
                S.dma("pool", db[:, 8:16, :], src3[:, 8:16, :])

            db_first = dbufs.get()
            dft_load(db_first, dftc[0].rearrange("p (t c) -> p t c", t=NT))
            S.dma("sp", cdft, cdft_d.rearrange("(k p) c -> p k c", p=128))

            w4 = wview(ws_next(), 2, 256)
            for o in range(2):
                for m in range(2):
                    bnk = rotA.get()
                    for kc in range(2):
                        mm(ps[bnk][:, 0:256], cdft[:, kc, o * 256 + m * 128:o * 256 + (m + 1) * 128], w4[:, kc, :],
                           start=(kc == 0), stop=(kc == 1))
                    act(W4x[:, m, o * 256:(o + 1) * 256], ps[bnk][:, 0:256], AF.Copy)
            wv = wview(ws_next(), 8, 256)
            for sub in range(2):
                for tg in range(4):
                    bnk = rotA.get()
                    proj_ws(wv, sub, tg, bnk)
                    act(fxT[:, sub, tg * 512:(tg + 1) * 512], ps[bnk][:, :], AF.Copy)
            for t in range(NT):
                bnk = rotA.get()
                for m in range(2):
                    mm(ps[bnk][:, :], fxT[:, m, t * 128:(t + 1) * 128], W4x[:, m, :], start=(m == 0), stop=(m == 1))
                cp("dve", Abuf[:, t, :], ps[bnk][:, :])
            for kg in range(4):
                bks = [rotA.get(), rotA.get()]
                for o in range(2):
                    if kg == 0 and o == 0:
                        db = db_first
                    else:
                        db = dbufs.get()
                        src = (dftc if o == 0 else dfts)[kg].rearrange("p (t c) -> p t c", t=NT)
                        dft_load(db, src)
                    for m in range(2):
                        for t in range(NT):
                            mm(ps[bks[m]][:, :], Abuf[:, t, o * 256 + m * 128:o * 256 + (m + 1) * 128], db[:, t, :],
                               start=(o == 0 and t == 0), stop=(o == 1 and t == NT - 1))
                for m in range(2):
                    act(ybuf[:, m, kg * 512:(kg + 1) * 512], ps[bks[m]][:, :], AF.Copy)
            if l == 0:
                dbg("yapre", ybuf[:, 0, :])
                ckpt("yapre")

            rqT = arena[:, 0:4096].rearrange("p (m t) -> p m t", m=2)
            rkT = arena[:, 4096:8192].rearrange("p (m t) -> p m t", m=2)
            rvb = arena[:, 8192:12288].rearrange("p (t c) -> p t c", t=NT)
            SBin = arena[:, 12288:16384].rearrange("p (t m c) -> p t m c", t=NT, m=2)
            Vaug = arena[:, 16384:20480].rearrange("p (t g e) -> p t g e", t=NT, g=2)
            vcx = arena[:, 20480:21504].rearrange("p (t g e) -> p t g e", t=4, g=2)
            kcx = arena[:, 21504:22528].rearrange("p (g s) -> p g s", g=2)
            lgb = small[:, 0:8]
            lg = small[:, 8:16]
            lgsel = small[:, 16:20].rearrange("p (d m) -> p d m", d=2)
            g128 = small[:, 20:24].rearrange("p (d m) -> p d m", d=2)
            wfb = small[:, 24:32]
            tmp8 = small[:, 32:40]
            gk = small[:, 40:104].rearrange("p (d m c) -> p d m c", d=2, m=2)
            RF = arena[:, 22528:22784].rearrange("p (m i) -> p m i", m=2)
            RB = arena[:, 22784:23040].rearrange("p (m i) -> p m i", m=2)

            S.dma("sp", lgb, ret_decay[l].partition_broadcast(128))
            act(tmp8, lgb, AF.Exp, scale=-1.0)
            act(tmp8, tmp8, AF.Ln, bias=1.0)
            ts("dve", lg, tmp8, -1.0, None, ALU.mult)
            lgv = lg.rearrange("p (d m q) -> p d m q", d=2, m=2)
            cp("dve", lgsel[0:64, :, :], lgv[0:64, :, :, 0])
            cp("dve", lgsel[64:128, :, :], lgv[64:128, :, :, 1])
            for m in range(2):
                act(RF[:, m, :], ctab[:, CT_IP1:CT_IP1 + 128], AF.Exp, scale=lgsel[:, 0, m:m + 1])
                act(RB[:, m, :], ctab[:, CT_CMI:CT_CMI + 128], AF.Exp, scale=lgsel[:, 1, m:m + 1])
            for h in range(4):
                f1 = ftr.get()
                act(f1[:, 0:128], ctab[:, CT_DPOS:CT_DPOS + 128], AF.Exp, scale=lg[:, h:h + 1])
                tt("dve", DT[:, h, :], f1[:, 0:128], ctab[:, CT_MGE:CT_MGE + 128], ALU.mult)
                act(f1[:, 128:256], ctab[:, CT_DNEG:CT_DNEG + 128], AF.Exp, scale=lg[:, 4 + h:5 + h])
                tt("dve", f1[:, 128:256], f1[:, 128:256], ctab[:, CT_MLE:CT_MLE + 128], ALU.mult)
                tt("dve", DT[:, h, :], DT[:, h, :], f1[:, 128:256], ALU.add)
            ts("dve", tmp8[:, 0:4], lg[:, 0:4], ctab[:, CT_CM1MP:CT_CM1MP + 1], None, ALU.mult)
            ts("dve", tmp8[:, 4:8], lg[:, 4:8], ctab[:, CT_P:CT_P + 1], None, ALU.mult)
            act(wfb, tmp8, AF.Exp)
            act(g128.rearrange("p d m -> p (d m)"), lgsel.rearrange("p d m -> p (d m)"), AF.Exp, scale=128.0)
            for d_ in range(2):
                kcol = CT_KEEPF if d_ == 0 else CT_KEEPB
                for m in range(2):
                    ts("dve", gk[:, d_, m, :], ctab[:, kcol:kcol + 16], g128[:, d_, m:m + 1], None, ALU.mult)
            S.dma("sp", gn_bc[:], ret_gn[l].partition_broadcast(128))

            ckpt("ret_tables")
            wv = wview(ws_next(), 8, 256)
            for sub in range(2):
                for tg in range(4):
                    bnk = rotA.get()
                    proj_ws(wv, sub, tg, bnk)
                    rope_from_psum(bnk, rqT[:, sub, tg * 512:(tg + 1) * 512], tg)
            wv = wview(ws_next(), 8, 256)
            for sub in range(2):
                for tg in range(4):
                    bnk = rotA.get()
                    proj_ws(wv, sub, tg, bnk)
                    rope_from_psum(bnk, rkT[:, sub, tg * 512:(tg + 1) * 512], tg, scale=0.125)
            ckpt("ret_proj")
            rope_flush()
            wrv = wview(ws_next(), 8, 256)
            wkv = wview(ws_next(hold=1), 8, 256)
            S.op("pool", lambda e: e.memset(Vaug[:, :, :, 64:128], 1.0), writes=[Vaug[:, :, :, 64:128]])
            S.op("pool", lambda e: e.memset(vcx[:, :, :, 64:128], 1.0), writes=[vcx[:, :, :, 64:128]])
            for t in range(NT):
                bnk = rotA.get()
                for k in range(8):
                    mm(ps[bnk][:, 0:256], hT[:, k, t * 128:(t + 1) * 128], wrv[:, k, :], start=(k == 0), stop=(k == 7),
                       signal=False)
                for k in range(8):
                    mm(ps[bnk][:, 256:512], hT[:, k, t * 128:(t + 1) * 128], wkv[:, k, :], start=(k == 0), stop=(k == 7),
                       check_w=False)
                act(rvb[:, t, :], ps[bnk][:, 0:256], AF.Copy)
                kvs = ftr.get()
                cp("dve", kvs[:, 0:256], ps[bnk][:, 256:512])
                cp("pool", Vaug[:, t, :, 0:64], kvs[:, 128:256].rearrange("p (g d) -> p g d", g=2))
                S.dma("sp", ck_out[l, t * 128:(t + 1) * 128, :], kvs[:, 0:128])
                S.dma("sp", cv_out[l, t * 128:(t + 1) * 128, :], kvs[:, 128:256])

            ckpt("ret_tok")
            def load_s0(dir_):
                st = Sring.get()
                S.op("pool", lambda e: e.memset(st[:], 0.0), writes=[st[:]])
                S.dma("sp", st[0:64, :, 0:64], s0[l, dir_, 0:64, :, :])
                S.dma("sp", st[64:128, :, 64:128], s0[l, dir_, 64:128, :, :])
                return st

            ubanks = Ring([2, 3])
            obanks = Ring([0, 1])
            ktr = Ring([kt_a, kt_b])

            def compute_U(c, dir_):
                bnk = rotB.get()
                for m in range(2):
                    tr(psbf[bnk][:, m * 128:(m + 1) * 128], rkT[:, m, c * 128:(c + 1) * 128])
                kt = ktr.get()
                tt("dve", kt[:, 0:256].rearrange("p (h d) -> p h d", h=4),
                   psbf[bnk][:, 0:256].rearrange("p (h d) -> p h d", h=4),
                   wfb[:, dir_ * 4:(dir_ + 1) * 4].unsqueeze(2).to_broadcast([128, 4, 64]), ALU.mult)
                ub = ubanks.get()
                for m in range(2):
                    mm(ps[ub][:, m * 128:(m + 1) * 128], kt[:, m * 128:(m + 1) * 128], rvb[:, c, m * 128:(m + 1) * 128],
                       signal=(m == 1), check_w=(m == 0))
                return ub

            def state_update(c, dir_, sprev, ub):
                snew = Sring.get()
                for m in range(2):
                    stt("dve", snew[:, m, :], sprev[:, m, :], gk[:, dir_, m, c:c + 1], ps[ub][:, m * 128:(m + 1) * 128],
                        ALU.mult, ALU.add)
                return snew

            def store_state(dir_, seq, st):
                S.dma("sp", st_out[l, dir_, seq, 0:64, :, :], st[0:64, :, 0:64])
                S.dma("sp", st_out[l, dir_, seq, 64:128, :, :], st[64:128, :, 64:128])

            sprev = load_s0(1)
            ub = compute_U(NT - 1, 1)
            for c in range(NT - 1, -1, -1):
                ub_next = compute_U(c - 1, 1) if c > 0 else None
                stt("dve", SBin[:, c, :, :], sprev[:], ctab[:, CT_KEEPB + c:CT_KEEPB + c + 1], bmask3, ALU.mult, ALU.mult)
                sprev = state_update(c, 1, sprev, ub)
                if c % 2 == 0:
                    store_state(1, c // 2, sprev)
                ub = ub_next

            ckpt("ret_bwd")
            fstate = {"s": load_s0(0), "ub": compute_U(0, 0)}
            bOs, robs = {}, {}

            def fwd_S1(c):
                sprev = fstate["s"]
                ub_next = compute_U(c + 1, 0) if c + 1 < NT else None
                sfin = btr.get()
                sfv = sfin[:, 0:256].rearrange("p (m c) -> p m c", m=2)
                stt("dve", sfv, sprev[:], ctab[:, CT_KEEPF + c:CT_KEEPF + c + 1], bmask3, ALU.mult, ALU.mult)
                qs = btr.get()
                qf = qs[:, 0:256].rearrange("p (m i) -> p m i", m=2)
                qbk = qs[:, 256:512].rearrange("p (m i) -> p m i", m=2)
                tt("pool", qf, rqT[:, :, c * 128:(c + 1) * 128], RF, ALU.mult)
                tt("pool", qbk, rqT[:, :, c * 128:(c + 1) * 128], RB, ALU.mult)
                bA = pairs.get()
                for h in range(4):
                    m, par = h // 2, h % 2
                    mm(ps[bA + par][:, m * 128:(m + 1) * 128], rkT[par * 64:(par + 1) * 64, m, c * 128:(c + 1) * 128],
                       rqT[par * 64:(par + 1) * 64, m, c * 128:(c + 1) * 128], signal=(h == 3), check_w=(h < 2))
                attb = btr.get()
                tt("dve", attb[:].rearrange("p (m r i) -> p r m i", m=2, r=2),
                   PS[:, bA * 512:(bA + 2) * 512].rearrange("p (r x) -> p r x", r=2)[:, :, 0:256].rearrange(
                       "p r (m i) -> p r m i", m=2),
                   DT[:].rearrange("p (m r) i -> p r m i", m=2), ALU.mult)
                bO = obanks.get()
                bOs[c] = bO
                for m in range(2):
                    for par in range(2):
                        h = 2 * m + par
                        mm(ps[bO][:, h * 64:(h + 1) * 64], attb[:, h * 128:(h + 1) * 128], rvb[:, c, h * 64:(h + 1) * 64],
                           start=(h == 0), stop=False, signal=False, check_w=(h == 0), skip=True)
                    mm(ps[bO][:, m * 128:(m + 1) * 128], qf[:, m, :], sfv[:, m, :],
                       start=False, stop=False, signal=False, check_w=False, skip=True)
                    mm(ps[bO][:, m * 128:(m + 1) * 128], qbk[:, m, :], SBin[:, c, m, :],
                       start=False, stop=True, signal=(m == 1), check_w=False, skip=True)
                fstate["s"] = state_update(c, 0, sprev, fstate["ub"])
                fstate["ub"] = ub_next
                if c % 2 == 1:
                    store_state(0, c // 2, fstate["s"])

            def fwd_S2(c):
                bO = bOs[c]
                sq = ftr.get()
                act(sq[:, 0:256], ps[bO][:, 0:256], AF.Square)
                st4 = stat[:, 0:4, c]
                S.op("dve", lambda e, sq=sq, st4=st4: e.reduce_sum(st4, sq[:, 0:256].rearrange("p (h d) -> p h d", h=4), AX.X),
                     reads=[sq[:, 0:256]], writes=[st4])
                ts("dve", st4, st4, 1.0 / 64, EPS, ALU.mult, ALU.add)
                act(st4, st4, AF.Ln)
                act(st4, st4, AF.Exp, scale=-0.5)
                tt("dve", sq[:, 256:512].rearrange("p (h d) -> p h d", h=4),
                   ps[bO][:, 0:256].rearrange("p (h d) -> p h d", h=4),
                   st4.unsqueeze(2).to_broadcast([128, 4, 64]), ALU.mult)
                rob = robr.get()
                robs[c] = rob
                tt("pool", rob[:, 0:256], sq[:, 256:512], gn_bc[:], ALU.mult)

            def fwd_S3(c):
                rob = robs[c]
                bT = rotB.get()
                for m in range(2):
                    tr(psbf[bT][:, m * 128:(m + 1) * 128], rob[:, m * 128:(m + 1) * 128])
                act(ybuf[:, 2:4, c * 128:(c + 1) * 128], psbf[bT][:, 0:256].rearrange("p (m t) -> p m t", m=2), AF.Copy)

            for step in range(NT + 2):
                if step < NT:
                    fwd_S1(step)
                if 0 <= step - 1 < NT:
                    fwd_S2(step - 1)
                if 0 <= step - 2 < NT:
                    fwd_S3(step - 2)
            if l == 0:
                dbg("roT", ybuf[:, 2, :])
                ckpt("roT")

            aqT = arena[:, 0:8192].rearrange("p (m t) -> p m t", m=4)
            akT = arena[:, 8192:12288].rearrange("p (g t) -> p g t", g=2)
            esk = small[:, 104:112]
            eskp = small[:, 112:120].rearrange("p (g j) -> p g j", g=2)
            den = small[:, 120:128]
            for kv in range(2):
                S.dma("pool", kcx[0:64, kv, :], kctxT[l, kv * 64:(kv + 1) * 64, :])
                S.dma("pool", kcx[64:128, kv, :], kctxT[l, kv * 64:(kv + 1) * 64, :])
            for g_ in range(2):
                S.dma("pool", vcx[:, :, g_, 0:64], vctx[l][:, g_ * 64:(g_ + 1) * 64].rearrange("(c p) d -> p c d", p=128))
            S.dma("sp", esk, attn_sink[l].partition_broadcast(128))
            act(esk, esk, AF.Exp)
            cp("dve", eskp.rearrange("p g (q c) -> p g q c", q=2),
               esk.rearrange("p (g c q) -> p g q c", g=2, c=2))

            for half in range(2):
                wv = wview(ws_next(), 8, 256)
                for sub in range(2):
                    for tg in range(4):
                        bnk = rotA.get()
                        proj_ws(wv, sub, tg, bnk)
                        rope_from_psum(bnk, aqT[:, half * 2 + sub, tg * 512:(tg + 1) * 512], tg)
            wv = wview(ws_next(), 8, 256)
            for kv in range(2):
                for tg in range(4):
                    bnk = rotA.get()
                    proj_ws(wv, kv, tg, bnk)
                    rope_from_psum(bnk, akT[:, kv, tg * 512:(tg + 1) * 512], tg)

            rope_flush()
            tprev = btab[:, BT_TPREV:BT_TPREV + 128]
            tnext = btab[:, BT_TNEXT:BT_TNEXT + 128]
            LOOK = 2
            apairs = Ring([2, 4, 6])
            pend = []

            def att_front(b, g, ci, kind, idx, bias, tri):
                bnk = apairs.get()
                for par in range(2):
                    pr = slice(par * 64, (par + 1) * 64)
                    if kind == "loc":
                        kk = akT[pr, g, idx * 128:(idx + 1) * 128]
                    else:
                        kk = kcx[pr, g, idx * 128:(idx + 1) * 128]
                    if tri is None:
                        mm(ps[bnk + par][:, 0:256], kk, aqT[pr, 2 * g:2 * g + 2, b * 128:(b + 1) * 128],
                           signal=(par == 1))
                    else:
                        mm(ps[bnk + par][:, 0:256], kk, aqT[pr, 2 * g:2 * g + 2, b * 128:(b + 1) * 128],
                           start=True, stop=False, signal=False, check_w=True)
                        mm(ps[bnk + par][:, 0:256], ident, tri.unsqueeze(1).to_broadcast([128, 2, 128]),
                           start=False, stop=True, signal=(par == 1), check_w=False)
                pt = btr.get()
                act(pt[:].rearrange("p (r x) -> p r x", r=2),
                    PS[:, bnk * 512:(bnk + 2) * 512].rearrange("p (r x) -> p r x", r=2)[:, :, 0:256],
                    AF.Exp, scale=0.125, bias=bias)
                return pt

            def att_back(b, g, ci, kind, idx, pt):
                ob = g
                vv = Vaug[:, idx, g, :] if kind == "loc" else vcx[:, idx, g, :]
                mm(ps[ob][:, :], vv, pt[:], start=(ci == 0), stop=(ci == 6))
                if ci < 6:
                    return
                rec = ftr.get()
                for par in range(2):
                    tt("dve", rec[par * 64:(par + 1) * 64, 0:256].rearrange("p (c q) -> p c q", c=2),
                       ps[ob][64:128, par * 256:(par + 1) * 256].rearrange("p (c q) -> p c q", c=2),
                       eskp[64:128, g, par * 2:par * 2 + 2].unsqueeze(2).to_broadcast([64, 2, 128]), ALU.add)
                S.op("dve", lambda e, rec=rec: e.reciprocal(rec[:, 0:256], rec[:, 0:256]),
                     reads=[rec[:, 0:256]], writes=[rec[:, 0:256]])
                for par in range(2):
                    tt("dve", ybuf[par * 64:(par + 1) * 64, 4 + 2 * g:6 + 2 * g, b * 128:(b + 1) * 128],
                       ps[ob][0:64, par * 256:(par + 1) * 256].rearrange("p (c q) -> p c q", c=2),
                       rec[par * 64:(par + 1) * 64, 0:256].rearrange("p (c q) -> p c q", c=2), ALU.mult)

            for b in range(NT):
                for g in range(2):
                    chunks = [("loc", max(b - 1, 0), ctab[:, CT_BPREV + b:CT_BPREV + b + 1], tprev),
                              ("loc", b, ctab[:, CT_ZERO:CT_ZERO + 1], None),
                              ("loc", min(b + 1, NT - 1), ctab[:, CT_BNEXT + b:CT_BNEXT + b + 1], tnext)]
                    for cc in range(4):
                        chunks.append(("ctx", cc, ctab[:, CT_BCTX:CT_BCTX + 1], None))
                    for ci, (kind, idx, bias, tri) in enumerate(chunks):
                        pt = att_front(b, g, ci, kind, idx, bias, tri)
                        pend.append((b, g, ci, kind, idx, pt))
                        if len(pend) > LOOK:
                            att_back(*pend.pop(0))
            while pend:
                att_back(*pend.pop(0))
            if l == 0:
                dbg("aoT", ybuf[:, 4, :])
                ckpt("aoT")

            if l + 1 < DEPTH:
                compute_mod(l + 1)
            for gi in range(4):
                wv = wview(ws_next(), 8, 256)
                for sub in range(2):
                    for tg in range(4):
                        bnk = rotAll.get()
                        proj_ws(wv, sub, tg, bnk)
                        sg = btr.get()
                        act(sg[:], ps[bnk][:, :], AF.Silu)
                        yv = ybuf[:, gi * 2 + sub, tg * 512:(tg + 1) * 512]
                        tt("pool", yv, yv, sg[:], ALU.mult)
            if l == 0:
                dbg("ya", ybuf[:, 0, :])
                ckpt("ya")

            nxt = l + 1 < DEPTH
            mergedH = arena[:, 0:8192].rearrange("p (k t) -> p k t", k=8)
            wpa_sb = arena[:, 8192:10240].rearrange("p (k c) -> p k c", k=2)
            wpb_sb = arena[:, 10240:12288].rearrange("p (k c) -> p k c", k=2)
            wpc_sb = arena[:, 12288:16384].rearrange("p (k c) -> p k c", k=4)
            wout_sb = arena[:, 16384:24576].rearrange("p (k c) -> p k c", k=8)
            S.dma("pool", wpa_sb, w_pa[l].rearrange("(k p) c -> p k c", p=128))
            S.dma("pool", wpb_sb, w_pb[l].rearrange("(k p) c -> p k c", p=128))
            S.dma("pool", wpc_sb, w_pc[l].rearrange("(k p) c -> p k c", p=128))
            S.dma("pool", wout_sb, w_out[l].rearrange("(k p) c -> p k c", p=128))
            wps = (wpa_sb, wpb_sb, wpc_sb)
            ybase = (0, 2, 4)
            ykc = (2, 2, 4)

            def merge_unit(wgv, dp, br, sub, tg):
                bg = rotAll.get()
                proj_ws(wgv, sub, tg, bg)
                sg = btr.get()
                act(sg[:], ps[bg][:, :], AF.Sigmoid)
                bp = rotAll.get()
                dcol = dp * 256 + sub * 128
                for kc in range(ykc[br]):
                    mm(ps[bp][:, :], wps[br][:, kc, dcol:dcol + 128],
                       ybuf[:, ybase[br] + kc, tg * 512:(tg + 1) * 512],
                       start=(kc == 0), stop=(kc == ykc[br] - 1))
                dst = mergedH[:, dp * 2 + sub, (tg % 2) * 512:(tg % 2 + 1) * 512]
                if br == 0:
                    tt("dve", dst, ps[bp][:, :], sg[:], ALU.mult)
                else:
                    a = ftr.get()
                    tt("dve", a[:], ps[bp][:, :], sg[:], ALU.mult)
                    tt("pool", dst, dst, a[:], ALU.add)

            def obuf_ap(t, n):
                half, tl = t // 8, t % 8
                return ybuf[:, tl, half * 1024 + n * 512:half * 1024 + (n + 1) * 512]

            def passA_tile(t):
                tl = t % 8
                bks = [(t % 4) * 2, (t % 4) * 2 + 1]
                for n in range(2):
                    for k in range(8):
                        mm(ps[bks[n]][:, :], mergedH[:, k, tl * 128:(tl + 1) * 128], wout_sb[:, k, n * 512:(n + 1) * 512],
                           start=(k == 0), stop=(k == 7))
                    act(junk[:, 0:512], ps[bks[n]][:, :], AF.Square, accum_out=stat[:, n, t:t + 1])
                    cp("dve", obuf_ap(t, n), ps[bks[n]][:, :])

            def statsA(c0, c1):
                tt("dve", stat[:, 2, c0:c1], stat[:, 0, c0:c1], stat[:, 1, c0:c1], ALU.add)
                ts("dve", stat[:, 2, c0:c1], stat[:, 2, c0:c1], 1.0 / D, EPS, ALU.mult, ALU.add)
                act(stat[:, 3, c0:c1], stat[:, 2, c0:c1], AF.Ln)
                act(stat[:, 3, c0:c1], stat[:, 3, c0:c1], AF.Exp, scale=-0.5)

            xtB = {}

            def passB_load(t):
                xtB[t] = xst.get()
                S.dma("sp", xtB[t][:], xsrc[t * 128:(t + 1) * 128, :])

            def passB_tile(t, tmp_ring):
                xt = xtB[t]
                for n in range(2):
                    f = tmp_ring.get()
                    stt("dve", f[:], obuf_ap(t, n), stat[:, 3, t:t + 1],
                        gg_bc[l][:, n * 512:(n + 1) * 512], ALU.mult, ALU.mult)
                    tt("pool" if (t + n) % 2 == 0 else "dve", xt[:, n * 512:(n + 1) * 512],
                       xt[:, n * 512:(n + 1) * 512], f[:], ALU.add)
                S.dma("sp", xdst[t * 128:(t + 1) * 128, :], xt[:, :])
                if nxt:
                    act(junk[:], xt[:, :], AF.Square, accum_out=stat[:, 4, t:t + 1])

            def tail_steps(half, tmp_ring):
                c0, c1 = half * 8, half * 8 + 8
                steps = []
                for t in range(c0, c1):
                    def stepB(t=t):
                        if t == c0 and t not in xtB:
                            passB_load(t)
                        if t + 1 < c1 and (t + 1) not in xtB:
                            passB_load(t + 1)
                        passB_tile(t, tmp_ring)
                    steps.append(stepB)
                if nxt:
                    steps.append(lambda: nrm_batch_stats(c0, c1))
                    steps += norm_pass_C_steps(xdst, l + 1, range(c0, c1), tmp_ring,
                                               resident=(xtB if half == 1 else None))
                return steps

            deferred = []
            for half in range(2):
                for dp in range(4):
                    for br in range(3):
                        wgv = wview(ws_next(), 8, 256)
                        for sub in range(2):
                            for tg in (2 * half, 2 * half + 1):
                                merge_unit(wgv, dp, br, sub, tg)
                                if deferred:
                                    deferred.pop(0)()
                while deferred:
                    deferred.pop(0)()
                if l == 0 and half == 0:
                    dbg("merged", mergedH[:, 0, :])
                    ckpt("merged")
                if half == 1 and nxt:
                    for t in range(8, 12):
                        xtB[t] = arena[:, 8192 + (t - 8) * 2048:8192 + (t - 7) * 2048].bitcast(F32)
                        S.dma("sp", xtB[t], xsrc[t * 128:(t + 1) * 128, :])
                if half == 1 and not nxt:
                    for t in range(8, NT):
                        xtB[t] = hT[:, t - 8, :].bitcast(F32)
                        S.dma("sp", xtB[t], xsrc[t * 128:(t + 1) * 128, :])
                for t in range(half * 8, half * 8 + 8):
                    passA_tile(t)
                statsA(half * 8, half * 8 + 8)
                if half == 1 and nxt:
                    for t in range(12, NT):
                        xtB[t] = arena[:, (t - 12) * 2048:(t - 11) * 2048].bitcast(F32)
                        S.dma("sp", xtB[t], xsrc[t * 128:(t + 1) * 128, :])
                deferred = tail_steps(half, dfr if half == 0 else ftr)
            while deferred:
                deferred.pop(0)()
            ckpt(f"layer{l}")

    except _Stop:
        pass
    S.finish()
    S.run()
    return nc, dbg_outs, S


def _bf(a):
    return np.ascontiguousarray(a.astype(ml_dtypes.bfloat16))


def _const_tables(is_latent):
    C = 128
    seqlen = T if is_latent else 256
    n = np.arange(seqlen)
    ang = 2.0 * np.pi * ((n[:, None] * n[None, :]) % seqlen) / seqlen
    cb = (np.cos(ang) / np.sqrt(seqlen)).astype(np.float32)
    sbk = (-np.sin(ang) / np.sqrt(seqlen)).astype(np.float32)
    dc = np.zeros((T, T), np.float32)
    ds = np.zeros((T, T), np.float32)
    for s in range(T // seqlen):
        dc[s * seqlen:(s + 1) * seqlen, s * seqlen:(s + 1) * seqlen] = cb
        ds[s * seqlen:(s + 1) * seqlen, s * seqlen:(s + 1) * seqlen] = sbk
    m = np.arange(64)
    a64 = 2.0 * np.pi * ((m[:, None] * m[None, :]) % 64) / 64
    c64 = np.cos(a64) / 8.0
    s64 = np.sin(a64) / 8.0
    cd = np.zeros((256, 512), np.float32)
    for g in range(4):
        cd[g * 64:(g + 1) * 64, g * 64:(g + 1) * 64] = c64
        cd[g * 64:(g + 1) * 64, 256 + g * 64:256 + (g + 1) * 64] = s64
    rc = np.ones((128, T), np.float32)
    rs = np.zeros((128, T), np.float32)
    if is_latent:
        pos_row = (np.arange(T) // 64).astype(np.float32)
        pos_col = (np.arange(T) % 64).astype(np.float32)
        inv = (10000.0 ** (-np.arange(16, dtype=np.float32) / 16)).astype(np.float32)
        for p in range(128):
            d = p % 64
            half, idx = d // 32, d % 32
            pos = pos_row if half == 0 else pos_col
            a = pos * inv[idx % 16]
            rc[p] = np.cos(a)
            rs[p] = -np.sin(a) if idx < 16 else np.sin(a)
    perm = np.zeros((128, 128), np.float32)
    for mcol in range(128):
        idx = mcol % 32
        partner = mcol + 16 if idx < 16 else mcol - 16
        perm[partner, mcol] = 1.0
    j = np.arange(C)[:, None].astype(np.float32)
    i = np.arange(C)[None, :].astype(np.float32)
    ct = np.zeros((128, CT_W), np.float32)
    ct[:, CT_DPOS:CT_DPOS + 128] = np.maximum(i - j, 0)
    ct[:, CT_DNEG:CT_DNEG + 128] = np.maximum(j - i, 0)
    ct[:, CT_MGE:CT_MGE + 128] = (i >= j)
    ct[:, CT_MLE:CT_MLE + 128] = (i <= j)
    ct[:, CT_IP1:CT_IP1 + 128] = i + 1
    ct[:, CT_CMI:CT_CMI + 128] = C - i
    ct[:, CT_CM1MP] = C - 1 - np.arange(C)
    ct[:, CT_P] = np.arange(C)
    for b in range(NT):
        if is_latent:
            ct[:, CT_BPREV + b] = NEG if b == 0 else 0.0
            ct[:, CT_BNEXT + b] = NEG if b == NT - 1 else 0.0
            ct[:, CT_KEEPF + b] = 1.0
            ct[:, CT_KEEPB + b] = 1.0
        else:
            ct[:, CT_BPREV + b] = 0.0 if b % 2 == 1 else NEG
            ct[:, CT_BNEXT + b] = 0.0 if b % 2 == 0 else NEG
            ct[:, CT_KEEPF + b] = 0.0 if b % 2 == 0 else 1.0
            ct[:, CT_KEEPB + b] = 0.0 if b % 2 == 1 else 1.0
    ct[:, CT_BCTX] = 0.0 if is_latent else NEG
    bt = np.zeros((128, BT_W), np.float32)
    bt[:, BT_ID:BT_ID + 128] = np.eye(128)
    bt[:, BT_PERM:BT_PERM + 128] = perm
    if is_latent:
        bt[:, BT_TPREV:BT_TPREV + 128] = np.where(i <= j, 0.0, 8.0 * NEG)
        bt[:, BT_TNEXT:BT_TNEXT + 128] = np.where(j <= i, 0.0, 8.0 * NEG)
    for m_ in range(2):
        for p_ in range(128):
            lo = 0 if p_ < 64 else 64
            bt[p_, BT_BMASK + m_ * 128 + lo:BT_BMASK + m_ * 128 + lo + 64] = 1.0
    dc = dc.reshape(NT, 128, 4, 512).transpose(2, 1, 0, 3).reshape(4, 128, 8192)
    ds = ds.reshape(NT, 128, 4, 512).transpose(2, 1, 0, 3).reshape(4, 128, 8192)
    return {"dftc": _bf(dc), "dfts": _bf(ds), "cdft": _bf(cd), "ropec": _bf(rc), "ropes": _bf(rs),
            "ctab": ct, "btab": _bf(bt)}


_CACHE = {}


def kernel(x_prompt, x_sample, cache_k, cache_v, state_ret, c, c_ctx, w_mod, b_mod, g_pre, g_post, w_in,
           w_four, ret_decay, ret_gn, attn_sink, w_branch_a, w_branch_b, w_branch_c, w_out):
    f = lambda a: np.ascontiguousarray(np.asarray(a, dtype=np.float32))
    x_prompt, x_sample, cache_k, cache_v, state_ret, c, c_ctx = map(f, (x_prompt, x_sample, cache_k, cache_v,
                                                                       state_ret, c, c_ctx))
    shared = {
        "w_mod": f(w_mod), "b_mod": f(b_mod), "g_pre": f(g_pre), "g_post": f(g_post), "w_in": f(w_in),
        "w_four": f(w_four), "ret_decay": f(ret_decay).reshape(DEPTH, 8), "ret_gn": f(ret_gn),
        "attn_sink": f(attn_sink).reshape(DEPTH, 8), "w_pa": f(w_branch_a), "w_pb": f(w_branch_b),
        "w_pc": f(w_branch_c), "w_out": f(w_out),
    }
    if "nc" not in _CACHE:
        _CACHE["nc"] = build_nc(DEBUG)
        _CACHE["tabs"] = (_const_tables(False), _const_tables(True))
    nc, dbg_outs, _ = _CACHE["nc"]
    tabs_p, tabs_l = _CACHE["tabs"]
    in_maps = []
    for core in range(8):
        m = dict(shared)
        if core < 4:
            m.update(tabs_p)
            m["x"] = x_prompt[core * 8:(core + 1) * 8].reshape(T, D)
            m["cvec"] = c_ctx
            m["kctxT"] = np.zeros((DEPTH, 128, 512), np.float32)
            m["vctx"] = np.zeros((DEPTH, 512, 128), np.float32)
            m["s0"] = np.zeros((DEPTH, 2, 128, 2, 64), np.float32)
        else:
            b = core - 4
            m.update(tabs_l)
            m["x"] = x_sample[b]
            m["cvec"] = c[b]
            m["kctxT"] = np.ascontiguousarray(cache_k[b].transpose(0, 2, 3, 1).reshape(DEPTH, 128, 512))
            m["vctx"] = np.ascontiguousarray(cache_v[b].reshape(DEPTH, 512, 128))
            s = state_ret[b].reshape(DEPTH, 2, 2, 2, 64, 64)
            m["s0"] = np.ascontiguousarray(s.transpose(0, 1, 3, 4, 2, 5).reshape(DEPTH, 2, 128, 2, 64))
        in_maps.append(m)
    res = run_bass_kernel_spmd(nc, in_maps, core_ids=list(range(8)))
    R = res.results
    _CACHE["last"] = R
    y_prompt = np.concatenate([R[i]["y"].reshape(8, 256, D) for i in range(4)], axis=0)
    y_sample = np.stack([R[4 + i]["y"] for i in range(4)], axis=0)
    ck = np.concatenate([R[i]["ck"].reshape(DEPTH, 8, 256, 2, 64).transpose(1, 0, 2, 3, 4) for i in range(4)], axis=0)
    cv = np.concatenate([R[i]["cv"].reshape(DEPTH, 8, 256, 2, 64).transpose(1, 0, 2, 3, 4) for i in range(4)], axis=0)
    sts = []
    for i in range(4):
        s = R[i]["st"].reshape(DEPTH, 2, 8, 2, 64, 2, 64)
        s = s.transpose(2, 0, 1, 5, 3, 4, 6).reshape(8, DEPTH, 2, 4, 64, 64)
        sts.append(s)
    st = np.concatenate(sts, axis=0)
    return (y_prompt.astype(np.float32), y_sample.astype(np.float32), np.ascontiguousarray(ck, dtype=np.float32),
            np.ascontiguousarray(cv, dtype=np.float32), np.ascontiguousarray(st, dtype=np.float32))
```

```python
import numpy as np
import ml_dtypes
import concourse.bass as bass
import concourse.mybir as mybir
from concourse.bass_utils import run_bass_kernel_spmd

F32 = mybir.dt.float32
BF = mybir.dt.bfloat16
AF = mybir.ActivationFunctionType
ALU = mybir.AluOpType
AX = mybir.AxisListType

D = 1024
T = 2048
NT = 16
DEPTH = 2
EPS = 1e-6
NEG = -30000.0
DEBUG = False
STOP = None


class _Stop(Exception):
    pass

C_FX, C_FZ, C_RQ, C_RK, C_RV, C_RZ, C_AQ, C_AK, C_AV, C_AZ, C_GA, C_GB, C_GC = (
    0, 256, 512, 768, 1024, 1280, 1536, 2048, 2176, 2304, 2816, 3840, 4864)

CT_DPOS, CT_DNEG, CT_MGE, CT_MLE, CT_IP1, CT_CMI = 0, 128, 256, 384, 512, 640
CT_CM1MP, CT_P, CT_BPREV, CT_BNEXT, CT_BCTX, CT_ZERO, CT_KEEPF, CT_KEEPB, CT_W = 768, 769, 770, 786, 802, 803, 804, 820, 836
BT_ID, BT_PERM, BT_TPREV, BT_TNEXT, BT_BMASK, BT_W = 0, 128, 256, 384, 512, 768

BIG = 1 << 40


class Sched:
    ENG = ("pe", "act", "dve", "pool", "sp")

    def __init__(self, nc, n_dma_sp=24, n_dma_pool=16):
        self.nc = nc
        self.q = {e: [] for e in self.ENG}
        self.sem = {e: nc.alloc_semaphore(f"s_{e}") for e in ("pe", "act", "dve", "pool")}
        self.cnt = {e: 0 for e in ("pe", "act", "dve", "pool")}
        self.dpool = {
            "sp": [nc.alloc_semaphore(f"dsp{i}") for i in range(n_dma_sp)],
            "pool": [nc.alloc_semaphore(f"dpl{i}") for i in range(n_dma_pool)],
        }
        self.duse = {"sp": [0] * n_dma_sp, "pool": [0] * n_dma_pool}
        self.drr = {"sp": 0, "pool": 0}
        self.clock = {e: {} for e in self.ENG}
        self.tokclock = {}
        self.acc = {}
        self.ninst = 0

    @staticmethod
    def _intervals(off, dims):
        dims = [(abs(s), c) for s, c in dims if c > 1 and s != 0]
        dims.sort()
        ivs = [(off, off + 1)]
        for s, c in dims:
            if len(ivs) == 1 and s <= ivs[0][1] - ivs[0][0]:
                lo, hi = ivs[0]
                ivs = [(lo, hi + (c - 1) * s)]
            elif len(ivs) * c <= 64:
                ivs = [(lo + k * s, hi + k * s) for k in range(c) for lo, hi in ivs]
            else:
                lo = min(i[0] for i in ivs)
                hi = max(i[1] for i in ivs)
                ivs = [(lo, hi + (c - 1) * s)]
        return tuple(sorted(ivs))

    @classmethod
    def region(cls, ap):
        t = ap.tensor
        name = t.name
        aps = ap.ap
        off = int(ap.offset)
        tn = type(t).__name__
        esz = {BF: 2, F32: 4}.get(ap.dtype, 4)
        if tn.startswith("DRam"):
            return (name, 0, 1, cls._intervals(off * esz, [(s_ * esz, c_) for s_, c_ in aps] + [(1, esz)]))
        if tn.startswith("PSum"):
            ivs = cls._intervals(off, aps[1:]) if aps[0][0] else cls._intervals(off, aps)
            pstep = aps[0][0]
            banks = set()
            for lo, hi in ivs:
                if pstep:
                    lo, hi = lo % pstep, (hi - 1) % pstep + 1
                b0, b1 = (lo * esz) // 2048, ((hi * esz) - 1) // 2048
                banks.update(range(b0, b1 + 1))
            return ("PSUM", 0, 128, tuple((b * 2048, (b + 1) * 2048) for b in sorted(banks)))
        pstep, npart = aps[0]
        if pstep == 0:
            p0, f0 = 0, off
        else:
            p0, f0 = off // pstep, off % pstep
        return (name, p0, p0 + npart, cls._intervals(f0 * esz, [(s_ * esz, c_) for s_, c_ in aps[1:]] + [(1, esz)]))

    @staticmethod
    def _ov(a, b):
        if not (a[1] < b[2] and b[1] < a[2]):
            return False
        for lo, hi in a[3]:
            for lo2, hi2 in b[3]:
                if lo < hi2 and lo2 < hi:
                    return True
        return False

    @staticmethod
    def _contains(outer, inner):
        if not (outer[1] <= inner[1] and inner[2] <= outer[2]):
            return False
        for lo, hi in inner[3]:
            ok = False
            for lo2, hi2 in outer[3]:
                if lo2 <= lo and hi <= hi2:
                    ok = True
                    break
            if not ok:
                return False
        return True

    @staticmethod
    def _need(need, key, val):
        if need.get(key, 0) < val:
            need[key] = val

    def _collect(self, rregs, wregs):
        need = {}
        for r in rregs:
            ispsum = r[0] == "PSUM"
            for rec in self.acc.get(r[0], ()):
                if (rec[1] or ispsum) and self._ov(r, rec[0]):
                    self._need(need, rec[2], rec[3])
        for w in wregs:
            for rec in self.acc.get(w[0], ()):
                if self._ov(w, rec[0]):
                    self._need(need, rec[2], rec[3])
        return need

    def _record(self, rregs, wregs, key, val):
        for w in wregs:
            lst = self.acc.setdefault(w[0], [])
            lst[:] = [rec for rec in lst if not self._contains(w, rec[0])]
            lst.append((w, True, key, val))
        for r in rregs:
            lst = self.acc.setdefault(r[0], [])
            if r[0] == "PSUM":
                lst[:] = [rec for rec in lst if not self._contains(r, rec[0])]
                lst.append((r, True, key, val))
                continue
            if not isinstance(key, tuple):
                lst[:] = [rec for rec in lst if not (rec[2] == key and not rec[1] and rec[0] == r)]
            lst.append((r, False, key, val))

    def _semof(self, key):
        if isinstance(key, tuple):
            return self.dpool[key[0]][key[1]]
        return self.sem[key]

    def _waits(self, eng, need):
        ck = self.clock[eng]
        out = []
        for key, val in need.items():
            if key == "pe" and eng == "pe":
                continue
            if ck.get(key, 0) >= val:
                continue
            out.append((key, val))
        for key, val in out:
            tc = self.tokclock.get((key, val))
            if tc:
                for k2, v2 in tc.items():
                    if ck.get(k2, 0) < v2:
                        ck[k2] = v2
            if ck.get(key, 0) < val:
                ck[key] = val
        return [(self._semof(k), v) for k, v in out]

    def op(self, eng, fn, reads=(), writes=(), signal=True, check_w=True):
        rregs = [self.region(a) for a in reads]
        wregs = [self.region(a) for a in writes]
        need = self._collect(rregs, wregs if check_w else ())
        waits = self._waits(eng, need)
        val = self.cnt[eng] + 1
        sem = self.sem[eng]
        if signal:
            self.cnt[eng] = val
            self.tokclock[(eng, val)] = dict(self.clock[eng])

        def emit(e, fn=fn, waits=waits, signal=signal, sem=sem):
            for s, v in waits:
                e.wait_ge(s, v)
            ins = fn(e)
            if signal:
                ins.then_inc(sem, 1)

        self.q[eng].append(emit)
        self._record(rregs, wregs, eng, val)
        self.ninst += 1

    def dma(self, eng, out, in_, **kw):
        i = self.drr[eng]
        self.drr[eng] = (i + 1) % len(self.dpool[eng])
        use = self.duse[eng][i]
        key = (eng, i)
        rregs = [self.region(in_)]
        wregs = [self.region(out)]
        need = self._collect(rregs, wregs)
        if use > 0:
            self._need(need, key, 16 * use)
        waits = self._waits(eng, need)
        self.duse[eng][i] = use + 1
        val = 16 * (use + 1)
        sem = self.dpool[eng][i]
        self.tokclock[(key, val)] = dict(self.clock[eng])

        def emit(e, waits=waits, sem=sem, out=out, in_=in_, kw=kw):
            for s, v in waits:
                e.wait_ge(s, v)
            e.dma_start(out=out, in_=in_, **kw).then_inc(sem, 16)

        self.q[eng].append(emit)
        self._record(rregs, wregs, key, val)
        self.ninst += 1

    def finish(self):
        need = {}
        for eng in ("sp", "pool"):
            for i, use in enumerate(self.duse[eng]):
                if use:
                    need[(eng, i)] = 16 * use
        for e in ("pe", "act", "dve", "pool"):
            if self.cnt[e]:
                need[e] = self.cnt[e]
        waits = [(self._semof(k), v) for k, v in need.items()]

        def emit(e, waits=waits):
            for s, v in waits:
                e.wait_ge(s, v)

        self.q["sp"].append(emit)

    def run(self):
        nc = self.nc
        q = self.q
        with nc.Block() as block:

            @block.tensor
            def _(e):
                for f in q["pe"]:
                    f(e)

            @block.scalar
            def _(e):
                for f in q["act"]:
                    f(e)

            @block.vector
            def _(e):
                for f in q["dve"]:
                    f(e)

            @block.gpsimd
            def _(e):
                for f in q["pool"]:
                    f(e)

            @block.sync
            def _(e):
                for f in q["sp"]:
                    f(e)


class Ring:
    def __init__(self, items):
        self.items = list(items)
        self.i = 0

    def get(self):
        x = self.items[self.i]
        self.i = (self.i + 1) % len(self.items)
        return x


def build_nc(debug=False):
    nc = bass.Bass("TRN2", target_bir_lowering=False)
    S = Sched(nc)
    dbg_outs = []

    def din(name, shape, dt=F32):
        return nc.dram_tensor(name, list(shape), dt, kind="ExternalInput").ap()

    def dout(name, shape, dt=F32):
        return nc.dram_tensor(name, list(shape), dt, kind="ExternalOutput").ap()

    x_in = din("x", [T, D])
    cvec = din("cvec", [D])
    kctxT = din("kctxT", [DEPTH, 128, 512])
    vctx = din("vctx", [DEPTH, 512, 128])
    s0 = din("s0", [DEPTH, 2, 128, 2, 64])
    w_mod = din("w_mod", [DEPTH, D, 3 * D])
    b_mod = din("b_mod", [DEPTH, 3 * D])
    g_pre = din("g_pre", [DEPTH, D])
    g_post = din("g_post", [DEPTH, D])
    w_in = din("w_in", [DEPTH, D, 5888])
    w_four = din("w_four", [DEPTH, 256, 256])
    ret_decay = din("ret_decay", [DEPTH, 8])
    ret_gn = din("ret_gn", [DEPTH, 256])
    attn_sink = din("attn_sink", [DEPTH, 8])
    w_pa = din("w_pa", [DEPTH, 256, D])
    w_pb = din("w_pb", [DEPTH, 256, D])
    w_pc = din("w_pc", [DEPTH, 512, D])
    w_out = din("w_out", [DEPTH, D, D])
    dftc = din("dftc", [4, 128, 8192], BF)
    dfts = din("dfts", [4, 128, 8192], BF)
    cdft_d = din("cdft", [256, 512], BF)
    ropec_d = din("ropec", [128, T], BF)
    ropes_d = din("ropes", [128, T], BF)
    ctab_d = din("ctab", [128, CT_W])
    btab_d = din("btab", [128, BT_W], BF)

    y_out = dout("y", [T, D])
    ck_out = dout("ck", [DEPTH, T, 128])
    cv_out = dout("cv", [DEPTH, T, 128])
    st_out = dout("st", [DEPTH, 2, 8, 128, 2, 64])
    x1s = nc.dram_tensor("x1s", [T, D], F32, kind="Internal").ap()

    sb = nc.alloc_sbuf_tensor
    hT = sb("hT", [128, 8, T], BF)
    ybuf = sb("ybuf", [128, 8, T], BF)
    ARN = 24576
    arena = sb("arena", [128, ARN], BF)
    NWS = 4
    wsb = [sb(f"ws{i}", [128, 2048], BF) for i in range(NWS)]
    gv_bc = [sb(f"gvbc{l}", [128, D], F32) for l in range(DEPTH)]
    sh_bc = [sb(f"shbc{l}", [128, D], F32) for l in range(DEPTH)]
    gg_bc = [sb(f"ggbc{l}", [128, D], F32) for l in range(DEPTH)]
    ropec = sb("ropec_sb", [128, T], BF)
    ropes = sb("ropes_sb", [128, T], BF)
    ctab = sb("ctab_sb", [128, CT_W], F32)
    btab = sb("btab_sb", [128, BT_W], BF)
    DT = sb("DT", [128, 4, 128], BF)
    gn_bc = sb("gn_bc", [128, 256], F32)
    xst = Ring([sb(f"xst{i}", [128, D], F32) for i in range(3)])
    ftr = Ring([sb(f"ft{i}", [128, 512], F32) for i in range(4)])
    _bt = [sb(f"bt{i}", [128, 512], BF) for i in range(6)]
    btr = Ring(_bt[0:4])
    hbr = Ring([sb(f"hb{i}", [128, D], BF) for i in range(2)])
    robr = Ring([sb(f"rob{i}", [128, 256], BF) for i in range(2)])
    dfr = Ring([sb(f"df{i}", [128, 512], BF) for i in range(2)])
    kt_a = _bt[4]
    kt_b = _bt[5]
    junk = sb("junk", [128, D], BF)
    small = sb("small", [128, 256], F32)
    stat = sb("stat", [128, 8, NT], F32)
    sc_sb = sb("sc_sb", [128, 8], BF)
    Sring = Ring([sb(f"S{i}", [128, 2, 128], F32) for i in range(4)])

    ident = btab[:, BT_ID:BT_ID + 128]
    perm = btab[:, BT_PERM:BT_PERM + 128]

    PS = nc.alloc_psum_tensor("PS", [128, 4096], F32)
    ps = [PS[:, i * 512:(i + 1) * 512] for i in range(8)]
    psbf = [p.bitcast(BF) for p in ps]
    bmask3 = btab[:, BT_BMASK:BT_BMASK + 256].rearrange("p (m c) -> p m c", m=2)
    rotA = Ring([2, 3, 4, 5, 6, 7])
    rotAll = Ring(list(range(8)))
    pairs = Ring([4, 6])
    rotB = Ring([4, 5, 6, 7])

    def mm(out, lhsT, rhs, start=True, stop=True, signal=None, check_w=None, skip=False):
        kw = {"skip_group_check": True} if skip else {}
        S.op("pe", lambda e: e.matmul(out, lhsT, rhs, start=start, stop=stop, **kw),
             reads=[lhsT, rhs], writes=[out],
             signal=stop if signal is None else signal,
             check_w=start if check_w is None else check_w)

    def tr(out, in_):
        S.op("pe", lambda e: e.transpose(out, in_, ident), reads=[in_, ident], writes=[out])

    def act(out, in_, func, reads=None, **kw):
        extra = [v for v in (kw.get("scale"), kw.get("bias"), kw.get("accum_out")) if hasattr(v, "tensor")]
        wr = [out] + ([kw["accum_out"]] if kw.get("accum_out") is not None else [])
        rd = [in_] + [v for v in (kw.get("scale"), kw.get("bias")) if hasattr(v, "tensor")]
        S.op("act", lambda e: e.activation(out, in_, func, **kw), reads=rd, writes=wr)

    def tt(eng, out, in0, in1, op):
        S.op(eng, lambda e: e.tensor_tensor(out, in0, in1, op), reads=[in0, in1], writes=[out])

    def ts(eng, out, in0, s1, s2, op0, op1=None):
        rd = [in0] + [v for v in (s1, s2) if hasattr(v, "tensor")]
        if op1 is None:
            S.op(eng, lambda e: e.tensor_scalar(out, in0, s1, None, op0), reads=rd, writes=[out])
        else:
            S.op(eng, lambda e: e.tensor_scalar(out, in0, s1, s2, op0, op1), reads=rd, writes=[out])

    def stt(eng, out, in0, scalar, in1, op0, op1):
        rd = [in0, in1] + ([scalar] if hasattr(scalar, "tensor") else [])
        S.op(eng, lambda e: e.scalar_tensor_tensor(out, in0, scalar, in1, op0, op1), reads=rd, writes=[out])

    def cp(eng, out, in_):
        S.op(eng, lambda e: e.tensor_copy(out, in_), reads=[in_], writes=[out])

    def ckpt(name):
        if STOP == name:
            raise _Stop()

    def dbg(name, ap):
        if not debug:
            return
        shp = list(ap.shape)
        d = nc.dram_tensor("dbg_" + name, shp, ap.dtype, kind="ExternalOutput").ap()
        S.dma("sp", d, ap)
        dbg_outs.append("dbg_" + name)

    jobs = []

    class WS:
        issued = 0
        used = 0

    def ws_issue_upto(n):
        while WS.issued < min(n, len(jobs)):
            j = WS.issued
            buf = wsb[j % NWS]
            for dst_fn, src in jobs[j]:
                S.dma("pool", dst_fn(buf), src)
            WS.issued += 1

    def ws_next(hold=0):
        j = WS.used
        ws_issue_upto(j - hold + NWS)
        WS.used += 1
        return wsb[j % NWS]

    def job_cols(src2d, c0, ncols, kc=8):
        src = src2d[:, c0:c0 + ncols].rearrange("(k p) c -> p k c", p=128)
        return [(lambda b, kc=kc, ncols=ncols: b[:, 0:kc * ncols].rearrange("p (k c) -> p k c", k=kc), src)]

    def wview(buf, kc, ncols):
        return buf[:, 0:kc * ncols].rearrange("p (k c) -> p k c", k=kc)

    for ng in range(12):
        jobs.append(job_cols(w_mod[0], ng * 256, 256))
    for l in range(DEPTH):
        jobs.append(job_cols(w_four[l], 0, 256, kc=2))
        jobs.append(job_cols(w_in[l], C_FX, 256))
        jobs.append(job_cols(w_in[l], C_RQ, 256))
        jobs.append(job_cols(w_in[l], C_RK, 256))
        jobs.append(job_cols(w_in[l], C_RV, 256))
        jobs.append(job_cols(w_in[l], C_AK, 256))
        jobs.append(job_cols(w_in[l], C_AQ, 256))
        jobs.append(job_cols(w_in[l], C_AQ + 256, 256))
        akj = []
        for g_ in range(2):
            srck = w_in[l][:, C_AK + g_ * 64:C_AK + (g_ + 1) * 64].rearrange("(k p) d -> p k d", p=128)
            for u_ in range(2):
                akj.append((lambda b, g_=g_, u_=u_: b[:, 0:2048].rearrange(
                    "p (k g u d) -> p k g u d", k=8, g=2, u=2)[:, :, g_, u_, :], srck))
        jobs.append(akj)
        if l + 1 < DEPTH:
            for ng in range(12):
                jobs.append(job_cols(w_mod[l + 1], ng * 256, 256))
        jobs.append(job_cols(w_in[l], C_FZ, 256))
        jobs.append(job_cols(w_in[l], C_RZ, 256))
        jobs.append(job_cols(w_in[l], C_AZ, 256))
        jobs.append(job_cols(w_in[l], C_AZ + 256, 256))
        for half in range(2):
            for dp in range(4):
                jobs.append(job_cols(w_in[l], C_GA + dp * 256, 256))
                jobs.append(job_cols(w_in[l], C_GB + dp * 256, 256))
                jobs.append(job_cols(w_in[l], C_GC + dp * 256, 256))

    try:
        S.dma("sp", ctab[:], ctab_d)
        S.dma("sp", btab[:], btab_d)
        S.dma("sp", ropec[:], ropec_d)
        S.dma("sp", ropes[:], ropes_d)

        cv_col = small[:, 0:8]
        sc_col = sc_sb[:, :]
        ones_r = small[0:1, 128:256]
        S.dma("sp", cv_col, cvec.rearrange("(k p) -> p k", p=128), allow_slow_non_contiguous=True)
        act(sc_col, cv_col, AF.Silu)
        S.op("dve", lambda e: e.memset(ones_r, 1.0), writes=[ones_r])

        def compute_mod(l):
            xA, xB = xst.items[0], xst.items[1]
            rowm = xA[0:1, 0:512]
            rowb = xA[0:1, 512:1024]
            rowg = xB[0:1, 0:512]
            rowr = xB[0:1, 512:1024]
            for part in range(3):
                for n in range(2):
                    c0 = part * D + n * 512
                    S.dma("sp", rowb, b_mod[l:l + 1, c0:c0 + 512])
                    bnk = rotA.get()
                    for q2 in range(2):
                        wv = wview(ws_next(), 8, 256)
                        for k in range(8):
                            mm(ps[bnk][0:1, q2 * 256:(q2 + 1) * 256], sc_col[:, k:k + 1], wv[:, k, :],
                               start=(k == 0), stop=(k == 7), signal=(q2 == 1 and k == 7), check_w=(q2 == 0 and k == 0))
                    tt("dve", rowm, ps[bnk][0:1, :], rowb, ALU.add)
                    if part == 0:
                        src_row, dst = rowm, sh_bc[l]
                    elif part == 1:
                        S.dma("sp", rowg, g_pre[l:l + 1, n * 512:(n + 1) * 512])
                        stt("dve", rowr, rowm, 1.0, rowg, ALU.add, ALU.mult)
                        src_row, dst = rowr, gv_bc[l]
                    else:
                        S.dma("sp", rowg, g_post[l:l + 1, n * 512:(n + 1) * 512])
                        tt("dve", rowr, rowm, rowg, ALU.mult)
                        src_row, dst = rowr, gg_bc[l]
                    b2 = rotA.get()
                    mm(ps[b2][:, :], ones_r, src_row)
                    act(dst[:, n * 512:(n + 1) * 512], ps[b2][:, :], AF.Copy)

        xres = [ybuf[:, t, :].bitcast(F32) for t in range(8)] + \
               [arena[:, (t - 8) * 2048:(t - 7) * 2048].bitcast(F32) for t in range(8, NT)]
        for t in range(NT):
            S.dma("sp", xres[t], x_in[t * 128:(t + 1) * 128, :])
            act(junk[:], xres[t], AF.Square, accum_out=stat[:, 4, t:t + 1])
        compute_mod(0)

        def nrm_batch_stats(c0=0, c1=NT):
            ts("dve", stat[:, 5, c0:c1], stat[:, 4, c0:c1], 1.0 / D, EPS, ALU.mult, ALU.add)
            act(stat[:, 6, c0:c1], stat[:, 5, c0:c1], AF.Ln)
            act(stat[:, 7, c0:c1], stat[:, 6, c0:c1], AF.Exp, scale=-0.5)

        def nrm_apply(xt, t, l, tmp_ring=None):
            hb = hbr.get()
            for n in range(2):
                f = (tmp_ring or ftr).get()
                stt("dve", f[:], xt[:, n * 512:(n + 1) * 512], stat[:, 7, t:t + 1], gv_bc[l][:, n * 512:(n + 1) * 512],
                    ALU.mult, ALU.mult)
                tt("pool" if (t + n) % 2 == 0 else "dve", hb[:, n * 512:(n + 1) * 512], f[:],
                   sh_bc[l][:, n * 512:(n + 1) * 512], ALU.add)
            return hb

        def nrm_transpose(hb, t, bnk):
            for k in range(8):
                tr(psbf[bnk][:, k * 128:(k + 1) * 128], hb[:, k * 128:(k + 1) * 128])
            act(hT[:, :, t * 128:(t + 1) * 128], psbf[bnk][:, :].rearrange("p (k c) -> p k c", k=8), AF.Copy)

        ckpt("setup")
        for l in range(DEPTH):
            xsrc = x_in if l == 0 else x1s
            xdst = x1s if l == 0 else y_out

            def norm_pass_C_steps(src_dram, lnorm, tiles, tmp_ring=None, resident=None):
                hbs, xts_ = {}, {}
                steps = []
                seq = list(tiles)
                n_ = len(seq)

                def load(i):
                    if resident is not None and seq[i] in resident:
                        xts_[i] = resident[seq[i]]
                        return
                    xts_[i] = xst.get()
                    S.dma("sp", xts_[i][:], src_dram[seq[i] * 128:(seq[i] + 1) * 128, :])

                def mk(i):
                    def step():
                        if i == 0:
                            load(0)
                        if i + 1 < n_:
                            load(i + 1)
                        if i < n_:
                            hbs[i] = nrm_apply(xts_[i], seq[i], lnorm, tmp_ring)
                        if i >= 1:
                            nrm_transpose(hbs[i - 1], seq[i - 1], 6 + (i % 2))
                    return step
                for i in range(n_ + 1):
                    steps.append(mk(i))
                return steps

            def norm_pass_C(src_dram, lnorm):
                for st_ in norm_pass_C_steps(src_dram, lnorm, range(NT)):
                    st_()

            if l == 0:
                nrm_batch_stats()
                hbs0 = {}
                for step in range(NT + 1):
                    if step < NT:
                        hbs0[step] = nrm_apply(xres[step], step, 0)
                    if step >= 1:
                        nrm_transpose(hbs0[step - 1], step - 1, 6 + (step % 2))
            if l == 0:
                dbg("hT", hT[:, 0, :])
                ckpt("hT")

            def proj_ws(wv, sub, tg, bnk):
                for k in range(8):
                    mm(ps[bnk][:, :], wv[:, k, sub * 128:(sub + 1) * 128], hT[:, k, tg * 512:(tg + 1) * 512],
                       start=(k == 0), stop=(k == 7))

            rope_pend = []

            def rope_flush(keep=0):
                while len(rope_pend) > keep:
                    qb, dst, tg = rope_pend.pop(0)
                    b2 = rotA.get()
                    mm(ps[b2][:, :], perm, qb[:])
                    t1 = ftr.get()
                    tt("pool", t1[:], qb[:], ropec[:, tg * 512:(tg + 1) * 512], ALU.mult)
                    t2 = ftr.get()
                    tt("dve", t2[:], ps[b2][:, :], ropes[:, tg * 512:(tg + 1) * 512], ALU.mult)
                    tt("dve", dst, t1[:], t2[:], ALU.add)

            def rope_from_psum(bnk, dst, tg, scale=1.0):
                qb = btr.get()
                act(qb[:], ps[bnk][:, :], AF.Copy, scale=scale)
                rope_pend.append((qb, dst, tg))
                rope_flush(keep=1)

            Abuf = arena[:, 8192:16384].rearrange("p (t c) -> p t c", t=NT)
            dbufs = Ring([arena[:, 0:8192].rearrange("p (t c) -> p t c", t=NT),
                          arena[:, 16384:24576].rearrange("p (t c) -> p t c", t=NT)])
            fxT = arena[:, 16384:20480].rearrange("p (m t) -> p m t", m=2)
            W4x = arena[:, 20480:21504].rearrange("p (m c) -> p m c", m=2)
            cdft = arena[:, 21504:22528].rearrange("p (k c) -> p k c", k=2)
            def dft_load(db, src3):
                S.dma("sp", db[:, 0:8, :], src3[:, 0:8, :])
                S.dma("pool", db[:, 8:16, :], src3[:, 8:16, :])

            db_first = dbufs.get()
            dft_load(db_first, dftc[0].rearrange("p (t c) -> p t c", t=NT))
            S.dma("sp", cdft, cdft_d.rearrange("(k p) c -> p k c", p=128))

            w4 = wview(ws_next(), 2, 256)
            for o in range(2):
                for m in range(2):
                    bnk = rotA.get()
                    for kc in range(2):
                        mm(ps[bnk][:, 0:256], cdft[:, kc, o * 256 + m * 128:o * 256 + (m + 1) * 128], w4[:, kc, :],
                           start=(kc == 0), stop=(kc == 1))
                    act(W4x[:, m, o * 256:(o + 1) * 256], ps[bnk][:, 0:256], AF.Copy)
            wv = wview(ws_next(), 8, 256)
            for sub in range(2):
                for tg in range(4):
                    bnk = rotA.get()
                    proj_ws(wv, sub, tg, bnk)
                    act(fxT[:, sub, tg * 512:(tg + 1) * 512], ps[bnk][:, :], AF.Copy)
            for t in range(NT):
                bnk = rotA.get()
                for m in range(2):
                    mm(ps[bnk][:, :], fxT[:, m, t * 128:(t + 1) * 128], W4x[:, m, :], start=(m == 0), stop=(m == 1))
                cp("dve", Abuf[:, t, :], ps[bnk][:, :])
            for kg in range(4):
                bks = [rotA.get(), rotA.get()]
                for o in range(2):
                    if kg == 0 and o == 0:
                        db = db_first
                    else:
                        db = dbufs.get()
                        src = (dftc if o == 0 else dfts)[kg].rearrange("p (t c) -> p t c", t=NT)
                        dft_load(db, src)
                    for m in range(2):
                        for t in range(NT):
                            mm(ps[bks[m]][:, :], Abuf[:, t, o * 256 + m * 128:o * 256 + (m + 1) * 128], db[:, t, :],
                               start=(o == 0 and t == 0), stop=(o == 1 and t == NT - 1))
                for m in range(2):
                    act(ybuf[:, m, kg * 512:(kg + 1) * 512], ps[bks[m]][:, :], AF.Copy)
            if l == 0:
                dbg("yapre", ybuf[:, 0, :])
                ckpt("yapre")

            rqT = arena[:, 0:4096].rearrange("p (m t) -> p m t", m=2)
            rkT = arena[:, 4096:8192].rearrange("p (m t) -> p m t", m=2)
            rvb = arena[:, 8192:12288].rearrange("p (t c) -> p t c", t=NT)
            SBin = arena[:, 12288:16384].rearrange("p (t m c) -> p t m c", t=NT, m=2)
            Vaug = arena[:, 16384:20480].rearrange("p (t g e) -> p t g e", t=NT, g=2)
            vcx = arena[:, 20480:21504].rearrange("p (t g e) -> p t g e", t=4, g=2)
            kcx = arena[:, 21504:22528].rearrange("p (g s) -> p g s", g=2)
            lgb = small[:, 0:8]
            lg = small[:, 8:16]
            lgsel = small[:, 16:20].rearrange("p (d m) -> p d m", d=2)
            g128 = small[:, 20:24].rearrange("p (d m) -> p d m", d=2)
            wfb = small[:, 24:32]
            tmp8 = small[:, 32:40]
            gk = small[:, 40:104].rearrange("p (d m c) -> p d m c", d=2, m=2)
            RF = arena[:, 22528:22784].rearrange("p (m i) -> p m i", m=2)
            RB = arena[:, 22784:23040].rearrange("p (m i) -> p m i", m=2)

            S.dma("sp", lgb, ret_decay[l].partition_broadcast(128))
            act(tmp8, lgb, AF.Exp, scale=-1.0)
            act(tmp8, tmp8, AF.Ln, bias=1.0)
            ts("dve", lg, tmp8, -1.0, None, ALU.mult)
            lgv = lg.rearrange("p (d m q) -> p d m q", d=2, m=2)
            cp("dve", lgsel[0:64, :, :], lgv[0:64, :, :, 0])
            cp("dve", lgsel[64:128, :, :], lgv[64:128, :, :, 1])
            for m in range(2):
                act(RF[:, m, :], ctab[:, CT_IP1:CT_IP1 + 128], AF.Exp, scale=lgsel[:, 0, m:m + 1])
                act(RB[:, m, :], ctab[:, CT_CMI:CT_CMI + 128], AF.Exp, scale=lgsel[:, 1, m:m + 1])
            for h in range(4):
                f1 = ftr.get()
                act(f1[:, 0:128], ctab[:, CT_DPOS:CT_DPOS + 128], AF.Exp, scale=lg[:, h:h + 1])
                tt("dve", DT[:, h, :], f1[:, 0:128], ctab[:, CT_MGE:CT_MGE + 128], ALU.mult)
                act(f1[:, 128:256], ctab[:, CT_DNEG:CT_DNEG + 128], AF.Exp, scale=lg[:, 4 + h:5 + h])
                tt("dve", f1[:, 128:256], f1[:, 128:256], ctab[:, CT_MLE:CT_MLE + 128], ALU.mult)
                tt("dve", DT[:, h, :], DT[:, h, :], f1[:, 128:256], ALU.add)
            ts("dve", tmp8[:, 0:4], lg[:, 0:4], ctab[:, CT_CM1MP:CT_CM1MP + 1], None, ALU.mult)
            ts("dve", tmp8[:, 4:8], lg[:, 4:8], ctab[:, CT_P:CT_P + 1], None, ALU.mult)
            act(wfb, tmp8, AF.Exp)
            act(g128.rearrange("p d m -> p (d m)"), lgsel.rearrange("p d m -> p (d m)"), AF.Exp, scale=128.0)
            for d_ in range(2):
                kcol = CT_KEEPF if d_ == 0 else CT_KEEPB
                for m in range(2):
                    ts("dve", gk[:, d_, m, :], ctab[:, kcol:kcol + 16], g128[:, d_, m:m + 1], None, ALU.mult)
            S.dma("sp", gn_bc[:], ret_gn[l].partition_broadcast(128))

            ckpt("ret_tables")
            wv = wview(ws_next(), 8, 256)
            for sub in range(2):
                for tg in range(4):
                    bnk = rotA.get()
                    proj_ws(wv, sub, tg, bnk)
                    rope_from_psum(bnk, rqT[:, sub, tg * 512:(tg + 1) * 512], tg)
            wv = wview(ws_next(), 8, 256)
            for sub in range(2):
                for tg in range(4):
                    bnk = rotA.get()
                    proj_ws(wv, sub, tg, bnk)
                    rope_from_psum(bnk, rkT[:, sub, tg * 512:(tg + 1) * 512], tg, scale=0.125)
            ckpt("ret_proj")
            rope_flush()
            wrv = wview(ws_next(), 8, 256)
            wkv = wview(ws_next(hold=1), 8, 256)
            S.op("pool", lambda e: e.memset(Vaug[:, :, :, 64:128], 1.0), writes=[Vaug[:, :, :, 64:128]])
            S.op("pool", lambda e: e.memset(vcx[:, :, :, 64:128], 1.0), writes=[vcx[:, :, :, 64:128]])
            for t in range(NT):
                bnk = rotA.get()
                for k in range(8):
                    mm(ps[bnk][:, 0:256], hT[:, k, t * 128:(t + 1) * 128], wrv[:, k, :], start=(k == 0), stop=(k == 7),
                       signal=False)
                for k in range(8):
                    mm(ps[bnk][:, 256:512], hT[:, k, t * 128:(t + 1) * 128], wkv[:, k, :], start=(k == 0), stop=(k == 7),
                       check_w=False)
                act(rvb[:, t, :], ps[bnk][:, 0:256], AF.Copy)
                kvs = ftr.get()
                cp("dve", kvs[:, 0:256], ps[bnk][:, 256:512])
                cp("pool", Vaug[:, t, :, 0:64], kvs[:, 128:256].rearrange("p (g d) -> p g d", g=2))
                S.dma("sp", ck_out[l, t * 128:(t + 1) * 128, :], kvs[:, 0:128])
                S.dma("sp", cv_out[l, t * 128:(t + 1) * 128, :], kvs[:, 128:256])

            ckpt("ret_tok")
            def load_s0(dir_):
                st = Sring.get()
                S.op("pool", lambda e: e.memset(st[:], 0.0), writes=[st[:]])
                S.dma("sp", st[0:64, :, 0:64], s0[l, dir_, 0:64, :, :])
                S.dma("sp", st[64:128, :, 64:128], s0[l, dir_, 64:128, :, :])
                return st

            ubanks = Ring([2, 3])
            obanks = Ring([0, 1])
            ktr = Ring([kt_a, kt_b])

            def compute_U(c, dir_):
                bnk = rotB.get()
                for m in range(2):
                    tr(psbf[bnk][:, m * 128:(m + 1) * 128], rkT[:, m, c * 128:(c + 1) * 128])
                kt = ktr.get()
                tt("dve", kt[:, 0:256].rearrange("p (h d) -> p h d", h=4),
                   psbf[bnk][:, 0:256].rearrange("p (h d) -> p h d", h=4),
                   wfb[:, dir_ * 4:(dir_ + 1) * 4].unsqueeze(2).to_broadcast([128, 4, 64]), ALU.mult)
                ub = ubanks.get()
                for m in range(2):
                    mm(ps[ub][:, m * 128:(m + 1) * 128], kt[:, m * 128:(m + 1) * 128], rvb[:, c, m * 128:(m + 1) * 128],
                       signal=(m == 1), check_w=(m == 0))
                return ub

            def state_update(c, dir_, sprev, ub):
                snew = Sring.get()
                for m in range(2):
                    stt("dve", snew[:, m, :], sprev[:, m, :], gk[:, dir_, m, c:c + 1], ps[ub][:, m * 128:(m + 1) * 128],
                        ALU.mult, ALU.add)
                return snew

            def store_state(dir_, seq, st):
                S.dma("sp", st_out[l, dir_, seq, 0:64, :, :], st[0:64, :, 0:64])
                S.dma("sp", st_out[l, dir_, seq, 64:128, :, :], st[64:128, :, 64:128])

            sprev = load_s0(1)
            ub = compute_U(NT - 1, 1)
            for c in range(NT - 1, -1, -1):
                ub_next = compute_U(c - 1, 1) if c > 0 else None
                stt("dve", SBin[:, c, :, :], sprev[:], ctab[:, CT_KEEPB + c:CT_KEEPB + c + 1], bmask3, ALU.mult, ALU.mult)
                sprev = state_update(c, 1, sprev, ub)
                if c % 2 == 0:
                    store_state(1, c // 2, sprev)
                ub = ub_next

            ckpt("ret_bwd")
            fstate = {"s": load_s0(0), "ub": compute_U(0, 0)}
            bOs, robs = {}, {}

            def fwd_S1(c):
                sprev = fstate["s"]
                ub_next = compute_U(c + 1, 0) if c + 1 < NT else None
                sfin = btr.get()
                sfv = sfin[:, 0:256].rearrange("p (m c) -> p m c", m=2)
                stt("dve", sfv, sprev[:], ctab[:, CT_KEEPF + c:CT_KEEPF + c + 1], bmask3, ALU.mult, ALU.mult)
                qs = btr.get()
                qf = qs[:, 0:256].rearrange("p (m i) -> p m i", m=2)
                qbk = qs[:, 256:512].rearrange("p (m i) -> p m i", m=2)
                tt("pool", qf, rqT[:, :, c * 128:(c + 1) * 128], RF, ALU.mult)
                tt("pool", qbk, rqT[:, :, c * 128:(c + 1) * 128], RB, ALU.mult)
                bA = pairs.get()
                for h in range(4):
                    m, par = h // 2, h % 2
                    mm(ps[bA + par][:, m * 128:(m + 1) * 128], rkT[par * 64:(par + 1) * 64, m, c * 128:(c + 1) * 128],
                       rqT[par * 64:(par + 1) * 64, m, c * 128:(c + 1) * 128], signal=(h == 3), check_w=(h < 2))
                attb = btr.get()
                tt("dve", attb[:].rearrange("p (m r i) -> p r m i", m=2, r=2),
                   PS[:, bA * 512:(bA + 2) * 512].rearrange("p (r x) -> p r x", r=2)[:, :, 0:256].rearrange(
                       "p r (m i) -> p r m i", m=2),
                   DT[:].rearrange("p (m r) i -> p r m i", m=2), ALU.mult)
                bO = obanks.get()
                bOs[c] = bO
                for m in range(2):
                    for par in range(2):
                        h = 2 * m + par
                        mm(ps[bO][:, h * 64:(h + 1) * 64], attb[:, h * 128:(h + 1) * 128], rvb[:, c, h * 64:(h + 1) * 64],
                           start=(h == 0), stop=False, signal=False, check_w=(h == 0), skip=True)
                    mm(ps[bO][:, m * 128:(m + 1) * 128], qf[:, m, :], sfv[:, m, :],
                       start=False, stop=False, signal=False, check_w=False, skip=True)
                    mm(ps[bO][:, m * 128:(m + 1) * 128], qbk[:, m, :], SBin[:, c, m, :],
                       start=False, stop=True, signal=(m == 1), check_w=False, skip=True)
                fstate["s"] = state_update(c, 0, sprev, fstate["ub"])
                fstate["ub"] = ub_next
                if c % 2 == 1:
                    store_state(0, c // 2, fstate["s"])

            def fwd_S2(c):
                bO = bOs[c]
                sq = ftr.get()
                act(sq[:, 0:256], ps[bO][:, 0:256], AF.Square)
                st4 = stat[:, 0:4, c]
                S.op("dve", lambda e, sq=sq, st4=st4: e.reduce_sum(st4, sq[:, 0:256].rearrange("p (h d) -> p h d", h=4), AX.X),
                     reads=[sq[:, 0:256]], writes=[st4])
                ts("dve", st4, st4, 1.0 / 64, EPS, ALU.mult, ALU.add)
                act(st4, st4, AF.Ln)
                act(st4, st4, AF.Exp, scale=-0.5)
                tt("dve", sq[:, 256:512].rearrange("p (h d) -> p h d", h=4),
                   ps[bO][:, 0:256].rearrange("p (h d) -> p h d", h=4),
                   st4.unsqueeze(2).to_broadcast([128, 4, 64]), ALU.mult)
                rob = robr.get()
                robs[c] = rob
                tt("pool", rob[:, 0:256], sq[:, 256:512], gn_bc[:], ALU.mult)

            def fwd_S3(c):
                rob = robs[c]
                bT = rotB.get()
                for m in range(2):
                    tr(psbf[bT][:, m * 128:(m + 1) * 128], rob[:, m * 128:(m + 1) * 128])
                act(ybuf[:, 2:4, c * 128:(c + 1) * 128], psbf[bT][:, 0:256].rearrange("p (m t) -> p m t", m=2), AF.Copy)

            for step in range(NT + 2):
                if step < NT:
                    fwd_S1(step)
                if 0 <= step - 1 < NT:
                    fwd_S2(step - 1)
                if 0 <= step - 2 < NT:
                    fwd_S3(step - 2)
            if l == 0:
                dbg("roT", ybuf[:, 2, :])
                ckpt("roT")

            aqT = arena[:, 0:8192].rearrange("p (m t) -> p m t", m=4)
            akT = arena[:, 8192:12288].rearrange("p (g t) -> p g t", g=2)
            esk = small[:, 104:112]
            eskp = small[:, 112:120].rearrange("p (g j) -> p g j", g=2)
            den = small[:, 120:128]
            for kv in range(2):
                S.dma("pool", kcx[0:64, kv, :], kctxT[l, kv * 64:(kv + 1) * 64, :])
                S.dma("pool", kcx[64:128, kv, :], kctxT[l, kv * 64:(kv + 1) * 64, :])
            for g_ in range(2):
                S.dma("pool", vcx[:, :, g_, 0:64], vctx[l][:, g_ * 64:(g_ + 1) * 64].rearrange("(c p) d -> p c d", p=128))
            S.dma("sp", esk, attn_sink[l].partition_broadcast(128))
            act(esk, esk, AF.Exp)
            cp("dve", eskp.rearrange("p g (q c) -> p g q c", q=2),
               esk.rearrange("p (g c q) -> p g q c", g=2, c=2))

            for half in range(2):
                wv = wview(ws_next(), 8, 256)
                for sub in range(2):
                    for tg in range(4):
                        bnk = rotA.get()
                        proj_ws(wv, sub, tg, bnk)
                        rope_from_psum(bnk, aqT[:, half * 2 + sub, tg * 512:(tg + 1) * 512], tg)
            wv = wview(ws_next(), 8, 256)
            for kv in range(2):
                for tg in range(4):
                    bnk = rotA.get()
                    proj_ws(wv, kv, tg, bnk)
                    rope_from_psum(bnk, akT[:, kv, tg * 512:(tg + 1) * 512], tg)

            rope_flush()
            tprev = btab[:, BT_TPREV:BT_TPREV + 128]
            tnext = btab[:, BT_TNEXT:BT_TNEXT + 128]
            LOOK = 2
            apairs = Ring([2, 4, 6])
            pend = []

            def att_front(b, g, ci, kind, idx, bias, tri):
                bnk = apairs.get()
                for par in range(2):
                    pr = slice(par * 64, (par + 1) * 64)
                    if kind == "loc":
                        kk = akT[pr, g, idx * 128:(idx + 1) * 128]
                    else:
                        kk = kcx[pr, g, idx * 128:(idx + 1) * 128]
                    if tri is None:
                        mm(ps[bnk + par][:, 0:256], kk, aqT[pr, 2 * g:2 * g + 2, b * 128:(b + 1) * 128],
                           signal=(par == 1))
                    else:
                        mm(ps[bnk + par][:, 0:256], kk, aqT[pr, 2 * g:2 * g + 2, b * 128:(b + 1) * 128],
                           start=True, stop=False, signal=False, check_w=True)
                        mm(ps[bnk + par][:, 0:256], ident, tri.unsqueeze(1).to_broadcast([128, 2, 128]),
                           start=False, stop=True, signal=(par == 1), check_w=False)
                pt = btr.get()
                act(pt[:].rearrange("p (r x) -> p r x", r=2),
                    PS[:, bnk * 512:(bnk + 2) * 512].rearrange("p (r x) -> p r x", r=2)[:, :, 0:256],
                    AF.Exp, scale=0.125, bias=bias)
                return pt

            def att_back(b, g, ci, kind, idx, pt):
                ob = g
                vv = Vaug[:, idx, g, :] if kind == "loc" else vcx[:, idx, g, :]
                mm(ps[ob][:, :], vv, pt[:], start=(ci == 0), stop=(ci == 6))
                if ci < 6:
                    return
                rec = ftr.get()
                for par in range(2):
                    tt("dve", rec[par * 64:(par + 1) * 64, 0:256].rearrange("p (c q) -> p c q", c=2),
                       ps[ob][64:128, par * 256:(par + 1) * 256].rearrange("p (c q) -> p c q", c=2),
                       eskp[64:128, g, par * 2:par * 2 + 2].unsqueeze(2).to_broadcast([64, 2, 128]), ALU.add)
                S.op("dve", lambda e, rec=rec: e.reciprocal(rec[:, 0:256], rec[:, 0:256]),
                     reads=[rec[:, 0:256]], writes=[rec[:, 0:256]])
                for par in range(2):
                    tt("dve", ybuf[par * 64:(par + 1) * 64, 4 + 2 * g:6 + 2 * g, b * 128:(b + 1) * 128],
                       ps[ob][0:64, par * 256:(par + 1) * 256].rearrange("p (c q) -> p c q", c=2),
                       rec[par * 64:(par + 1) * 64, 0:256].rearrange("p (c q) -> p c q", c=2), ALU.mult)

            for b in range(NT):
                for g in range(2):
                    chunks = [("loc", max(b - 1, 0), ctab[:, CT_BPREV + b:CT_BPREV + b + 1], tprev),
                              ("loc", b, ctab[:, CT_ZERO:CT_ZERO + 1], None),
                              ("loc", min(b + 1, NT - 1), ctab[:, CT_BNEXT + b:CT_BNEXT + b + 1], tnext)]
                    for cc in range(4):
                        chunks.append(("ctx", cc, ctab[:, CT_BCTX:CT_BCTX + 1], None))
                    for ci, (kind, idx, bias, tri) in enumerate(chunks):
                        pt = att_front(b, g, ci, kind, idx, bias, tri)
                        pend.append((b, g, ci, kind, idx, pt))
                        if len(pend) > LOOK:
                            att_back(*pend.pop(0))
            while pend:
                att_back(*pend.pop(0))
            if l == 0:
                dbg("aoT", ybuf[:, 4, :])
                ckpt("aoT")

            if l + 1 < DEPTH:
                compute_mod(l + 1)
            for gi in range(4):
                wv = wview(ws_next(), 8, 256)
                for sub in range(2):
                    for tg in range(4):
                        bnk = rotAll.get()
                        proj_ws(wv, sub, tg, bnk)
                        sg = btr.get()
                        act(sg[:], ps[bnk][:, :], AF.Silu)
                        yv = ybuf[:, gi * 2 + sub, tg * 512:(tg + 1) * 512]
                        tt("pool", yv, yv, sg[:], ALU.mult)
            if l == 0:
                dbg("ya", ybuf[:, 0, :])
                ckpt("ya")

            nxt = l + 1 < DEPTH
            mergedH = arena[:, 0:8192].rearrange("p (k t) -> p k t", k=8)
            wpa_sb = arena[:, 8192:10240].rearrange("p (k c) -> p k c", k=2)
            wpb_sb = arena[:, 10240:12288].rearrange("p (k c) -> p k c", k=2)
            wpc_sb = arena[:, 12288:16384].rearrange("p (k c) -> p k c", k=4)
            wout_sb = arena[:, 16384:24576].rearrange("p (k c) -> p k c", k=8)
            S.dma("pool", wpa_sb, w_pa[l].rearrange("(k p) c -> p k c", p=128))
            S.dma("pool", wpb_sb, w_pb[l].rearrange("(k p) c -> p k c", p=128))
            S.dma("pool", wpc_sb, w_pc[l].rearrange("(k p) c -> p k c", p=128))
            S.dma("pool", wout_sb, w_out[l].rearrange("(k p) c -> p k c", p=128))
            wps = (wpa_sb, wpb_sb, wpc_sb)
            ybase = (0, 2, 4)
            ykc = (2, 2, 4)

            def merge_unit(wgv, dp, br, sub, tg):
                bg = rotAll.get()
                proj_ws(wgv, sub, tg, bg)
                sg = btr.get()
                act(sg[:], ps[bg][:, :], AF.Sigmoid)
                bp = rotAll.get()
                dcol = dp * 256 + sub * 128
                for kc in range(ykc[br]):
                    mm(ps[bp][:, :], wps[br][:, kc, dcol:dcol + 128],
                       ybuf[:, ybase[br] + kc, tg * 512:(tg + 1) * 512],
                       start=(kc == 0), stop=(kc == ykc[br] - 1))
                dst = mergedH[:, dp * 2 + sub, (tg % 2) * 512:(tg % 2 + 1) * 512]
                if br == 0:
                    tt("dve", dst, ps[bp][:, :], sg[:], ALU.mult)
                else:
                    a = ftr.get()
                    tt("dve", a[:], ps[bp][:, :], sg[:], ALU.mult)
                    tt("pool", dst, dst, a[:], ALU.add)

            def obuf_ap(t, n):
                half, tl = t // 8, t % 8
                return ybuf[:, tl, half * 1024 + n * 512:half * 1024 + (n + 1) * 512]

            def passA_tile(t):
                tl = t % 8
                bks = [(t % 4) * 2, (t % 4) * 2 + 1]
                for n in range(2):
                    for k in range(8):
                        mm(ps[bks[n]][:, :], mergedH[:, k, tl * 128:(tl + 1) * 128], wout_sb[:, k, n * 512:(n + 1) * 512],
                           start=(k == 0), stop=(k == 7))
                    act(junk[:, 0:512], ps[bks[n]][:, :], AF.Square, accum_out=stat[:, n, t:t + 1])
                    tt("dve", obuf_ap(t, n), ps[bks[n]][:, :], gg_bc[l][:, n * 512:(n + 1) * 512], ALU.mult)

            def statsA(c0, c1):
                tt("dve", stat[:, 2, c0:c1], stat[:, 0, c0:c1], stat[:, 1, c0:c1], ALU.add)
                ts("dve", stat[:, 2, c0:c1], stat[:, 2, c0:c1], 1.0 / D, EPS, ALU.mult, ALU.add)
                act(stat[:, 3, c0:c1], stat[:, 2, c0:c1], AF.Ln)
                act(stat[:, 3, c0:c1], stat[:, 3, c0:c1], AF.Exp, scale=-0.5)

            xtB = {}

            def passB_load(t):
                xtB[t] = xst.get()
                S.dma("sp", xtB[t][:], xsrc[t * 128:(t + 1) * 128, :])

            def passB_tile(t, tmp_ring):
                xt = xtB[t]
                for n in range(2):
                    stt("dve", xt[:, n * 512:(n + 1) * 512], obuf_ap(t, n), stat[:, 3, t:t + 1],
                        xt[:, n * 512:(n + 1) * 512], ALU.mult, ALU.add)
                S.dma("sp", xdst[t * 128:(t + 1) * 128, :], xt[:, :])
                if nxt:
                    act(junk[:], xt[:, :], AF.Square, accum_out=stat[:, 4, t:t + 1])

            def tail_steps(half, tmp_ring):
                c0, c1 = half * 8, half * 8 + 8
                steps = []
                for t in range(c0, c1):
                    def stepB(t=t):
                        if t == c0 and t not in xtB:
                            passB_load(t)
                        if t + 1 < c1 and (t + 1) not in xtB:
                            passB_load(t + 1)
                        passB_tile(t, tmp_ring)
                    steps.append(stepB)
                if nxt:
                    steps.append(lambda: nrm_batch_stats(c0, c1))
                    steps += norm_pass_C_steps(xdst, l + 1, range(c0, c1), tmp_ring,
                                               resident=(xtB if half == 1 else None))
                return steps

            deferred = []
            for half in range(2):
                for dp in range(4):
                    for br in range(3):
                        wgv = wview(ws_next(), 8, 256)
                        for sub in range(2):
                            for tg in (2 * half, 2 * half + 1):
                                merge_unit(wgv, dp, br, sub, tg)
                                if deferred:
                                    deferred.pop(0)()
                while deferred:
                    deferred.pop(0)()
                if l == 0 and half == 0:
                    dbg("merged", mergedH[:, 0, :])
                    ckpt("merged")
                if half == 1 and nxt:
                    for t in range(8, 12):
                        xtB[t] = arena[:, 8192 + (t - 8) * 2048:8192 + (t - 7) * 2048].bitcast(F32)
                        S.dma("sp", xtB[t], xsrc[t * 128:(t + 1) * 128, :])
                if half == 1 and not nxt:
                    for t in range(8, NT):
                        xtB[t] = hT[:, t - 8, :].bitcast(F32)
                        S.dma("sp", xtB[t], xsrc[t * 128:(t + 1) * 128, :])
                for t in range(half * 8, half * 8 + 8):
                    passA_tile(t)
                statsA(half * 8, half * 8 + 8)
                if half == 1 and nxt:
                    for t in range(12, NT):
                        xtB[t] = arena[:, (t - 12) * 2048:(t - 11) * 2048].bitcast(F32)
                        S.dma("sp", xtB[t], xsrc[t * 128:(t + 1) * 128, :])
                deferred = tail_steps(half, dfr if half == 0 else ftr)
            while deferred:
                deferred.pop(0)()
            ckpt(f"layer{l}")

    except _Stop:
        pass
    S.finish()
    S.run()
    return nc, dbg_outs, S


def _bf(a):
    return np.ascontiguousarray(a.astype(ml_dtypes.bfloat16))


def _const_tables(is_latent):
    C = 128
    seqlen = T if is_latent else 256
    n = np.arange(seqlen)
    ang = 2.0 * np.pi * ((n[:, None] * n[None, :]) % seqlen) / seqlen
    cb = (np.cos(ang) / np.sqrt(seqlen)).astype(np.float32)
    sbk = (-np.sin(ang) / np.sqrt(seqlen)).astype(np.float32)
    dc = np.zeros((T, T), np.float32)
    ds = np.zeros((T, T), np.float32)
    for s in range(T // seqlen):
        dc[s * seqlen:(s + 1) * seqlen, s * seqlen:(s + 1) * seqlen] = cb
        ds[s * seqlen:(s + 1) * seqlen, s * seqlen:(s + 1) * seqlen] = sbk
    m = np.arange(64)
    a64 = 2.0 * np.pi * ((m[:, None] * m[None, :]) % 64) / 64
    c64 = np.cos(a64) / 8.0
    s64 = np.sin(a64) / 8.0
    cd = np.zeros((256, 512), np.float32)
    for g in range(4):
        cd[g * 64:(g + 1) * 64, g * 64:(g + 1) * 64] = c64
        cd[g * 64:(g + 1) * 64, 256 + g * 64:256 + (g + 1) * 64] = s64
    rc = np.ones((128, T), np.float32)
    rs = np.zeros((128, T), np.float32)
    if is_latent:
        pos_row = (np.arange(T) // 64).astype(np.float32)
        pos_col = (np.arange(T) % 64).astype(np.float32)
        inv = (10000.0 ** (-np.arange(16, dtype=np.float32) / 16)).astype(np.float32)
        for p in range(128):
            d = p % 64
            half, idx = d // 32, d % 32
            pos = pos_row if half == 0 else pos_col
            a = pos * inv[idx % 16]
            rc[p] = np.cos(a)
            rs[p] = -np.sin(a) if idx < 16 else np.sin(a)
    perm = np.zeros((128, 128), np.float32)
    for mcol in range(128):
        idx = mcol % 32
        partner = mcol + 16 if idx < 16 else mcol - 16
        perm[partner, mcol] = 1.0
    j = np.arange(C)[:, None].astype(np.float32)
    i = np.arange(C)[None, :].astype(np.float32)
    ct = np.zeros((128, CT_W), np.float32)
    ct[:, CT_DPOS:CT_DPOS + 128] = np.maximum(i - j, 0)
    ct[:, CT_DNEG:CT_DNEG + 128] = np.maximum(j - i, 0)
    ct[:, CT_MGE:CT_MGE + 128] = (i >= j)
    ct[:, CT_MLE:CT_MLE + 128] = (i <= j)
    ct[:, CT_IP1:CT_IP1 + 128] = i + 1
    ct[:, CT_CMI:CT_CMI + 128] = C - i
    ct[:, CT_CM1MP] = C - 1 - np.arange(C)
    ct[:, CT_P] = np.arange(C)
    for b in range(NT):
        if is_latent:
            ct[:, CT_BPREV + b] = NEG if b == 0 else 0.0
            ct[:, CT_BNEXT + b] = NEG if b == NT - 1 else 0.0
            ct[:, CT_KEEPF + b] = 1.0
            ct[:, CT_KEEPB + b] = 1.0
        else:
            ct[:, CT_BPREV + b] = 0.0 if b % 2 == 1 else NEG
            ct[:, CT_BNEXT + b] = 0.0 if b % 2 == 0 else NEG
            ct[:, CT_KEEPF + b] = 0.0 if b % 2 == 0 else 1.0
            ct[:, CT_KEEPB + b] = 0.0 if b % 2 == 1 else 1.0
    ct[:, CT_BCTX] = 0.0 if is_latent else NEG
    bt = np.zeros((128, BT_W), np.float32)
    bt[:, BT_ID:BT_ID + 128] = np.eye(128)
    bt[:, BT_PERM:BT_PERM + 128] = perm
    if is_latent:
        bt[:, BT_TPREV:BT_TPREV + 128] = np.where(i <= j, 0.0, 8.0 * NEG)
        bt[:, BT_TNEXT:BT_TNEXT + 128] = np.where(j <= i, 0.0, 8.0 * NEG)
    for m_ in range(2):
        for p_ in range(128):
            lo = 0 if p_ < 64 else 64
            bt[p_, BT_BMASK + m_ * 128 + lo:BT_BMASK + m_ * 128 + lo + 64] = 1.0
    dc = dc.reshape(NT, 128, 4, 512).transpose(2, 1, 0, 3).reshape(4, 128, 8192)
    ds = ds.reshape(NT, 128, 4, 512).transpose(2, 1, 0, 3).reshape(4, 128, 8192)
    return {"dftc": _bf(dc), "dfts": _bf(ds), "cdft": _bf(cd), "ropec": _bf(rc), "ropes": _bf(rs),
            "ctab": ct, "btab": _bf(bt)}


_CACHE = {}


def kernel(x_prompt, x_sample, cache_k, cache_v, state_ret, c, c_ctx, w_mod, b_mod, g_pre, g_post, w_in,
           w_four, ret_decay, ret_gn, attn_sink, w_branch_a, w_branch_b, w_branch_c, w_out):
    f = lambda a: np.ascontiguousarray(np.asarray(a, dtype=np.float32))
    x_prompt, x_sample, cache_k, cache_v, state_ret, c, c_ctx = map(f, (x_prompt, x_sample, cache_k, cache_v,
                                                                       state_ret, c, c_ctx))
    shared = {
        "w_mod": f(w_mod), "b_mod": f(b_mod), "g_pre": f(g_pre), "g_post": f(g_post), "w_in": f(w_in),
        "w_four": f(w_four), "ret_decay": f(ret_decay).reshape(DEPTH, 8), "ret_gn": f(ret_gn),
        "attn_sink": f(attn_sink).reshape(DEPTH, 8), "w_pa": f(w_branch_a), "w_pb": f(w_branch_b),
        "w_pc": f(w_branch_c), "w_out": f(w_out),
    }
    if "nc" not in _CACHE:
        _CACHE["nc"] = build_nc(DEBUG)
        _CACHE["tabs"] = (_const_tables(False), _const_tables(True))
    nc, dbg_outs, _ = _CACHE["nc"]
    tabs_p, tabs_l = _CACHE["tabs"]
    in_maps = []
    for core in range(8):
        m = dict(shared)
        if core < 4:
            m.update(tabs_p)
            m["x"] = x_prompt[core * 8:(core + 1) * 8].reshape(T, D)
            m["cvec"] = c_ctx
            m["kctxT"] = np.zeros((DEPTH, 128, 512), np.float32)
            m["vctx"] = np.zeros((DEPTH, 512, 128), np.float32)
            m["s0"] = np.zeros((DEPTH, 2, 128, 2, 64), np.float32)
        else:
            b = core - 4
            m.update(tabs_l)
            m["x"] = x_sample[b]
            m["cvec"] = c[b]
            m["kctxT"] = np.ascontiguousarray(cache_k[b].transpose(0, 2, 3, 1).reshape(DEPTH, 128, 512))
            m["vctx"] = np.ascontiguousarray(cache_v[b].reshape(DEPTH, 512, 128))
            s = state_ret[b].reshape(DEPTH, 2, 2, 2, 64, 64)
            m["s0"] = np.ascontiguousarray(s.transpose(0, 1, 3, 4, 2, 5).reshape(DEPTH, 2, 128, 2, 64))
        in_maps.append(m)
    res = run_bass_kernel_spmd(nc, in_maps, core_ids=list(range(8)))
    R = res.results
    _CACHE["last"] = R
    y_prompt = np.concatenate([R[i]["y"].reshape(8, 256, D) for i in range(4)], axis=0)
    y_sample = np.stack([R[4 + i]["y"] for i in range(4)], axis=0)
    ck = np.concatenate([R[i]["ck"].reshape(DEPTH, 8, 256, 2, 64).transpose(1, 0, 2, 3, 4) for i in range(4)], axis=0)
    cv = np.concatenate([R[i]["cv"].reshape(DEPTH, 8, 256, 2, 64).transpose(1, 0, 2, 3, 4) for i in range(4)], axis=0)
    sts = []
    for i in range(4):
        s = R[i]["st"].reshape(DEPTH, 2, 8, 2, 64, 2, 64)
        s = s.transpose(2, 0, 1, 5, 3, 4, 6).reshape(8, DEPTH, 2, 4, 64, 64)
        sts.append(s)
    st = np.concatenate(sts, axis=0)
    return (y_prompt.astype(np.float32), y_sample.astype(np.float32), np.ascontiguousarray(ck, dtype=np.float32),
            np.ascontiguousarray(cv, dtype=np.float32), np.ascontiguousarray(st, dtype=np.float32))
```

```python
import numpy as np
import ml_dtypes
import concourse.bass as bass
import concourse.mybir as mybir
from concourse.bass_utils import run_bass_kernel_spmd

F32 = mybir.dt.float32
BF = mybir.dt.bfloat16
AF = mybir.ActivationFunctionType
ALU = mybir.AluOpType
AX = mybir.AxisListType

D = 1024
T = 2048
NT = 16
DEPTH = 2
EPS = 1e-6
NEG = -30000.0
DEBUG = False
STOP = None


class _Stop(Exception):
    pass

C_FX, C_FZ, C_RQ, C_RK, C_RV, C_RZ, C_AQ, C_AK, C_AV, C_AZ, C_GA, C_GB, C_GC = (
    0, 256, 512, 768, 1024, 1280, 1536, 2048, 2176, 2304, 2816, 3840, 4864)

CT_DPOS, CT_DNEG, CT_MGE, CT_MLE, CT_IP1, CT_CMI = 0, 128, 256, 384, 512, 640
CT_CM1MP, CT_P, CT_BPREV, CT_BNEXT, CT_BCTX, CT_ZERO, CT_KEEPF, CT_KEEPB, CT_W = 768, 769, 770, 786, 802, 803, 804, 820, 836
BT_ID, BT_PERM, BT_TPREV, BT_TNEXT, BT_BMASK, BT_W = 0, 128, 256, 384, 512, 768

BIG = 1 << 40


class Sched:
    ENG = ("pe", "act", "dve", "pool", "sp")

    def __init__(self, nc, n_dma_sp=24, n_dma_pool=16):
        self.nc = nc
        self.q = {e: [] for e in self.ENG}
        self.sem = {e: nc.alloc_semaphore(f"s_{e}") for e in ("pe", "act", "dve", "pool")}
        self.cnt = {e: 0 for e in ("pe", "act", "dve", "pool")}
        self.dpool = {
            "sp": [nc.alloc_semaphore(f"dsp{i}") for i in range(n_dma_sp)],
            "pool": [nc.alloc_semaphore(f"dpl{i}") for i in range(n_dma_pool)],
        }
        self.duse = {"sp": [0] * n_dma_sp, "pool": [0] * n_dma_pool}
        self.drr = {"sp": 0, "pool": 0}
        self.clock = {e: {} for e in self.ENG}
        self.tokclock = {}
        self.acc = {}
        self.ninst = 0

    @staticmethod
    def _intervals(off, dims):
        dims = [(abs(s), c) for s, c in dims if c > 1 and s != 0]
        dims.sort()
        ivs = [(off, off + 1)]
        for s, c in dims:
            if len(ivs) == 1 and s <= ivs[0][1] - ivs[0][0]:
                lo, hi = ivs[0]
                ivs = [(lo, hi + (c - 1) * s)]
            elif len(ivs) * c <= 64:
                ivs = [(lo + k * s, hi + k * s) for k in range(c) for lo, hi in ivs]
            else:
                lo = min(i[0] for i in ivs)
                hi = max(i[1] for i in ivs)
                ivs = [(lo, hi + (c - 1) * s)]
        return tuple(sorted(ivs))

    @classmethod
    def region(cls, ap):
        t = ap.tensor
        name = t.name
        aps = ap.ap
        off = int(ap.offset)
        tn = type(t).__name__
        esz = {BF: 2, F32: 4}.get(ap.dtype, 4)
        if tn.startswith("DRam"):
            return (name, 0, 1, cls._intervals(off * esz, [(s_ * esz, c_) for s_, c_ in aps] + [(1, esz)]))
        if tn.startswith("PSum"):
            ivs = cls._intervals(off, aps[1:]) if aps[0][0] else cls._intervals(off, aps)
            pstep = aps[0][0]
            banks = set()
            for lo, hi in ivs:
                if pstep:
                    lo, hi = lo % pstep, (hi - 1) % pstep + 1
                b0, b1 = (lo * esz) // 2048, ((hi * esz) - 1) // 2048
                banks.update(range(b0, b1 + 1))
            return ("PSUM", 0, 128, tuple((b * 2048, (b + 1) * 2048) for b in sorted(banks)))
        pstep, npart = aps[0]
        if pstep == 0:
            p0, f0 = 0, off
        else:
            p0, f0 = off // pstep, off % pstep
        return (name, p0, p0 + npart, cls._intervals(f0 * esz, [(s_ * esz, c_) for s_, c_ in aps[1:]] + [(1, esz)]))

    @staticmethod
    def _ov(a, b):
        if not (a[1] < b[2] and b[1] < a[2]):
            return False
        for lo, hi in a[3]:
            for lo2, hi2 in b[3]:
                if lo < hi2 and lo2 < hi:
                    return True
        return False

    @staticmethod
    def _contains(outer, inner):
        if not (outer[1] <= inner[1] and inner[2] <= outer[2]):
            return False
        for lo, hi in inner[3]:
            ok = False
            for lo2, hi2 in outer[3]:
                if lo2 <= lo and hi <= hi2:
                    ok = True
                    break
            if not ok:
                return False
        return True

    @staticmethod
    def _need(need, key, val):
        if need.get(key, 0) < val:
            need[key] = val

    def _collect(self, rregs, wregs):
        need = {}
        for r in rregs:
            ispsum = r[0] == "PSUM"
            for rec in self.acc.get(r[0], ()):
                if (rec[1] or ispsum) and self._ov(r, rec[0]):
                    self._need(need, rec[2], rec[3])
        for w in wregs:
            for rec in self.acc.get(w[0], ()):
                if self._ov(w, rec[0]):
                    self._need(need, rec[2], rec[3])
        return need

    def _record(self, rregs, wregs, key, val):
        for w in wregs:
            lst = self.acc.setdefault(w[0], [])
            lst[:] = [rec for rec in lst if not self._contains(w, rec[0])]
            lst.append((w, True, key, val))
        for r in rregs:
            lst = self.acc.setdefault(r[0], [])
            if r[0] == "PSUM":
                lst[:] = [rec for rec in lst if not self._contains(r, rec[0])]
                lst.append((r, True, key, val))
                continue
            if not isinstance(key, tuple):
                lst[:] = [rec for rec in lst if not (rec[2] == key and not rec[1] and rec[0] == r)]
            lst.append((r, False, key, val))

    def _semof(self, key):
        if isinstance(key, tuple):
            return self.dpool[key[0]][key[1]]
        return self.sem[key]

    def _waits(self, eng, need):
        ck = self.clock[eng]
        out = []
        for key, val in need.items():
            if key == "pe" and eng == "pe":
                continue
            if ck.get(key, 0) >= val:
                continue
            out.append((key, val))
        for key, val in out:
            tc = self.tokclock.get((key, val))
            if tc:
                for k2, v2 in tc.items():
                    if ck.get(k2, 0) < v2:
                        ck[k2] = v2
            if ck.get(key, 0) < val:
                ck[key] = val
        return [(self._semof(k), v) for k, v in out]

    def op(self, eng, fn, reads=(), writes=(), signal=True, check_w=True):
        rregs = [self.region(a) for a in reads]
        wregs = [self.region(a) for a in writes]
        need = self._collect(rregs, wregs if check_w else ())
        waits = self._waits(eng, need)
        val = self.cnt[eng] + 1
        sem = self.sem[eng]
        if signal:
            self.cnt[eng] = val
            self.tokclock[(eng, val)] = dict(self.clock[eng])

        def emit(e, fn=fn, waits=waits, signal=signal, sem=sem):
            for s, v in waits:
                e.wait_ge(s, v)
            ins = fn(e)
            if signal:
                ins.then_inc(sem, 1)

        self.q[eng].append(emit)
        self._record(rregs, wregs, eng, val)
        self.ninst += 1

    def dma(self, eng, out, in_, **kw):
        i = self.drr[eng]
        self.drr[eng] = (i + 1) % len(self.dpool[eng])
        use = self.duse[eng][i]
        key = (eng, i)
        rregs = [self.region(in_)]
        wregs = [self.region(out)]
        need = self._collect(rregs, wregs)
        if use > 0:
            self._need(need, key, 16 * use)
        waits = self._waits(eng, need)
        self.duse[eng][i] = use + 1
        val = 16 * (use + 1)
        sem = self.dpool[eng][i]
        self.tokclock[(key, val)] = dict(self.clock[eng])

        def emit(e, waits=waits, sem=sem, out=out, in_=in_, kw=kw):
            for s, v in waits:
                e.wait_ge(s, v)
            e.dma_start(out=out, in_=in_, **kw).then_inc(sem, 16)

        self.q[eng].append(emit)
        self._record(rregs, wregs, key, val)
        self.ninst += 1

    def finish(self):
        need = {}
        for eng in ("sp", "pool"):
            for i, use in enumerate(self.duse[eng]):
                if use:
                    need[(eng, i)] = 16 * use
        for e in ("pe", "act", "dve", "pool"):
            if self.cnt[e]:
                need[e] = self.cnt[e]
        waits = [(self._semof(k), v) for k, v in need.items()]

        def emit(e, waits=waits):
            for s, v in waits:
                e.wait_ge(s, v)

        self.q["sp"].append(emit)

    def run(self):
        nc = self.nc
        q = self.q
        with nc.Block() as block:

            @block.tensor
            def _(e):
                for f in q["pe"]:
                    f(e)

            @block.scalar
            def _(e):
                for f in q["act"]:
                    f(e)

            @block.vector
            def _(e):
                for f in q["dve"]:
                    f(e)

            @block.gpsimd
            def _(e):
                for f in q["pool"]:
                    f(e)

            @block.sync
            def _(e):
                for f in q["sp"]:
                    f(e)


class Ring:
    def __init__(self, items):
        self.items = list(items)
        self.i = 0

    def get(self):
        x = self.items[self.i]
        self.i = (self.i + 1) % len(self.items)
        return x


def build_nc(debug=False):
    nc = bass.Bass("TRN2", target_bir_lowering=False)
    S = Sched(nc)
    dbg_outs = []

    def din(name, shape, dt=F32):
        return nc.dram_tensor(name, list(shape), dt, kind="ExternalInput").ap()

    def dout(name, shape, dt=F32):
        return nc.dram_tensor(name, list(shape), dt, kind="ExternalOutput").ap()

    x_in = din("x", [T, D])
    cvec = din("cvec", [D])
    kctxT = din("kctxT", [DEPTH, 128, 512])
    vctx = din("vctx", [DEPTH, 512, 128])
    s0 = din("s0", [DEPTH, 2, 128, 2, 64])
    w_mod = din("w_mod", [DEPTH, D, 3 * D])
    b_mod = din("b_mod", [DEPTH, 3 * D])
    g_pre = din("g_pre", [DEPTH, D])
    g_post = din("g_post", [DEPTH, D])
    w_in = din("w_in", [DEPTH, D, 5888])
    w_four = din("w_four", [DEPTH, 256, 256])
    ret_decay = din("ret_decay", [DEPTH, 8])
    ret_gn = din("ret_gn", [DEPTH, 256])
    attn_sink = din("attn_sink", [DEPTH, 8])
    w_pa = din("w_pa", [DEPTH, 256, D])
    w_pb = din("w_pb", [DEPTH, 256, D])
    w_pc = din("w_pc", [DEPTH, 512, D])
    w_out = din("w_out", [DEPTH, D, D])
    dftc = din("dftc", [4, 128, 8192], BF)
    dfts = din("dfts", [4, 128, 8192], BF)
    cdft_d = din("cdft", [256, 512], BF)
    ropec_d = din("ropec", [128, T], BF)
    ropes_d = din("ropes", [128, T], BF)
    ctab_d = din("ctab", [128, CT_W])
    btab_d = din("btab", [128, BT_W], BF)

    y_out = dout("y", [T, D])
    ck_out = dout("ck", [DEPTH, T, 128])
    cv_out = dout("cv", [DEPTH, T, 128])
    st_out = dout("st", [DEPTH, 2, 8, 128, 2, 64])
    x1s = nc.dram_tensor("x1s", [T, D], F32, kind="Internal").ap()

    sb = nc.alloc_sbuf_tensor
    hT = sb("hT", [128, 8, T], BF)
    ybuf = sb("ybuf", [128, 8, T], BF)
    ARN = 24576
    arena = sb("arena", [128, ARN], BF)
    NWS = 4
    wsb = [sb(f"ws{i}", [128, 2048], BF) for i in range(NWS)]
    gv_bc = [sb(f"gvbc{l}", [128, D], F32) for l in range(DEPTH)]
    sh_bc = [sb(f"shbc{l}", [128, D], F32) for l in range(DEPTH)]
    gg_bc = [sb(f"ggbc{l}", [128, D], F32) for l in range(DEPTH)]
    ropec = sb("ropec_sb", [128, T], BF)
    ropes = sb("ropes_sb", [128, T], BF)
    ctab = sb("ctab_sb", [128, CT_W], F32)
    btab = sb("btab_sb", [128, BT_W], BF)
    DT = sb("DT", [128, 4, 128], BF)
    gn_bc = sb("gn_bc", [128, 256], F32)
    xst = Ring([sb(f"xst{i}", [128, D], F32) for i in range(3)])
    ftr = Ring([sb(f"ft{i}", [128, 512], F32) for i in range(4)])
    _bt = [sb(f"bt{i}", [128, 512], BF) for i in range(6)]
    btr = Ring(_bt[0:4])
    hbr = Ring([sb(f"hb{i}", [128, D], BF) for i in range(2)])
    robr = Ring([sb(f"rob{i}", [128, 256], BF) for i in range(2)])
    dfr = Ring([sb(f"df{i}", [128, 512], BF) for i in range(2)])
    kt_a = _bt[4]
    kt_b = _bt[5]
    junk = sb("junk", [128, D], BF)
    small = sb("small", [128, 256], F32)
    stat = sb("stat", [128, 8, NT], F32)
    sc_sb = sb("sc_sb", [128, 8], BF)
    Sring = Ring([sb(f"S{i}", [128, 2, 128], F32) for i in range(4)])

    ident = btab[:, BT_ID:BT_ID + 128]
    perm = btab[:, BT_PERM:BT_PERM + 128]

    PS = nc.alloc_psum_tensor("PS", [128, 4096], F32)
    ps = [PS[:, i * 512:(i + 1) * 512] for i in range(8)]
    psbf = [p.bitcast(BF) for p in ps]
    bmask3 = btab[:, BT_BMASK:BT_BMASK + 256].rearrange("p (m c) -> p m c", m=2)
    rotA = Ring([2, 3, 4, 5, 6, 7])
    rotAll = Ring(list(range(8)))
    pairs = Ring([4, 6])
    rotB = Ring([4, 5, 6, 7])

    def mm(out, lhsT, rhs, start=True, stop=True, signal=None, check_w=None, skip=False):
        kw = {"skip_group_check": True} if skip else {}
        S.op("pe", lambda e: e.matmul(out, lhsT, rhs, start=start, stop=stop, **kw),
             reads=[lhsT, rhs], writes=[out],
             signal=stop if signal is None else signal,
             check_w=start if check_w is None else check_w)

    def tr(out, in_):
        S.op("pe", lambda e: e.transpose(out, in_, ident), reads=[in_, ident], writes=[out])

    def act(out, in_, func, reads=None, **kw):
        extra = [v for v in (kw.get("scale"), kw.get("bias"), kw.get("accum_out")) if hasattr(v, "tensor")]
        wr = [out] + ([kw["accum_out"]] if kw.get("accum_out") is not None else [])
        rd = [in_] + [v for v in (kw.get("scale"), kw.get("bias")) if hasattr(v, "tensor")]
        S.op("act", lambda e: e.activation(out, in_, func, **kw), reads=rd, writes=wr)

    def tt(eng, out, in0, in1, op):
        S.op(eng, lambda e: e.tensor_tensor(out, in0, in1, op), reads=[in0, in1], writes=[out])

    def ts(eng, out, in0, s1, s2, op0, op1=None):
        rd = [in0] + [v for v in (s1, s2) if hasattr(v, "tensor")]
        if op1 is None:
            S.op(eng, lambda e: e.tensor_scalar(out, in0, s1, None, op0), reads=rd, writes=[out])
        else:
            S.op(eng, lambda e: e.tensor_scalar(out, in0, s1, s2, op0, op1), reads=rd, writes=[out])

    def stt(eng, out, in0, scalar, in1, op0, op1):
        rd = [in0, in1] + ([scalar] if hasattr(scalar, "tensor") else [])
        S.op(eng, lambda e: e.scalar_tensor_tensor(out, in0, scalar, in1, op0, op1), reads=rd, writes=[out])

    def cp(eng, out, in_):
        S.op(eng, lambda e: e.tensor_copy(out, in_), reads=[in_], writes=[out])

    def ckpt(name):
        if STOP == name:
            raise _Stop()

    def dbg(name, ap):
        if not debug:
            return
        shp = list(ap.shape)
        d = nc.dram_tensor("dbg_" + name, shp, ap.dtype, kind="ExternalOutput").ap()
        S.dma("sp", d, ap)
        dbg_outs.append("dbg_" + name)

    jobs = []

    class WS:
        issued = 0
        used = 0

    def ws_issue_upto(n):
        while WS.issued < min(n, len(jobs)):
            j = WS.issued
            buf = wsb[j % NWS]
            for dst_fn, src in jobs[j]:
                S.dma("pool", dst_fn(buf), src)
            WS.issued += 1

    def ws_next(hold=0):
        j = WS.used
        ws_issue_upto(j - hold + NWS)
        WS.used += 1
        return wsb[j % NWS]

    def job_cols(src2d, c0, ncols, kc=8):
        src = src2d[:, c0:c0 + ncols].rearrange("(k p) c -> p k c", p=128)
        return [(lambda b, kc=kc, ncols=ncols: b[:, 0:kc * ncols].rearrange("p (k c) -> p k c", k=kc), src)]

    def wview(buf, kc, ncols):
        return buf[:, 0:kc * ncols].rearrange("p (k c) -> p k c", k=kc)

    for ng in range(12):
        jobs.append(job_cols(w_mod[0], ng * 256, 256))
    for l in range(DEPTH):
        jobs.append(job_cols(w_four[l], 0, 256, kc=2))
        jobs.append(job_cols(w_in[l], C_FX, 256))
        jobs.append(job_cols(w_in[l], C_RQ, 256))
        jobs.append(job_cols(w_in[l], C_RK, 256))
        jobs.append(job_cols(w_in[l], C_RV, 256))
        jobs.append(job_cols(w_in[l], C_AK, 256))
        jobs.append(job_cols(w_in[l], C_AQ, 256))
        jobs.append(job_cols(w_in[l], C_AQ + 256, 256))
        akj = []
        for g_ in range(2):
            srck = w_in[l][:, C_AK + g_ * 64:C_AK + (g_ + 1) * 64].rearrange("(k p) d -> p k d", p=128)
            for u_ in range(2):
                akj.append((lambda b, g_=g_, u_=u_: b[:, 0:2048].rearrange(
                    "p (k g u d) -> p k g u d", k=8, g=2, u=2)[:, :, g_, u_, :], srck))
        jobs.append(akj)
        if l + 1 < DEPTH:
            for ng in range(12):
                jobs.append(job_cols(w_mod[l + 1], ng * 256, 256))
        jobs.append(job_cols(w_in[l], C_FZ, 256))
        jobs.append(job_cols(w_in[l], C_RZ, 256))
        jobs.append(job_cols(w_in[l], C_AZ, 256))
        jobs.append(job_cols(w_in[l], C_AZ + 256, 256))
        for half in range(2):
            for dp in range(4):
                jobs.append(job_cols(w_in[l], C_GA + dp * 256, 256))
                jobs.append(job_cols(w_in[l], C_GB + dp * 256, 256))
                jobs.append(job_cols(w_in[l], C_GC + dp * 256, 256))

    try:
        S.dma("sp", ctab[:], ctab_d)
        S.dma("sp", btab[:], btab_d)
        S.dma("sp", ropec[:], ropec_d)
        S.dma("sp", ropes[:], ropes_d)

        cv_col = small[:, 0:8]
        sc_col = sc_sb[:, :]
        ones_r = small[0:1, 128:256]
        S.dma("sp", cv_col, cvec.rearrange("(k p) -> p k", p=128), allow_slow_non_contiguous=True)
        act(sc_col, cv_col, AF.Silu)
        S.op("dve", lambda e: e.memset(ones_r, 1.0), writes=[ones_r])

        def compute_mod(l):
            xA, xB = xst.items[0], xst.items[1]
            rowm = xA[0:1, 0:512]
            rowb = xA[0:1, 512:1024]
            rowg = xB[0:1, 0:512]
            rowr = xB[0:1, 512:1024]
            for part in range(3):
                for n in range(2):
                    c0 = part * D + n * 512
                    S.dma("sp", rowb, b_mod[l:l + 1, c0:c0 + 512])
                    bnk = rotA.get()
                    for q2 in range(2):
                        wv = wview(ws_next(), 8, 256)
                        for k in range(8):
                            mm(ps[bnk][0:1, q2 * 256:(q2 + 1) * 256], sc_col[:, k:k + 1], wv[:, k, :],
                               start=(k == 0), stop=(k == 7), signal=(q2 == 1 and k == 7), check_w=(q2 == 0 and k == 0))
                    tt("dve", rowm, ps[bnk][0:1, :], rowb, ALU.add)
                    if part == 0:
                        src_row, dst = rowm, sh_bc[l]
                    elif part == 1:
                        S.dma("sp", rowg, g_pre[l:l + 1, n * 512:(n + 1) * 512])
                        stt("dve", rowr, rowm, 1.0, rowg, ALU.add, ALU.mult)
                        src_row, dst = rowr, gv_bc[l]
                    else:
                        S.dma("sp", rowg, g_post[l:l + 1, n * 512:(n + 1) * 512])
                        tt("dve", rowr, rowm, rowg, ALU.mult)
                        src_row, dst = rowr, gg_bc[l]
                    b2 = rotA.get()
                    mm(ps[b2][:, :], ones_r, src_row)
                    act(dst[:, n * 512:(n + 1) * 512], ps[b2][:, :], AF.Copy)

        xres = [ybuf[:, t, :].bitcast(F32) for t in range(8)] + \
               [arena[:, (t - 8) * 2048:(t - 7) * 2048].bitcast(F32) for t in range(8, NT)]
        for t in range(NT):
            S.dma("sp", xres[t], x_in[t * 128:(t + 1) * 128, :])
            act(junk[:], xres[t], AF.Square, accum_out=stat[:, 4, t:t + 1])
        compute_mod(0)

        def nrm_batch_stats(c0=0, c1=NT):
            ts("dve", stat[:, 5, c0:c1], stat[:, 4, c0:c1], 1.0 / D, EPS, ALU.mult, ALU.add)
            act(stat[:, 6, c0:c1], stat[:, 5, c0:c1], AF.Ln)
            act(stat[:, 7, c0:c1], stat[:, 6, c0:c1], AF.Exp, scale=-0.5)

        def nrm_apply(xt, t, l, tmp_ring=None):
            hb = hbr.get()
            for n in range(2):
                f = (tmp_ring or ftr).get()
                stt("dve", f[:], xt[:, n * 512:(n + 1) * 512], stat[:, 7, t:t + 1], gv_bc[l][:, n * 512:(n + 1) * 512],
                    ALU.mult, ALU.mult)
                tt("pool" if (t + n) % 2 == 0 else "dve", hb[:, n * 512:(n + 1) * 512], f[:],
                   sh_bc[l][:, n * 512:(n + 1) * 512], ALU.add)
            return hb

        def nrm_transpose(hb, t, bnk):
            for k in range(8):
                tr(psbf[bnk][:, k * 128:(k + 1) * 128], hb[:, k * 128:(k + 1) * 128])
            act(hT[:, :, t * 128:(t + 1) * 128], psbf[bnk][:, :].rearrange("p (k c) -> p k c", k=8), AF.Copy)

        ckpt("setup")
        for l in range(DEPTH):
            xsrc = x_in if l == 0 else x1s
            xdst = x1s if l == 0 else y_out

            def norm_pass_C_steps(src_dram, lnorm, tiles, tmp_ring=None, resident=None):
                hbs, xts_ = {}, {}
                steps = []
                seq = list(tiles)
                n_ = len(seq)

                def load(i):
                    if resident is not None and seq[i] in resident:
                        xts_[i] = resident[seq[i]]
                        return
                    xts_[i] = xst.get()
                    S.dma("sp", xts_[i][:], src_dram[seq[i] * 128:(seq[i] + 1) * 128, :])

                def mk(i):
                    def step():
                        if i == 0:
                            load(0)
                        if i + 1 < n_:
                            load(i + 1)
                        if i < n_:
                            hbs[i] = nrm_apply(xts_[i], seq[i], lnorm, tmp_ring)
                        if i >= 1:
                            nrm_transpose(hbs[i - 1], seq[i - 1], 6 + (i % 2))
                    return step
                for i in range(n_ + 1):
                    steps.append(mk(i))
                return steps

            def norm_pass_C(src_dram, lnorm):
                for st_ in norm_pass_C_steps(src_dram, lnorm, range(NT)):
                    st_()

            if l == 0:
                nrm_batch_stats()
                hbs0 = {}
                for step in range(NT + 1):
                    if step < NT:
                        hbs0[step] = nrm_apply(xres[step], step, 0)
                    if step >= 1:
                        nrm_transpose(hbs0[step - 1], step - 1, 6 + (step % 2))
            if l == 0:
                dbg("hT", hT[:, 0, :])
                ckpt("hT")

            def proj_ws(wv, sub, tg, bnk):
                for k in range(8):
                    mm(ps[bnk][:, :], wv[:, k, sub * 128:(sub + 1) * 128], hT[:, k, tg * 512:(tg + 1) * 512],
                       start=(k == 0), stop=(k == 7))

            rope_pend = []

            def rope_flush(keep=0):
                while len(rope_pend) > keep:
                    qb, dst, tg = rope_pend.pop(0)
                    b2 = rotA.get()
                    mm(ps[b2][:, :], perm, qb[:])
                    t1 = ftr.get()
                    tt("pool", t1[:], qb[:], ropec[:, tg * 512:(tg + 1) * 512], ALU.mult)
                    t2 = ftr.get()
                    tt("dve", t2[:], ps[b2][:, :], ropes[:, tg * 512:(tg + 1) * 512], ALU.mult)
                    tt("dve", dst, t1[:], t2[:], ALU.add)

            def rope_from_psum(bnk, dst, tg, scale=1.0):
                qb = btr.get()
                act(qb[:], ps[bnk][:, :], AF.Copy, scale=scale)
                rope_pend.append((qb, dst, tg))
                rope_flush(keep=1)

            Abuf = arena[:, 8192:16384].rearrange("p (t c) -> p t c", t=NT)
            dbufs = Ring([arena[:, 0:8192].rearrange("p (t c) -> p t c", t=NT),
                          arena[:, 16384:24576].rearrange("p (t c) -> p t c", t=NT)])
            fxT = arena[:, 16384:20480].rearrange("p (m t) -> p m t", m=2)
            W4x = arena[:, 20480:21504].rearrange("p (m c) -> p m c", m=2)
            cdft = arena[:, 21504:22528].rearrange("p (k c) -> p k c", k=2)
            def dft_load(db, src3):
                S.dma("sp", db[:, 0:8, :], src3[:, 0:8, :])
                S.dma("pool", db[:, 8:16, :], src3[:, 8:16, :])

            db_first = dbufs.get()
            dft_load(db_first, dftc[0].rearrange("p (t c) -> p t c", t=NT))
            S.dma("sp", cdft, cdft_d.rearrange("(k p) c -> p k c", p=128))

            w4 = wview(ws_next(), 2, 256)
            for o in range(2):
                for m in range(2):
                    bnk = rotA.get()
                    for kc in range(2):
                        mm(ps[bnk][:, 0:256], cdft[:, kc, o * 256 + m * 128:o * 256 + (m + 1) * 128], w4[:, kc, :],
                           start=(kc == 0), stop=(kc == 1))
                    act(W4x[:, m, o * 256:(o + 1) * 256], ps[bnk][:, 0:256], AF.Copy)
            wv = wview(ws_next(), 8, 256)
            for sub in range(2):
                for tg in range(4):
                    bnk = rotA.get()
                    proj_ws(wv, sub, tg, bnk)
                    act(fxT[:, sub, tg * 512:(tg + 1) * 512], ps[bnk][:, :], AF.Copy)
            for t in range(NT):
                bnk = rotA.get()
                for m in range(2):
                    mm(ps[bnk][:, :], fxT[:, m, t * 128:(t + 1) * 128], W4x[:, m, :], start=(m == 0), stop=(m == 1))
                cp("dve", Abuf[:, t, :], ps[bnk][:, :])
            for kg in range(4):
                bks = [rotA.get(), rotA.get()]
                for o in range(2):
                    if kg == 0 and o == 0:
                        db = db_first
                    else:
                        db = dbufs.get()
                        src = (dftc if o == 0 else dfts)[kg].rearrange("p (t c) -> p t c", t=NT)
                        dft_load(db, src)
                    for m in range(2):
                        for t in range(NT):
                            mm(ps[bks[m]][:, :], Abuf[:, t, o * 256 + m * 128:o * 256 + (m + 1) * 128], db[:, t, :],
                               start=(o == 0 and t == 0), stop=(o == 1 and t == NT - 1))
                for m in range(2):
                    act(ybuf[:, m, kg * 512:(kg + 1) * 512], ps[bks[m]][:, :], AF.Copy)
            if l == 0:
                dbg("yapre", ybuf[:, 0, :])
                ckpt("yapre")

            rqT = arena[:, 0:4096].rearrange("p (m t) -> p m t", m=2)
            rkT = arena[:, 4096:8192].rearrange("p (m t) -> p m t", m=2)
            rvb = arena[:, 8192:12288].rearrange("p (t c) -> p t c", t=NT)
            SBin = arena[:, 12288:16384].rearrange("p (t m c) -> p t m c", t=NT, m=2)
            Vaug = arena[:, 16384:20480].rearrange("p (t g e) -> p t g e", t=NT, g=2)
            vcx = arena[:, 20480:21504].rearrange("p (t g e) -> p t g e", t=4, g=2)
            kcx = arena[:, 21504:22528].rearrange("p (g s) -> p g s", g=2)
            lgb = small[:, 0:8]
            lg = small[:, 8:16]
            lgsel = small[:, 16:20].rearrange("p (d m) -> p d m", d=2)
            g128 = small[:, 20:24].rearrange("p (d m) -> p d m", d=2)
            wfb = small[:, 24:32]
            tmp8 = small[:, 32:40]
            gk = small[:, 40:104].rearrange("p (d m c) -> p d m c", d=2, m=2)
            RF = arena[:, 22528:22784].rearrange("p (m i) -> p m i", m=2)
            RB = arena[:, 22784:23040].rearrange("p (m i) -> p m i", m=2)

            S.dma("sp", lgb, ret_decay[l].partition_broadcast(128))
            act(tmp8, lgb, AF.Exp, scale=-1.0)
            act(tmp8, tmp8, AF.Ln, bias=1.0)
            ts("dve", lg, tmp8, -1.0, None, ALU.mult)
            lgv = lg.rearrange("p (d m q) -> p d m q", d=2, m=2)
            cp("dve", lgsel[0:64, :, :], lgv[0:64, :, :, 0])
            cp("dve", lgsel[64:128, :, :], lgv[64:128, :, :, 1])
            for m in range(2):
                act(RF[:, m, :], ctab[:, CT_IP1:CT_IP1 + 128], AF.Exp, scale=lgsel[:, 0, m:m + 1])
                act(RB[:, m, :], ctab[:, CT_CMI:CT_CMI + 128], AF.Exp, scale=lgsel[:, 1, m:m + 1])
            for h in range(4):
                f1 = ftr.get()
                act(f1[:, 0:128], ctab[:, CT_DPOS:CT_DPOS + 128], AF.Exp, scale=lg[:, h:h + 1])
                tt("dve", DT[:, h, :], f1[:, 0:128], ctab[:, CT_MGE:CT_MGE + 128], ALU.mult)
                act(f1[:, 128:256], ctab[:, CT_DNEG:CT_DNEG + 128], AF.Exp, scale=lg[:, 4 + h:5 + h])
                tt("dve", f1[:, 128:256], f1[:, 128:256], ctab[:, CT_MLE:CT_MLE + 128], ALU.mult)
                tt("dve", DT[:, h, :], DT[:, h, :], f1[:, 128:256], ALU.add)
            ts("dve", tmp8[:, 0:4], lg[:, 0:4], ctab[:, CT_CM1MP:CT_CM1MP + 1], None, ALU.mult)
            ts("dve", tmp8[:, 4:8], lg[:, 4:8], ctab[:, CT_P:CT_P + 1], None, ALU.mult)
            act(wfb, tmp8, AF.Exp)
            act(g128.rearrange("p d m -> p (d m)"), lgsel.rearrange("p d m -> p (d m)"), AF.Exp, scale=128.0)
            for d_ in range(2):
                kcol = CT_KEEPF if d_ == 0 else CT_KEEPB
                for m in range(2):
                    ts("dve", gk[:, d_, m, :], ctab[:, kcol:kcol + 16], g128[:, d_, m:m + 1], None, ALU.mult)
            S.dma("sp", gn_bc[:], ret_gn[l].partition_broadcast(128))

            ckpt("ret_tables")
            wv = wview(ws_next(), 8, 256)
            for sub in range(2):
                for tg in range(4):
                    bnk = rotA.get()
                    proj_ws(wv, sub, tg, bnk)
                    rope_from_psum(bnk, rqT[:, sub, tg * 512:(tg + 1) * 512], tg)
            wv = wview(ws_next(), 8, 256)
            for sub in range(2):
                for tg in range(4):
                    bnk = rotA.get()
                    proj_ws(wv, sub, tg, bnk)
                    rope_from_psum(bnk, rkT[:, sub, tg * 512:(tg + 1) * 512], tg, scale=0.125)
            ckpt("ret_proj")
            rope_flush()
            wrv = wview(ws_next(), 8, 256)
            wkv = wview(ws_next(hold=1), 8, 256)
            S.op("pool", lambda e: e.memset(Vaug[:, :, :, 64:128], 1.0), writes=[Vaug[:, :, :, 64:128]])
            S.op("pool", lambda e: e.memset(vcx[:, :, :, 64:128], 1.0), writes=[vcx[:, :, :, 64:128]])
            for t in range(NT):
                bnk = rotA.get()
                for k in range(8):
                    mm(ps[bnk][:, 0:256], hT[:, k, t * 128:(t + 1) * 128], wrv[:, k, :], start=(k == 0), stop=(k == 7),
                       signal=False)
                for k in range(8):
                    mm(ps[bnk][:, 256:512], hT[:, k, t * 128:(t + 1) * 128], wkv[:, k, :], start=(k == 0), stop=(k == 7),
                       check_w=False)
                act(rvb[:, t, :], ps[bnk][:, 0:256], AF.Copy)
                kvs = ftr.get()
                cp("dve", kvs[:, 0:256], ps[bnk][:, 256:512])
                cp("pool", Vaug[:, t, :, 0:64], kvs[:, 128:256].rearrange("p (g d) -> p g d", g=2))
                S.dma("sp", ck_out[l, t * 128:(t + 1) * 128, :], kvs[:, 0:128])
                S.dma("sp", cv_out[l, t * 128:(t + 1) * 128, :], kvs[:, 128:256])

            ckpt("ret_tok")
            def load_s0(dir_):
                st = Sring.get()
                S.op("pool", lambda e: e.memset(st[:], 0.0), writes=[st[:]])
                S.dma("sp", st[0:64, :, 0:64], s0[l, dir_, 0:64, :, :])
                S.dma("sp", st[64:128, :, 64:128], s0[l, dir_, 64:128, :, :])
                return st

            ubanks = Ring([2, 3])
            obanks = Ring([0, 1])
            ktr = Ring([kt_a, kt_b])

            def compute_U(c, dir_):
                bnk = rotB.get()
                for m in range(2):
                    tr(psbf[bnk][:, m * 128:(m + 1) * 128], rkT[:, m, c * 128:(c + 1) * 128])
                kt = ktr.get()
                tt("dve", kt[:, 0:256].rearrange("p (h d) -> p h d", h=4),
                   psbf[bnk][:, 0:256].rearrange("p (h d) -> p h d", h=4),
                   wfb[:, dir_ * 4:(dir_ + 1) * 4].unsqueeze(2).to_broadcast([128, 4, 64]), ALU.mult)
                ub = ubanks.get()
                for m in range(2):
                    mm(ps[ub][:, m * 128:(m + 1) * 128], kt[:, m * 128:(m + 1) * 128], rvb[:, c, m * 128:(m + 1) * 128],
                       signal=(m == 1), check_w=(m == 0))
                return ub

            def state_update(c, dir_, sprev, ub):
                snew = Sring.get()
                for m in range(2):
                    stt("dve", snew[:, m, :], sprev[:, m, :], gk[:, dir_, m, c:c + 1], ps[ub][:, m * 128:(m + 1) * 128],
                        ALU.mult, ALU.add)
                return snew

            def store_state(dir_, seq, st):
                S.dma("sp", st_out[l, dir_, seq, 0:64, :, :], st[0:64, :, 0:64])
                S.dma("sp", st_out[l, dir_, seq, 64:128, :, :], st[64:128, :, 64:128])

            sprev = load_s0(1)
            ub = compute_U(NT - 1, 1)
            for c in range(NT - 1, -1, -1):
                ub_next = compute_U(c - 1, 1) if c > 0 else None
                stt("dve", SBin[:, c, :, :], sprev[:], ctab[:, CT_KEEPB + c:CT_KEEPB + c + 1], bmask3, ALU.mult, ALU.mult)
                sprev = state_update(c, 1, sprev, ub)
                if c % 2 == 0:
                    store_state(1, c // 2, sprev)
                ub = ub_next

            ckpt("ret_bwd")
            fstate = {"s": load_s0(0), "ub": compute_U(0, 0)}
            bOs, robs = {}, {}

            def fwd_S1(c):
                sprev = fstate["s"]
                ub_next = compute_U(c + 1, 0) if c + 1 < NT else None
                sfin = btr.get()
                sfv = sfin[:, 0:256].rearrange("p (m c) -> p m c", m=2)
                stt("dve", sfv, sprev[:], ctab[:, CT_KEEPF + c:CT_KEEPF + c + 1], bmask3, ALU.mult, ALU.mult)
                qs = btr.get()
                qf = qs[:, 0:256].rearrange("p (m i) -> p m i", m=2)
                qbk = qs[:, 256:512].rearrange("p (m i) -> p m i", m=2)
                tt("pool", qf, rqT[:, :, c * 128:(c + 1) * 128], RF, ALU.mult)
                tt("pool", qbk, rqT[:, :, c * 128:(c + 1) * 128], RB, ALU.mult)
                bA = pairs.get()
                for h in range(4):
                    m, par = h // 2, h % 2
                    mm(ps[bA + par][:, m * 128:(m + 1) * 128], rkT[par * 64:(par + 1) * 64, m, c * 128:(c + 1) * 128],
                       rqT[par * 64:(par + 1) * 64, m, c * 128:(c + 1) * 128], signal=(h == 3), check_w=(h < 2))
                attb = btr.get()
                tt("dve", attb[:].rearrange("p (m r i) -> p r m i", m=2, r=2),
                   PS[:, bA * 512:(bA + 2) * 512].rearrange("p (r x) -> p r x", r=2)[:, :, 0:256].rearrange(
                       "p r (m i) -> p r m i", m=2),
                   DT[:].rearrange("p (m r) i -> p r m i", m=2), ALU.mult)
                bO = obanks.get()
                bOs[c] = bO
                for m in range(2):
                    for par in range(2):
                        h = 2 * m + par
                        mm(ps[bO][:, h * 64:(h + 1) * 64], attb[:, h * 128:(h + 1) * 128], rvb[:, c, h * 64:(h + 1) * 64],
                           start=(h == 0), stop=False, signal=False, check_w=(h == 0), skip=True)
                    mm(ps[bO][:, m * 128:(m + 1) * 128], qf[:, m, :], sfv[:, m, :],
                       start=False, stop=False, signal=False, check_w=False, skip=True)
                    mm(ps[bO][:, m * 128:(m + 1) * 128], qbk[:, m, :], SBin[:, c, m, :],
                       start=False, stop=True, signal=(m == 1), check_w=False, skip=True)
                fstate["s"] = state_update(c, 0, sprev, fstate["ub"])
                fstate["ub"] = ub_next
                if c % 2 == 1:
                    store_state(0, c // 2, fstate["s"])

            def fwd_S2(c):
                bO = bOs[c]
                sq = ftr.get()
                act(sq[:, 0:256], ps[bO][:, 0:256], AF.Square)
                st4 = stat[:, 0:4, c]
                S.op("dve", lambda e, sq=sq, st4=st4: e.reduce_sum(st4, sq[:, 0:256].rearrange("p (h d) -> p h d", h=4), AX.X),
                     reads=[sq[:, 0:256]], writes=[st4])
                ts("dve", st4, st4, 1.0 / 64, EPS, ALU.mult, ALU.add)
                act(st4, st4, AF.Ln)
                act(st4, st4, AF.Exp, scale=-0.5)
                tt("dve", sq[:, 256:512].rearrange("p (h d) -> p h d", h=4),
                   ps[bO][:, 0:256].rearrange("p (h d) -> p h d", h=4),
                   st4.unsqueeze(2).to_broadcast([128, 4, 64]), ALU.mult)
                rob = robr.get()
                robs[c] = rob
                tt("pool", rob[:, 0:256], sq[:, 256:512], gn_bc[:], ALU.mult)

            def fwd_S3(c):
                rob = robs[c]
                bT = rotB.get()
                for m in range(2):
                    tr(psbf[bT][:, m * 128:(m + 1) * 128], rob[:, m * 128:(m + 1) * 128])
                act(ybuf[:, 2:4, c * 128:(c + 1) * 128], psbf[bT][:, 0:256].rearrange("p (m t) -> p m t", m=2), AF.Copy)

            for step in range(NT + 2):
                if step < NT:
                    fwd_S1(step)
                if 0 <= step - 1 < NT:
                    fwd_S2(step - 1)
                if 0 <= step - 2 < NT:
                    fwd_S3(step - 2)
            if l == 0:
                dbg("roT", ybuf[:, 2, :])
                ckpt("roT")

            aqT = arena[:, 0:8192].rearrange("p (m t) -> p m t", m=4)
            akT = arena[:, 8192:12288].rearrange("p (g t) -> p g t", g=2)
            esk = small[:, 104:112]
            eskp = small[:, 112:120].rearrange("p (g j) -> p g j", g=2)
            den = small[:, 120:128]
            for kv in range(2):
                S.dma("pool", kcx[0:64, kv, :], kctxT[l, kv * 64:(kv + 1) * 64, :])
                S.dma("pool", kcx[64:128, kv, :], kctxT[l, kv * 64:(kv + 1) * 64, :])
            for g_ in range(2):
                S.dma("pool", vcx[:, :, g_, 0:64], vctx[l][:, g_ * 64:(g_ + 1) * 64].rearrange("(c p) d -> p c d", p=128))
            S.dma("sp", esk, attn_sink[l].partition_broadcast(128))
            act(esk, esk, AF.Exp)
            cp("dve", eskp.rearrange("p g (q c) -> p g q c", q=2),
               esk.rearrange("p (g c q) -> p g q c", g=2, c=2))

            for half in range(2):
                wv = wview(ws_next(), 8, 256)
                for sub in range(2):
                    for tg in range(4):
                        bnk = rotA.get()
                        proj_ws(wv, sub, tg, bnk)
                        rope_from_psum(bnk, aqT[:, half * 2 + sub, tg * 512:(tg + 1) * 512], tg)
            wv = wview(ws_next(), 8, 256)
            for kv in range(2):
                for tg in range(4):
                    bnk = rotA.get()
                    proj_ws(wv, kv, tg, bnk)
                    rope_from_psum(bnk, akT[:, kv, tg * 512:(tg + 1) * 512], tg)

            rope_flush()
            tprev = btab[:, BT_TPREV:BT_TPREV + 128]
            tnext = btab[:, BT_TNEXT:BT_TNEXT + 128]
            LOOK = 2
            apairs = Ring([2, 4, 6])
            pend = []

            def att_front(b, g, ci, kind, idx, bias, tri):
                bnk = apairs.get()
                for par in range(2):
                    pr = slice(par * 64, (par + 1) * 64)
                    if kind == "loc":
                        kk = akT[pr, g, idx * 128:(idx + 1) * 128]
                    else:
                        kk = kcx[pr, g, idx * 128:(idx + 1) * 128]
                    if tri is None:
                        mm(ps[bnk + par][:, 0:256], kk, aqT[pr, 2 * g:2 * g + 2, b * 128:(b + 1) * 128],
                           signal=(par == 1))
                    else:
                        mm(ps[bnk + par][:, 0:256], kk, aqT[pr, 2 * g:2 * g + 2, b * 128:(b + 1) * 128],
                           start=True, stop=False, signal=False, check_w=True)
                        mm(ps[bnk + par][:, 0:256], ident, tri.unsqueeze(1).to_broadcast([128, 2, 128]),
                           start=False, stop=True, signal=(par == 1), check_w=False)
                pt = btr.get()
                act(pt[:].rearrange("p (r x) -> p r x", r=2),
                    PS[:, bnk * 512:(bnk + 2) * 512].rearrange("p (r x) -> p r x", r=2)[:, :, 0:256],
                    AF.Exp, scale=0.125, bias=bias)
                return pt

            def att_back(b, g, ci, kind, idx, pt):
                ob = g
                vv = Vaug[:, idx, g, :] if kind == "loc" else vcx[:, idx, g, :]
                mm(ps[ob][:, :], vv, pt[:], start=(ci == 0), stop=(ci == 6))
                if ci < 6:
                    return
                rec = ftr.get()
                for par in range(2):
                    tt("dve", rec[par * 64:(par + 1) * 64, 0:256].rearrange("p (c q) -> p c q", c=2),
                       ps[ob][64:128, par * 256:(par + 1) * 256].rearrange("p (c q) -> p c q", c=2),
                       eskp[64:128, g, par * 2:par * 2 + 2].unsqueeze(2).to_broadcast([64, 2, 128]), ALU.add)
                S.op("dve", lambda e, rec=rec: e.reciprocal(rec[:, 0:256], rec[:, 0:256]),
                     reads=[rec[:, 0:256]], writes=[rec[:, 0:256]])
                for par in range(2):
                    tt("dve", ybuf[par * 64:(par + 1) * 64, 4 + 2 * g:6 + 2 * g, b * 128:(b + 1) * 128],
                       ps[ob][0:64, par * 256:(par + 1) * 256].rearrange("p (c q) -> p c q", c=2),
                       rec[par * 64:(par + 1) * 64, 0:256].rearrange("p (c q) -> p c q", c=2), ALU.mult)

            for b in range(NT):
                for g in range(2):
                    chunks = [("loc", max(b - 1, 0), ctab[:, CT_BPREV + b:CT_BPREV + b + 1], tprev),
                              ("loc", b, ctab[:, CT_ZERO:CT_ZERO + 1], None),
                              ("loc", min(b + 1, NT - 1), ctab[:, CT_BNEXT + b:CT_BNEXT + b + 1], tnext)]
                    for cc in range(4):
                        chunks.append(("ctx", cc, ctab[:, CT_BCTX:CT_BCTX + 1], None))
                    for ci, (kind, idx, bias, tri) in enumerate(chunks):
                        pt = att_front(b, g, ci, kind, idx, bias, tri)
                        pend.append((b, g, ci, kind, idx, pt))
                        if len(pend) > LOOK:
                            att_back(*pend.pop(0))
            while pend:
                att_back(*pend.pop(0))
            if l == 0:
                dbg("aoT", ybuf[:, 4, :])
                ckpt("aoT")

            S.dma("pool", arena[:, 8192:10240].rearrange("p (k c) -> p k c", k=2),
                  w_pa[l].rearrange("(k p) c -> p k c", p=128))
            S.dma("pool", arena[:, 10240:12288].rearrange("p (k c) -> p k c", k=2),
                  w_pb[l].rearrange("(k p) c -> p k c", p=128))
            S.dma("pool", arena[:, 12288:16384].rearrange("p (k c) -> p k c", k=4),
                  w_pc[l].rearrange("(k p) c -> p k c", p=128))
            if l + 1 < DEPTH:
                compute_mod(l + 1)
            for gi in range(4):
                wv = wview(ws_next(), 8, 256)
                for sub in range(2):
                    for tg in range(4):
                        bnk = rotAll.get()
                        proj_ws(wv, sub, tg, bnk)
                        sg = btr.get()
                        act(sg[:], ps[bnk][:, :], AF.Silu)
                        yv = ybuf[:, gi * 2 + sub, tg * 512:(tg + 1) * 512]
                        tt("pool", yv, yv, sg[:], ALU.mult)
            if l == 0:
                dbg("ya", ybuf[:, 0, :])
                ckpt("ya")

            nxt = l + 1 < DEPTH
            mergedH = arena[:, 0:8192].rearrange("p (k t) -> p k t", k=8)
            wpa_sb = arena[:, 8192:10240].rearrange("p (k c) -> p k c", k=2)
            wpb_sb = arena[:, 10240:12288].rearrange("p (k c) -> p k c", k=2)
            wpc_sb = arena[:, 12288:16384].rearrange("p (k c) -> p k c", k=4)
            wout_sb = arena[:, 16384:24576].rearrange("p (k c) -> p k c", k=8)
            wps = (wpa_sb, wpb_sb, wpc_sb)
            ybase = (0, 2, 4)
            ykc = (2, 2, 4)

            def merge_unit(wgv, dp, br, sub, tg):
                bg = rotAll.get()
                proj_ws(wgv, sub, tg, bg)
                sg = btr.get()
                act(sg[:], ps[bg][:, :], AF.Sigmoid)
                bp = rotAll.get()
                dcol = dp * 256 + sub * 128
                for kc in range(ykc[br]):
                    mm(ps[bp][:, :], wps[br][:, kc, dcol:dcol + 128],
                       ybuf[:, ybase[br] + kc, tg * 512:(tg + 1) * 512],
                       start=(kc == 0), stop=(kc == ykc[br] - 1))
                dst = mergedH[:, dp * 2 + sub, (tg % 2) * 512:(tg % 2 + 1) * 512]
                if br == 0:
                    tt("dve", dst, ps[bp][:, :], sg[:], ALU.mult)
                else:
                    a = ftr.get()
                    tt("dve", a[:], ps[bp][:, :], sg[:], ALU.mult)
                    tt("pool", dst, dst, a[:], ALU.add)

            def obuf_ap(t, n):
                half, tl = t // 8, t % 8
                return ybuf[:, tl, half * 1024 + n * 512:half * 1024 + (n + 1) * 512]

            def passA_tile(t):
                tl = t % 8
                bks = [(t % 4) * 2, (t % 4) * 2 + 1]
                for n in range(2):
                    for k in range(8):
                        mm(ps[bks[n]][:, :], mergedH[:, k, tl * 128:(tl + 1) * 128], wout_sb[:, k, n * 512:(n + 1) * 512],
                           start=(k == 0), stop=(k == 7))
                    act(junk[:, 0:512], ps[bks[n]][:, :], AF.Square, accum_out=stat[:, n, t:t + 1])
                    tt("dve", obuf_ap(t, n), ps[bks[n]][:, :], gg_bc[l][:, n * 512:(n + 1) * 512], ALU.mult)

            def statsA(c0, c1):
                tt("dve", stat[:, 2, c0:c1], stat[:, 0, c0:c1], stat[:, 1, c0:c1], ALU.add)
                ts("dve", stat[:, 2, c0:c1], stat[:, 2, c0:c1], 1.0 / D, EPS, ALU.mult, ALU.add)
                act(stat[:, 3, c0:c1], stat[:, 2, c0:c1], AF.Ln)
                act(stat[:, 3, c0:c1], stat[:, 3, c0:c1], AF.Exp, scale=-0.5)

            xtB = {}

            def passB_load(t):
                xtB[t] = xst.get()
                S.dma("sp", xtB[t][:], xsrc[t * 128:(t + 1) * 128, :])

            def passB_tile(t, tmp_ring):
                xt = xtB[t]
                for n in range(2):
                    stt("dve", xt[:, n * 512:(n + 1) * 512], obuf_ap(t, n), stat[:, 3, t:t + 1],
                        xt[:, n * 512:(n + 1) * 512], ALU.mult, ALU.add)
                S.dma("sp", xdst[t * 128:(t + 1) * 128, :], xt[:, :])
                if nxt:
                    act(junk[:], xt[:, :], AF.Square, accum_out=stat[:, 4, t:t + 1])

            def tail_steps(half, tmp_ring):
                c0, c1 = half * 8, half * 8 + 8
                steps = []
                for t in range(c0, c1):
                    def stepB(t=t):
                        if t == c0 and t not in xtB:
                            passB_load(t)
                        if t + 1 < c1 and (t + 1) not in xtB:
                            passB_load(t + 1)
                        passB_tile(t, tmp_ring)
                    steps.append(stepB)
                if nxt:
                    steps.append(lambda: nrm_batch_stats(c0, c1))
                    steps += norm_pass_C_steps(xdst, l + 1, range(c0, c1), tmp_ring,
                                               resident=(xtB if half == 1 else None))
                return steps

            deferred = []
            for half in range(2):
                for dp in range(4):
                    if half == 0 and dp == 1:
                        S.dma("pool", wout_sb, w_out[l].rearrange("(k p) c -> p k c", p=128))
                    for br in range(3):
                        wgv = wview(ws_next(), 8, 256)
                        for sub in range(2):
                            for tg in (2 * half, 2 * half + 1):
                                merge_unit(wgv, dp, br, sub, tg)
                                if deferred:
                                    deferred.pop(0)()
                while deferred:
                    deferred.pop(0)()
                if l == 0 and half == 0:
                    dbg("merged", mergedH[:, 0, :])
                    ckpt("merged")
                if half == 1 and nxt:
                    for t in range(8, 12):
                        xtB[t] = arena[:, 8192 + (t - 8) * 2048:8192 + (t - 7) * 2048].bitcast(F32)
                        S.dma("sp", xtB[t], xsrc[t * 128:(t + 1) * 128, :])
                if half == 1 and not nxt:
                    for t in range(8, NT):
                        xtB[t] = hT[:, t - 8, :].bitcast(F32)
                        S.dma("sp", xtB[t], xsrc[t * 128:(t + 1) * 128, :])
                for t in range(half * 8, half * 8 + 8):
                    passA_tile(t)
                statsA(half * 8, half * 8 + 8)
                if half == 1 and nxt:
                    for t in range(12, NT):
                        xtB[t] = arena[:, (t - 12) * 2048:(t - 11) * 2048].bitcast(F32)
                        S.dma("sp", xtB[t], xsrc[t * 128:(t + 1) * 128, :])
                deferred = tail_steps(half, dfr if half == 0 else ftr)
            while deferred:
                deferred.pop(0)()
            ckpt(f"layer{l}")

    except _Stop:
        pass
    S.finish()
    S.run()
    return nc, dbg_outs, S


def _bf(a):
    return np.ascontiguousarray(a.astype(ml_dtypes.bfloat16))


def _const_tables(is_latent):
    C = 128
    seqlen = T if is_latent else 256
    n = np.arange(seqlen)
    ang = 2.0 * np.pi * ((n[:, None] * n[None, :]) % seqlen) / seqlen
    cb = (np.cos(ang) / np.sqrt(seqlen)).astype(np.float32)
    sbk = (-np.sin(ang) / np.sqrt(seqlen)).astype(np.float32)
    dc = np.zeros((T, T), np.float32)
    ds = np.zeros((T, T), np.float32)
    for s in range(T // seqlen):
        dc[s * seqlen:(s + 1) * seqlen, s * seqlen:(s + 1) * seqlen] = cb
        ds[s * seqlen:(s + 1) * seqlen, s * seqlen:(s + 1) * seqlen] = sbk
    m = np.arange(64)
    a64 = 2.0 * np.pi * ((m[:, None] * m[None, :]) % 64) / 64
    c64 = np.cos(a64) / 8.0
    s64 = np.sin(a64) / 8.0
    cd = np.zeros((256, 512), np.float32)
    for g in range(4):
        cd[g * 64:(g + 1) * 64, g * 64:(g + 1) * 64] = c64
        cd[g * 64:(g + 1) * 64, 256 + g * 64:256 + (g + 1) * 64] = s64
    rc = np.ones((128, T), np.float32)
    rs = np.zeros((128, T), np.float32)
    if is_latent:
        pos_row = (np.arange(T) // 64).astype(np.float32)
        pos_col = (np.arange(T) % 64).astype(np.float32)
        inv = (10000.0 ** (-np.arange(16, dtype=np.float32) / 16)).astype(np.float32)
        for p in range(128):
            d = p % 64
            half, idx = d // 32, d % 32
            pos = pos_row if half == 0 else pos_col
            a = pos * inv[idx % 16]
            rc[p] = np.cos(a)
            rs[p] = -np.sin(a) if idx < 16 else np.sin(a)
    perm = np.zeros((128, 128), np.float32)
    for mcol in range(128):
        idx = mcol % 32
        partner = mcol + 16 if idx < 16 else mcol - 16
        perm[partner, mcol] = 1.0
    j = np.arange(C)[:, None].astype(np.float32)
    i = np.arange(C)[None, :].astype(np.float32)
    ct = np.zeros((128, CT_W), np.float32)
    ct[:, CT_DPOS:CT_DPOS + 128] = np.maximum(i - j, 0)
    ct[:, CT_DNEG:CT_DNEG + 128] = np.maximum(j - i, 0)
    ct[:, CT_MGE:CT_MGE + 128] = (i >= j)
    ct[:, CT_MLE:CT_MLE + 128] = (i <= j)
    ct[:, CT_IP1:CT_IP1 + 128] = i + 1
    ct[:, CT_CMI:CT_CMI + 128] = C - i
    ct[:, CT_CM1MP] = C - 1 - np.arange(C)
    ct[:, CT_P] = np.arange(C)
    for b in range(NT):
        if is_latent:
            ct[:, CT_BPREV + b] = NEG if b == 0 else 0.0
            ct[:, CT_BNEXT + b] = NEG if b == NT - 1 else 0.0
            ct[:, CT_KEEPF + b] = 1.0
            ct[:, CT_KEEPB + b] = 1.0
        else:
            ct[:, CT_BPREV + b] = 0.0 if b % 2 == 1 else NEG
            ct[:, CT_BNEXT + b] = 0.0 if b % 2 == 0 else NEG
            ct[:, CT_KEEPF + b] = 0.0 if b % 2 == 0 else 1.0
            ct[:, CT_KEEPB + b] = 0.0 if b % 2 == 1 else 1.0
    ct[:, CT_BCTX] = 0.0 if is_latent else NEG
    bt = np.zeros((128, BT_W), np.float32)
    bt[:, BT_ID:BT_ID + 128] = np.eye(128)
    bt[:, BT_PERM:BT_PERM + 128] = perm
    if is_latent:
        bt[:, BT_TPREV:BT_TPREV + 128] = np.where(i <= j, 0.0, 8.0 * NEG)
        bt[:, BT_TNEXT:BT_TNEXT + 128] = np.where(j <= i, 0.0, 8.0 * NEG)
    for m_ in range(2):
        for p_ in range(128):
            lo = 0 if p_ < 64 else 64
            bt[p_, BT_BMASK + m_ * 128 + lo:BT_BMASK + m_ * 128 + lo + 64] = 1.0
    dc = dc.reshape(NT, 128, 4, 512).transpose(2, 1, 0, 3).reshape(4, 128, 8192)
    ds = ds.reshape(NT, 128, 4, 512).transpose(2, 1, 0, 3).reshape(4, 128, 8192)
    return {"dftc": _bf(dc), "dfts": _bf(ds), "cdft": _bf(cd), "ropec": _bf(rc), "ropes": _bf(rs),
            "ctab": ct, "btab": _bf(bt)}


_CACHE = {}


def kernel(x_prompt, x_sample, cache_k, cache_v, state_ret, c, c_ctx, w_mod, b_mod, g_pre, g_post, w_in,
           w_four, ret_decay, ret_gn, attn_sink, w_branch_a, w_branch_b, w_branch_c, w_out):
    f = lambda a: np.ascontiguousarray(np.asarray(a, dtype=np.float32))
    x_prompt, x_sample, cache_k, cache_v, state_ret, c, c_ctx = map(f, (x_prompt, x_sample, cache_k, cache_v,
                                                                       state_ret, c, c_ctx))
    shared = {
        "w_mod": f(w_mod), "b_mod": f(b_mod), "g_pre": f(g_pre), "g_post": f(g_post), "w_in": f(w_in),
        "w_four": f(w_four), "ret_decay": f(ret_decay).reshape(DEPTH, 8), "ret_gn": f(ret_gn),
        "attn_sink": f(attn_sink).reshape(DEPTH, 8), "w_pa": f(w_branch_a), "w_pb": f(w_branch_b),
        "w_pc": f(w_branch_c), "w_out": f(w_out),
    }
    if "nc" not in _CACHE:
        _CACHE["nc"] = build_nc(DEBUG)
        _CACHE["tabs"] = (_const_tables(False), _const_tables(True))
    nc, dbg_outs, _ = _CACHE["nc"]
    tabs_p, tabs_l = _CACHE["tabs"]
    in_maps = []
    for core in range(8):
        m = dict(shared)
        if core < 4:
            m.update(tabs_p)
            m["x"] = x_prompt[core * 8:(core + 1) * 8].reshape(T, D)
            m["cvec"] = c_ctx
            m["kctxT"] = np.zeros((DEPTH, 128, 512), np.float32)
            m["vctx"] = np.zeros((DEPTH, 512, 128), np.float32)
            m["s0"] = np.zeros((DEPTH, 2, 128, 2, 64), np.float32)
        else:
            b = core - 4
            m.update(tabs_l)
            m["x"] = x_sample[b]
            m["cvec"] = c[b]
            m["kctxT"] = np.ascontiguousarray(cache_k[b].transpose(0, 2, 3, 1).reshape(DEPTH, 128, 512))
            m["vctx"] = np.ascontiguousarray(cache_v[b].reshape(DEPTH, 512, 128))
            s = state_ret[b].reshape(DEPTH, 2, 2, 2, 64, 64)
            m["s0"] = np.ascontiguousarray(s.transpose(0, 1, 3, 4, 2, 5).reshape(DEPTH, 2, 128, 2, 64))
        in_maps.append(m)
    res = run_bass_kernel_spmd(nc, in_maps, core_ids=list(range(8)))
    R = res.results
    _CACHE["last"] = R
    y_prompt = np.concatenate([R[i]["y"].reshape(8, 256, D) for i in range(4)], axis=0)
    y_sample = np.stack([R[4 + i]["y"] for i in range(4)], axis=0)
    ck = np.concatenate([R[i]["ck"].reshape(DEPTH, 8, 256, 2, 64).transpose(1, 0, 2, 3, 4) for i in range(4)], axis=0)
    cv = np.concatenate([R[i]["cv"].reshape(DEPTH, 8, 256, 2, 64).transpose(1, 0, 2, 3, 4) for i in range(4)], axis=0)
    sts = []
    for i in range(4):
        s = R[i]["st"].reshape(DEPTH, 2, 8, 2, 64, 2, 64)
        s = s.transpose(2, 0, 1, 5, 3, 4, 6).reshape(8, DEPTH, 2, 4, 64, 64)
        sts.append(s)
    st = np.concatenate(sts, axis=0)
    return (y_prompt.astype(np.float32), y_sample.astype(np.float32), np.ascontiguousarray(ck, dtype=np.float32),
            np.ascontiguousarray(cv, dtype=np.float32), np.ascontiguousarray(st, dtype=np.float32))
```
